# Optimizing a Trainium2 kernel written in Bass

```python
import jax
import jax.numpy as jnp
from jax import lax
import numpy as np

D_MODEL = 2048
BATCH = 4
SEQ = 4096
DEPTH = 4

CHUNK = 64
N_META = 16
SB_BLOCK = 128
NORM_EPS = 1e-6
D_FF = 256 * ((8 * D_MODEL // 3 + 255) // 256)

HG_KEY = 128
HG_VAL = 128
HG_WIDTH = D_MODEL // 4
HG_HEADS = HG_WIDTH // HG_VAL
HG_QK = HG_HEADS * HG_KEY

SB_HEAD_DIM = 128
SB_WIDTH = D_MODEL // 2
SB_HEADS = SB_WIDTH // SB_HEAD_DIM
SB_SCALE = SB_HEAD_DIM ** -0.5

RW_HEAD_DIM = 64
RW_WIDTH = D_MODEL - HG_WIDTH - SB_WIDTH
RW_HEADS = RW_WIDTH // RW_HEAD_DIM
RW_DECAY_LORA = max(32, int(round(1.8 * D_MODEL ** 0.5 / 32)) * 32)
RW_AAA_LORA = max(32, int(round(1.8 * D_MODEL ** 0.5 / 32)) * 32)
RW_MV_LORA = max(32, int(round(1.3 * D_MODEL ** 0.5 / 32)) * 32)
RW_GATE_LORA = max(32, int(round(0.6 * D_MODEL ** 0.8 / 32)) * 32)
RW_LN_EPS = 64e-5

MIX_WIDTH = HG_WIDTH + SB_WIDTH + RW_WIDTH
IN_SIZES = (HG_QK, HG_QK, HG_WIDTH, HG_WIDTH, SB_WIDTH, SB_WIDTH, SB_WIDTH,
            RW_WIDTH, RW_WIDTH, RW_WIDTH, RW_DECAY_LORA, RW_AAA_LORA, RW_GATE_LORA)
N_IN = sum(IN_SIZES)
RW_SHIFT = 3 * RW_WIDTH + RW_DECAY_LORA + RW_AAA_LORA + RW_GATE_LORA
RW_START = N_IN - RW_SHIFT

kernel_name = 'hybrid_hgrn2_stickbreak_rwkv7_macaron'


def rms_norm(x, w):
    xf = x.astype(jnp.float32)
    y = xf * lax.rsqrt(jnp.mean(xf * xf, axis=-1, keepdims=True) + NORM_EPS)
    return (y * w.astype(jnp.float32)).astype(x.dtype)


def swiglu(x, wi, wo):
    gate, up = jnp.split(x @ wi, 2, axis=-1)
    return (jax.nn.silu(gate) * up) @ wo


def token_shift(p, mu):
    prev = jnp.pad(p, ((0, 0), (1, 0), (0, 0)))[:, :-1]
    return p + (prev - p) * mu


def split_sizes(t, sizes):
    idx = [int(v) for v in np.cumsum(sizes)[:-1]]
    return jnp.split(t, idx, axis=-1)


def hgrn2_mixer(q, f_logit, i, g, lb, norm_w):
    f32 = jnp.float32
    B, L, _ = q.shape
    n_chunks = L // CHUNK
    fz = f_logit.astype(f32)
    lb = lb.astype(f32)
    log_f = jnp.logaddexp(jnp.log(lb), jnp.log1p(-lb) + jax.nn.log_sigmoid(fz))
    k = (1.0 - lb) * jax.nn.sigmoid(-fz)
    q = jax.nn.silu(q.astype(f32))

    def chunks(t, d):
        return t.reshape(B, n_chunks, CHUNK, HG_HEADS, d).transpose(1, 0, 3, 2, 4)

    causal = jnp.tril(jnp.ones((CHUNK, CHUNK), dtype=bool))[:, :, None]

    def step(S, inp):
        qc, kc, gc, vc = inp
        G = jnp.cumsum(gc, axis=2)
        rel = jnp.exp(jnp.where(causal, G[:, :, :, None, :] - G[:, :, None, :, :], -jnp.inf))
        att = jnp.einsum('bhtk,bhsk,bhtsk->bhts', qc, kc, rel)
        o = (jnp.einsum('bhts,bhsv->bhtv', att, vc)
             + jnp.einsum('bhtk,bhkv->bhtv', qc * jnp.exp(G), S))
        G_end = G[:, :, -1:, :]
        S = (jnp.exp(G_end[:, :, 0, :, None]) * S
             + jnp.einsum('bhsk,bhsv->bhkv', kc * jnp.exp(G_end - G), vc))
        return S, o

    S0 = jnp.zeros((B, HG_HEADS, HG_KEY, HG_VAL), f32)
    _, o = lax.scan(step, S0, (chunks(q, HG_KEY), chunks(k, HG_KEY), chunks(log_f, HG_KEY),
                               chunks(i.astype(f32), HG_VAL)))
    o = o.transpose(1, 0, 3, 2, 4).reshape(B, L, HG_HEADS, HG_VAL)
    o = rms_norm(o, norm_w) * jax.nn.silu(g.astype(f32).reshape(B, L, HG_HEADS, HG_VAL))
    return o.reshape(B, L, HG_WIDTH)


def stick_breaking_mixer(q, k, v, qn_w, kn_w, on_w):
    f32 = jnp.float32
    B, L, _ = q.shape

    def heads(t):
        return t.reshape(B, L, SB_HEADS, SB_HEAD_DIM)

    qh = rms_norm(heads(q), qn_w).astype(f32).transpose(0, 2, 1, 3) * SB_SCALE
    kh = rms_norm(heads(k), kn_w).astype(f32).transpose(0, 2, 1, 3)
    vh = heads(v).astype(f32).transpose(0, 2, 1, 3)
    outs = []
    for blk in range(L // SB_BLOCK):
        q0, q1 = blk * SB_BLOCK, (blk + 1) * SB_BLOCK
        z = jnp.einsum('bhqd,bhkd->bhqk', qh[:, :, q0:q1], kh[:, :, :q1])
        strict = jnp.arange(q1)[None, :] < jnp.arange(q0, q1)[:, None]
        log_1mb = jnp.where(strict, jax.nn.log_sigmoid(-z), 0.0)
        between = lax.cumsum(log_1mb, axis=3, reverse=True) - log_1mb
        A = jnp.where(strict, jnp.exp(jax.nn.log_sigmoid(z) + between), 0.0)
        outs.append(jnp.einsum('bhqk,bhkd->bhqd', A, vh[:, :, :q1]))
    o = jnp.concatenate(outs, axis=2).transpose(0, 2, 1, 3)
    return rms_norm(o, on_w).reshape(B, L, SB_WIDTH)


def rwkv7_mixer(r, k, v, w_lo, a_lo, g_lo, w0, w2, a0, a2, g2, k_k, k_a, r_k, ln_w, ln_b,
                v_first, v_res):
    f32 = jnp.float32
    B, L, _ = r.shape
    r, k, v = r.astype(f32), k.astype(f32), v.astype(f32)
    w = -jax.nn.softplus(-(w0.astype(f32) + jnp.tanh(w_lo.astype(f32)) @ w2.astype(f32))) - 0.5
    decay = jnp.exp(-jnp.exp(w))
    a = jax.nn.sigmoid(a0.astype(f32) + a_lo.astype(f32) @ a2.astype(f32))
    g = jax.nn.sigmoid(g_lo.astype(f32)) @ g2.astype(f32)
    if v_res is None:
        v_first = v
    else:
        v_lo, v0, v2 = v_res
        v = v + (v_first - v) * jax.nn.sigmoid(v0.astype(f32) + v_lo.astype(f32) @ v2.astype(f32))

    def hd(t):
        return t.reshape(B, L, RW_HEADS, RW_HEAD_DIM)

    kk = hd(k * k_k.astype(f32))
    kk = kk * lax.rsqrt(jnp.maximum(jnp.sum(kk * kk, axis=-1, keepdims=True), 1e-24))
    k = k * (1.0 + (a - 1.0) * k_a.astype(f32))
    rh, kh, vh, wh = hd(r), hd(k), hd(v), hd(decay)
    bh = kk * hd(a)

    def step(S, inp):
        r_t, w_t, k_t, v_t, kk_t, b_t = inp
        sa = -jnp.einsum('bhvk,bhk->bhv', S, kk_t)
        S = (S * w_t[:, :, None, :] + sa[..., None] * b_t[:, :, None, :]
             + v_t[..., None] * k_t[:, :, None, :])
        return S, jnp.einsum('bhvk,bhk->bhv', S, r_t)

    S0 = jnp.zeros((B, RW_HEADS, RW_HEAD_DIM, RW_HEAD_DIM), f32)
    seq = tuple(jnp.moveaxis(t, 1, 0) for t in (rh, wh, kh, vh, kk, bh))
    _, y = lax.scan(step, S0, seq)
    y = jnp.moveaxis(y, 0, 1)
    mu = jnp.mean(y, axis=-1, keepdims=True)
    var = jnp.mean(jnp.square(y - mu), axis=-1, keepdims=True)
    yn = ((y - mu) * lax.rsqrt(var + RW_LN_EPS)).reshape(B, L, RW_WIDTH)
    yn = yn * ln_w.astype(f32) + ln_b.astype(f32)
    bonus = jnp.sum(rh * kh * r_k.astype(f32), axis=-1, keepdims=True) * vh
    out = (yn + bonus.reshape(B, L, RW_WIDTH)) * g
    return out, v_first


def setup_inputs(seed: int = 0) -> dict:
    key = jax.random.key(seed)
    ks = iter(jax.random.split(key, 40))
    f32 = jnp.float32
    D = D_MODEL

    def nrm(shape, scale):
        return jax.random.normal(next(ks), shape, f32) * scale

    def gain(shape):
        return 1.0 + 0.05 * jax.random.normal(next(ks), shape, f32)

    def unif(shape, lo, hi):
        return jax.random.uniform(next(ks), shape, f32, lo, hi)

    return {
        'x': nrm((BATCH, SEQ, D), 1.0),
        'meta': nrm((N_META, D), 1.0),
        'norm_ffn1': gain((DEPTH, D)),
        'ffn1_wi': nrm((DEPTH, D, 2 * D_FF), D ** -0.5),
        'ffn1_wo': nrm((DEPTH, D_FF, D), D_FF ** -0.5),
        'norm_mix': gain((DEPTH, D)),
        'w_in': nrm((DEPTH, D, N_IN), D ** -0.5),
        'w_in_v': nrm((DEPTH - 1, D, RW_MV_LORA), D ** -0.5),
        'hg_lb': nrm((DEPTH, HG_QK), 1.0),
        'hg_norm': gain((DEPTH, HG_VAL)),
        'sb_qn': gain((DEPTH, SB_HEAD_DIM)),
        'sb_kn': gain((DEPTH, SB_HEAD_DIM)),
        'sb_on': gain((DEPTH, SB_HEAD_DIM)),
        'rw_mu': unif((DEPTH, RW_SHIFT), 0.0, 1.0),
        'rw_mu_v': unif((DEPTH - 1, RW_MV_LORA), 0.0, 1.0),
        'rw_w0': unif((DEPTH, RW_WIDTH), -6.5, -1.5),
        'rw_w2': nrm((DEPTH, RW_DECAY_LORA, RW_WIDTH), 0.1 * RW_DECAY_LORA ** -0.5),
        'rw_a0': nrm((DEPTH, RW_WIDTH), 0.1),
        'rw_a2': nrm((DEPTH, RW_AAA_LORA, RW_WIDTH), RW_AAA_LORA ** -0.5),
        'rw_g2': nrm((DEPTH, RW_GATE_LORA, RW_WIDTH), RW_GATE_LORA ** -0.5),
        'rw_v0': 1.0 + nrm((DEPTH - 1, RW_WIDTH), 0.1),
        'rw_v2': nrm((DEPTH - 1, RW_MV_LORA, RW_WIDTH), RW_MV_LORA ** -0.5),
        'rw_kk': 0.85 + nrm((DEPTH, RW_WIDTH), 0.05),
        'rw_ka': 1.0 + nrm((DEPTH, RW_WIDTH), 0.05),
        'rw_rk': nrm((DEPTH, RW_HEADS, RW_HEAD_DIM), 0.1),
        'rw_ln_w': gain((DEPTH, RW_WIDTH)),
        'rw_ln_b': nrm((DEPTH, RW_WIDTH), 0.02),
        'w_out': nrm((DEPTH, MIX_WIDTH, D), MIX_WIDTH ** -0.5),
        'norm_ffn2': gain((DEPTH, D)),
        'ffn2_wi': nrm((DEPTH, D, 2 * D_FF), D ** -0.5),
        'ffn2_wo': nrm((DEPTH, D_FF, D), D_FF ** -0.5),
    }


def reference(x, meta, norm_ffn1, ffn1_wi, ffn1_wo, norm_mix, w_in, w_in_v, hg_lb, hg_norm,
              sb_qn, sb_kn, sb_on, rw_mu, rw_mu_v, rw_w0, rw_w2, rw_a0, rw_a2, rw_g2, rw_v0, rw_v2,
              rw_kk, rw_ka, rw_rk, rw_ln_w, rw_ln_b, w_out, norm_ffn2, ffn2_wi, ffn2_wo):
    B, S, D = x.shape
    L_real = N_META + S
    pad = (-L_real) % SB_BLOCK
    h = jnp.concatenate([jnp.broadcast_to(meta.astype(x.dtype)[None], (B, N_META, D)), x], axis=1)
    h = jnp.pad(h, ((0, 0), (0, pad), (0, 0)))

    lb_all = jnp.cumsum(jax.nn.softmax(hg_lb.astype(jnp.float32), axis=0), axis=0)
    lb_all = lb_all - lb_all[0]

    v_first = None
    for l in range(DEPTH):
        h = h + 0.5 * swiglu(rms_norm(h, norm_ffn1[l]), ffn1_wi[l], ffn1_wo[l])

        u = rms_norm(h, norm_mix[l])
        w_l = w_in[l] if l == 0 else jnp.concatenate([w_in[l], w_in_v[l - 1]], axis=1)
        proj = u @ w_l
        hq, hf, hi, hg, sq, sk, sv = split_sizes(proj[..., :RW_START], IN_SIZES[:7])
        rr, rk, rv, rwl, ral, rgl = split_sizes(token_shift(proj[..., RW_START:N_IN], rw_mu[l]),
                                                IN_SIZES[7:])
        o_hg = hgrn2_mixer(hq, hf, hi, hg, lb_all[l], hg_norm[l])
        o_sb = stick_breaking_mixer(sq, sk, sv, sb_qn[l], sb_kn[l], sb_on[l])
        if l == 0:
            v_res = None
        else:
            v_res = (token_shift(proj[..., N_IN:], rw_mu_v[l - 1]), rw_v0[l - 1], rw_v2[l - 1])
        o_rw, v_first = rwkv7_mixer(rr, rk, rv, rwl, ral, rgl, rw_w0[l], rw_w2[l], rw_a0[l], rw_a2[l],
                                    rw_g2[l], rw_kk[l], rw_ka[l], rw_rk[l], rw_ln_w[l], rw_ln_b[l],
                                    v_first, v_res)
        mix = jnp.concatenate([o_hg.astype(h.dtype), o_sb.astype(h.dtype), o_rw.astype(h.dtype)], axis=-1)
        h = h + mix @ w_out[l]

        h = h + 0.5 * swiglu(rms_norm(h, norm_ffn2[l]), ffn2_wi[l], ffn2_wo[l])

    return h[:, N_META:L_real]
```

```python
from contextlib import ExitStack
import math
import numpy as np
import concourse.bass as bass
import concourse.mybir as mybir

F32 = mybir.dt.float32
BF16 = mybir.dt.bfloat16
AF = mybir.ActivationFunctionType
ALU = mybir.AluOpType
AX = mybir.AxisListType

ENGS = ("pe", "act", "dve", "pool", "sp")
NDMA = 12


def region(ap):
    t = ap.tensor
    shp = list(t.shape)
    rowlen = 1
    for s in shp[1:]:
        rowlen *= s
    off = int(ap.offset)
    r0 = off // rowlen
    c0 = off % rowlen
    rext = 0
    cext = 0
    for step, cnt in ap.ap:
        step = abs(int(step))
        cnt = int(cnt)
        if cnt <= 1 or step == 0:
            continue
        if step >= rowlen and step % rowlen == 0:
            rext += (cnt - 1) * (step // rowlen)
        else:
            cext += (cnt - 1) * step
    c1 = c0 + cext + 1
    if c1 > rowlen:
        extra = (c1 - 1) // rowlen
        rext += extra
        c0, c1 = 0, rowlen
    return (ap.name, r0, r0 + rext + 1, c0, c1)


class State:
    def __init__(self, nc, es):
        self.nc = nc
        self.sem = {}
        self.cnt = {}
        for e in ENGS:
            self.sem[e] = es.enter_context(nc.semaphore("s_" + e))
            self.cnt[e] = 0
        self.dsem = {}
        self.dcnt = {}
        self.dnext = {}
        for q in ("sp", "pool", "act"):
            self.dsem[q] = [es.enter_context(nc.semaphore("d_%s%d" % (q, i))) for i in range(NDMA)]
            self.dcnt[q] = [0] * NDMA
            self.dnext[q] = 0
        self.waited = {e: {} for e in ENGS}
        self.nops = 0


class Phase:
    def __init__(self, st):
        self.st = st
        self.nc = st.nc
        self.ops = {e: [] for e in ENGS}
        self.recs = {}
        self.order = 0

    def _deps(self, reads, writes):
        deps = {}

        def add(done):
            k = done[0]
            if k not in deps or deps[k][2] < done[2]:
                deps[k] = done

        for ap in reads:
            nm, r0, r1, c0, c1 = region(ap)
            for rec in self.recs.get(nm, ()):
                if rec[5] and rec[0] < r1 and r0 < rec[1] and rec[2] < c1 and c0 < rec[3]:
                    add(rec[4])
        for ap in writes:
            nm, r0, r1, c0, c1 = region(ap)
            for rec in self.recs.get(nm, ()):
                if rec[0] < r1 and r0 < rec[1] and rec[2] < c1 and c0 < rec[3]:
                    add(rec[4])
        return deps

    def _record(self, reads, writes, done):
        for ap in writes:
            nm, r0, r1, c0, c1 = region(ap)
            lst = self.recs.setdefault(nm, [])
            lst[:] = [rc for rc in lst if not (r0 <= rc[0] and rc[1] <= r1 and c0 <= rc[2] and rc[3] <= c1)]
            lst.append([r0, r1, c0, c1, done, True])
        for ap in reads:
            nm, r0, r1, c0, c1 = region(ap)
            lst = self.recs.setdefault(nm, [])
            lst[:] = [rc for rc in lst if not ((not rc[5]) and rc[4][0] == done[0] and r0 <= rc[0] and rc[1] <= r1 and c0 <= rc[2] and rc[3] <= c1)]
            lst.append([r0, r1, c0, c1, done, False])

    def add(self, eng, fn, reads, writes, pe_skip=True):
        st = self.st
        deps = self._deps(reads, writes)
        st.cnt[eng] += 1
        done = ("e_" + eng, st.sem[eng], st.cnt[eng])
        waits = []
        for k, d in deps.items():
            if eng == "pe" and k == "e_pe":
                continue
            if st.waited[eng].get(k, 0) >= d[2]:
                continue
            st.waited[eng][k] = d[2]
            waits.append((d[1], d[2]))
        self.ops[eng].append((fn, waits, (st.sem[eng], 1)))
        self._record(reads, writes, done)
        st.nops += 1

    def dma(self, q, out, in_, **kw):
        st = self.st
        reads, writes = [in_], [out]
        deps = self._deps(reads, writes)
        i = st.dnext[q]
        st.dnext[q] = (i + 1) % NDMA
        sem = st.dsem[q][i]
        key = "d_%s%d" % (q, i)
        waits = []
        if st.dcnt[q][i] > 0 and st.waited[q].get(key, 0) < st.dcnt[q][i]:
            waits.append((sem, st.dcnt[q][i]))
            st.waited[q][key] = st.dcnt[q][i]
        st.dcnt[q][i] += 16
        done = (key, sem, st.dcnt[q][i])
        for k, d in deps.items():
            if st.waited[q].get(k, 0) >= d[2]:
                continue
            st.waited[q][k] = d[2]
            waits.append((d[1], d[2]))
        self.ops[q].append((lambda e: e.dma_start(out=out, in_=in_, **kw), waits, (sem, 16)))
        self._record(reads, writes, done)
        st.nops += 1

    def mm(self, out, lhsT, rhs, start=True, stop=True):
        self.add("pe", lambda e: e.matmul(out, lhsT, rhs, start=start, stop=stop), [lhsT, rhs], [out])

    def transpose(self, out, in_, ident):
        self.add("pe", lambda e: e.transpose(out, in_, ident), [in_, ident], [out])

    def act(self, out, in_, func, bias=None, scale=1.0, eng="act"):
        rd = [in_]
        kw = {}
        if bias is not None:
            kw["bias"] = bias
            if not isinstance(bias, (int, float)):
                rd.append(bias)
        if not isinstance(scale, (int, float)):
            rd.append(scale)
        self.add("act", lambda e: e.activation(out, in_, func, scale=scale, **kw), rd, [out])

    def tt(self, out, in0, in1, op, eng="dve"):
        self.add(eng, lambda e: e.tensor_tensor(out, in0, in1, op), [in0, in1], [out])

    def ts(self, out, in0, s1, s2=None, op0=ALU.mult, op1=ALU.bypass, eng="dve"):
        rd = [in0]
        for s in (s1, s2):
            if s is not None and not isinstance(s, (int, float)):
                rd.append(s)
        if s2 is None:
            self.add(eng, lambda e: e.tensor_scalar(out, in0, s1, None, op0), rd, [out])
        else:
            self.add(eng, lambda e: e.tensor_scalar(out, in0, s1, s2, op0, op1), rd, [out])

    def stt(self, out, in0, scalar, in1, op0, op1):
        rd = [in0, in1]
        if not isinstance(scalar, (int, float)):
            rd.append(scalar)
        self.add("dve", lambda e: e.scalar_tensor_tensor(out, in0, scalar, in1, op0, op1), rd, [out])

    def copy(self, out, in_, eng="dve"):
        if eng == "act":
            self.add("act", lambda e: e.copy(out, in_), [in_], [out])
        else:
            self.add(eng, lambda e: e.tensor_copy(out, in_), [in_], [out])

    def memset(self, out, val, eng="dve"):
        self.add(eng, lambda e: e.memset(out, val), [], [out])

    def recip(self, out, in_):
        self.add("dve", lambda e: e.reciprocal(out, in_), [in_], [out])

    def scan(self, out, d0, d1, init, op0, op1):
        rd = [d0, d1]
        if not isinstance(init, (int, float)):
            rd.append(init)
        self.add("dve", lambda e: e.tensor_tensor_scan(out, d0, d1, init, op0, op1), rd, [out])

    def flush(self, final=False):
        st = self.st
        nc = self.nc
        finals = []
        for e in ENGS:
            if st.cnt[e] > 0:
                finals.append(("e_" + e, st.sem[e], st.cnt[e]))
        for q in st.dsem:
            for i in range(NDMA):
                if st.dcnt[q][i] > 0:
                    finals.append(("d_%s%d" % (q, i), st.dsem[q][i], st.dcnt[q][i]))
        ops = self.ops
        emap = {"pe": "tensor", "act": "scalar", "dve": "vector", "pool": "gpsimd", "sp": "sync"}

        def mk(ename):
            def body(eng):
                for fn, waits, inc in ops[ename]:
                    for s, v in waits:
                        eng.wait_ge(s, v)
                    ins = fn(eng)
                    ins.then_inc(inc[0], inc[1])
                for k, s, v in finals:
                    if st.waited[ename].get(k, 0) >= v:
                        continue
                    st.waited[ename][k] = v
                    eng.wait_ge(s, v)
            return body

        with nc.Block() as block:
            for ename in ENGS:
                getattr(block, emap[ename])(mk(ename))
        self.ops = {e: [] for e in ENGS}
        self.recs = {}


def _coll(self, in_ap, out_ap, groups):
    st = self.st
    q = "pool"
    reads, writes = [in_ap], [out_ap]
    deps = self._deps(reads, writes)
    i = st.dnext[q]
    st.dnext[q] = (i + 1) % NDMA
    sem = st.dsem[q][i]
    key = "d_%s%d" % (q, i)
    waits = []
    if st.dcnt[q][i] > 0 and st.waited[q].get(key, 0) < st.dcnt[q][i]:
        waits.append((sem, st.dcnt[q][i]))
        st.waited[q][key] = st.dcnt[q][i]
    st.dcnt[q][i] += 16
    done = (key, sem, st.dcnt[q][i])
    for k, d in deps.items():
        if st.waited[q].get(k, 0) >= d[2]:
            continue
        st.waited[q][k] = d[2]
        waits.append((d[1], d[2]))
    self.ops[q].append((lambda e: e.collective_compute("AllGather", ALU.bypass, replica_groups=groups, ins=[in_ap], outs=[out_ap]), waits, (sem, 16)))
    self._record(reads, writes, done)
    st.nops += 1


Phase.allgather = _coll


D = 2048
DC = 16
DFF = 5632
FC = 44
EPS = 1e-6


def tiles_of(L, TT=512):
    out = []
    t = 0
    while t < L:
        n = min(TT, L - t)
        out.append((t, n))
        t += n
    return out


class Ctx:
    pass


def make_ctx(nc, es):
    cx = Ctx()
    cx.nc = nc
    cx.uid = [0]
    cx.st = State(nc, es)
    cx.ps = [es.enter_context(nc.psum_tensor("ps%d" % i, [128, 512], F32)) for i in range(8)]
    cx.ones_bf = es.enter_context(nc.sbuf_tensor("ones_bf", [128, 128], BF16))
    cx.ones_f = es.enter_context(nc.sbuf_tensor("ones_f", [128, 128], F32))
    cx.ident_f = es.enter_context(nc.sbuf_tensor("ident_f", [128, 128], F32))
    return cx


def init_consts(cx):
    P = Phase(cx.st)
    P.memset(cx.ones_bf[:], 1.0)
    P.memset(cx.ones_f[:], 1.0)
    nc = cx.nc
    P.add("pool", lambda e: e.affine_select(cx.ident_f[:], cx.ones_f[:], pattern=[[1, 128]], compare_op=ALU.is_equal,
                                             fill=0.0, base=0, channel_multiplier=-1), [cx.ones_f[:]], [cx.ident_f[:]])
    P.flush()


def rmsnorm_fm(P, cx, h, g, u, sq, rstd, ncn, TT, dn, eps, psb, extra_scale=1.0):
    for c in range(ncn):
        P.act(sq[:, c, :TT], h[:, c, :TT], AF.Square)
    for c in range(ncn):
        P.mm(psb[:, :TT], cx.ones_bf[:], sq[:, c, :TT], start=(c == 0), stop=(c == ncn - 1))
    P.act(rstd[:, :TT], psb[:, :TT], AF.Sqrt, bias=cx.eps_ap(eps), scale=1.0 / dn)
    P.recip(rstd[:, :TT], rstd[:, :TT])
    if extra_scale != 1.0:
        P.ts(rstd[:, :TT], rstd[:, :TT], float(extra_scale), None, op0=ALU.mult)
    for c in range(ncn):
        P.stt(u[:, c, :TT], h[:, c, :TT], g[:, c:c + 1], rstd[:, :TT], ALU.mult, ALU.mult)


def ffn_phase(cx, hT, wi, wo, gvec, L, pre=None, Lr=None):
    nc = cx.nc
    with ExitStack() as es:
        cx.uid[0] += 1
        tg = "_%d" % cx.uid[0]
        sb = lambda n, s, d: es.enter_context(nc.sbuf_tensor(n + tg, s, d))
        h = sb("f_h", [128, DC, 512], F32)
        u = sb("f_u", [128, DC, 512], BF16)
        hid = sb("f_hid", [128, FC, 512], BF16)
        rstd = sb("f_rstd", [128, 512], F32)
        g = sb("f_g", [128, DC], F32)
        wg = [sb("f_wg%d" % i, [128, DC, 256], BF16) for i in range(2)]
        wu = [sb("f_wu%d" % i, [128, DC, 256], BF16) for i in range(2)]
        wos = [sb("f_wo%d" % i, [128, FC, 128], BF16) for i in range(2)]
        tmp = [sb("f_tmp%d" % i, [128, 512], F32) for i in range(2)]
        P = Phase(cx.st)
        P.dma("sp", g[:], gvec)
        hv = hT.rearrange("(c p) t -> p c t", p=128)
        wiv = wi.rearrange("(c p) n -> p c n", p=128)
        wov = wo.rearrange("(j p) d -> p j d", p=128)
        ps = cx.ps
        for (t0, TT) in tiles_of(L):
            P.dma("sp", h[:, :, :TT], hv[:, :, t0:t0 + TT])
            if pre is not None:
                mixT, w_out = pre
                mv = mixT.rearrange("(c p) t -> p c t", p=128)
                wov2 = w_out.rearrange("(c p) n -> p c n", p=128)
                P.dma("sp", u[:, :, :TT], mv[:, :, t0:t0 + TT])
                for ds in range(4):
                    slab = hid[:, (ds % 2) * 16:(ds % 2) * 16 + 16, :]
                    P.dma("pool", slab, wov2[:, :, ds * 512:(ds + 1) * 512])
                    for dj in range(4):
                        dc = ds * 4 + dj
                        po = ps[4 + dc % 2]
                        for mc in range(DC):
                            P.mm(po[:, :TT], slab[:, mc, dj * 128:(dj + 1) * 128], u[:, mc, :TT], start=(mc == 0), stop=(mc == DC - 1))
                        P.tt(h[:, dc, :TT], po[:, :TT], h[:, dc, :TT], ALU.add)
            rmsnorm_fm(P, cx, h, g, u, hid, rstd, DC, TT, D, EPS, ps[7])
            k = 0
            for j2 in range(FC // 2):
                b = j2 % 2
                P.dma("pool", wg[b][:], wiv[:, :, j2 * 256:(j2 + 1) * 256])
                P.dma("pool", wu[b][:], wiv[:, :, DFF + j2 * 256:DFF + (j2 + 1) * 256])
                for jj in range(2):
                    j = 2 * j2 + jj
                    pa = ps[(k % 2) * 2]
                    pb = ps[(k % 2) * 2 + 1]
                    tm = tmp[k % 2]
                    k += 1
                    for c in range(DC):
                        P.mm(pa[:, :TT], wg[b][:, c, jj * 128:(jj + 1) * 128], u[:, c, :TT], start=(c == 0), stop=(c == DC - 1))
                    for c in range(DC):
                        P.mm(pb[:, :TT], wu[b][:, c, jj * 128:(jj + 1) * 128], u[:, c, :TT], start=(c == 0), stop=(c == DC - 1))
                    P.act(tm[:, :TT], pa[:, :TT], AF.Silu)
                    P.tt(hid[:, j, :TT], tm[:, :TT], pb[:, :TT], ALU.mult)
            for dc in range(DC):
                b = dc % 2
                P.dma("pool", wos[b][:, 0:22, :], wov[:, 0:22, dc * 128:(dc + 1) * 128])
                P.dma("pool", wos[b][:, 22:44, :], wov[:, 22:44, dc * 128:(dc + 1) * 128])
                po = ps[4 + b]
                for j in range(FC):
                    P.mm(po[:, :TT], wos[b][:, j, :], hid[:, j, :TT], start=(j == 0), stop=(j == FC - 1))
                P.stt(h[:, dc, :TT], po[:, :TT], 0.5, h[:, dc, :TT], ALU.mult, ALU.add)
            if Lr is not None and t0 + TT > Lr:
                P.memset(h[:, :, max(0, Lr - t0):TT], 0.0)
            P.dma("sp", hv[:, :, t0:t0 + TT], h[:, :, :TT])
        P.flush()


R_HGQ, R_HGF, R_HGG = 0, 512, 1024
R_SBQ, R_SBK = 1536, 2560
R_RWR, R_RWK = 3584, 4096
R_WLO, R_ALO, R_GLO, R_VLO = 4608, 4736, 4864, 5120
NFM = 5248
C_HGI, C_SBV, C_RWV = 0, 512, 1536
NTM = 2048

SLABS = [
    (0, 512, "fm", R_HGQ), (512, 512, "fm", R_HGF), (1536, 512, "fm", R_HGG),
    (2048, 512, "fm", R_SBQ), (2560, 512, "fm", R_SBQ + 512),
    (3072, 512, "fm", R_SBK), (3584, 512, "fm", R_SBK + 512),
    (5120, 512, "fm", R_RWR), (5632, 512, "fm", R_RWK),
    (6656, 448, "lo", 0),
    (1024, 512, "tm", C_HGI), (4096, 512, "tm", C_SBV), (4608, 512, "tm", C_SBV + 512), (6144, 512, "tm", C_RWV),
]


def proj_phase(cx, hT, w_in, w_in_v, gvec, pfm, ptm, L):
    nc = cx.nc
    with ExitStack() as es:
        cx.uid[0] += 1
        tg = "_%d" % cx.uid[0]
        sb = lambda n, s, d: es.enter_context(nc.sbuf_tensor(n + tg, s, d))
        h = sb("p_h", [128, DC, 512], F32)
        u = sb("p_u", [128, DC, 512], BF16)
        sq = sb("p_sq", [128, DC, 512], BF16)
        rstd = sb("p_rstd", [128, 512], F32)
        g = sb("p_g", [128, DC], F32)
        ws = [sb("p_w%d" % i, [128, DC, 512], BF16) for i in range(2)]
        wv = sb("p_wv", [128, DC, 64], BF16)
        stg = [sb("p_stg%d" % i, [128, 512], F32) for i in range(4)]
        P = Phase(cx.st)
        P.dma("sp", g[:], gvec)
        hv = hT.rearrange("(c p) t -> p c t", p=128)
        wiv = w_in.rearrange("(c p) n -> p c n", p=128)
        if w_in_v is not None:
            P.dma("pool", wv[:], w_in_v.rearrange("(c p) n -> p c n", p=128))
        ps = cx.ps
        k = 0
        nslab = 0
        for (t0, TT) in tiles_of(L):
            P.dma("sp", h[:, :, :TT], hv[:, :, t0:t0 + TT])
            rmsnorm_fm(P, cx, h, g, u, sq, rstd, DC, TT, D, EPS, ps[7])

            def fm_chunk(wt, cs, M, drow):
                nonlocal k
                pb = ps[k % 4]
                sg = stg[k % 4]
                for c in range(DC):
                    P.mm(pb[:M, :TT], wt[:, c, cs:cs + M], u[:, c, :TT], start=(c == 0), stop=(c == DC - 1))
                if k % 2 == 0:
                    P.copy(sg[:M, :TT], pb[:M, :TT], eng="act")
                else:
                    P.copy(sg[:M, :TT], pb[:M, :TT], eng="dve")
                P.dma("sp", pfm[drow:drow + M, t0:t0 + TT], sg[:M, :TT])
                k += 1

            for (c0, ncol, kind, dst) in SLABS:
                wt = ws[nslab % 2]
                nslab += 1
                P.dma("pool", wt[:, :, :ncol], wiv[:, :, c0:c0 + ncol])
                if kind == "fm":
                    for j in range(ncol // 128):
                        fm_chunk(wt, j * 128, 128, dst + j * 128)
                elif kind == "lo":
                    fm_chunk(wt, 0, 96, R_WLO)
                    fm_chunk(wt, 96, 96, R_ALO)
                    fm_chunk(wt, 192, 128, R_GLO)
                    fm_chunk(wt, 320, 128, R_GLO + 128)
                else:
                    for tb in range(TT // 128):
                        pb = ps[k % 4]
                        sg = stg[k % 4]
                        for c in range(DC):
                            P.mm(pb[:, :ncol], u[:, c, tb * 128:(tb + 1) * 128], wt[:, c, :ncol], start=(c == 0), stop=(c == DC - 1))
                        if k % 2 == 0:
                            P.copy(sg[:, :ncol], pb[:, :ncol], eng="act")
                        else:
                            P.copy(sg[:, :ncol], pb[:, :ncol], eng="dve")
                        P.dma("sp", ptm[t0 + tb * 128:t0 + (tb + 1) * 128, dst:dst + ncol], sg[:, :ncol])
                        k += 1
            if w_in_v is not None:
                fm_chunk(wv, 0, 64, R_VLO)
        P.flush()


SB_HEADS = 8
R_MIX_HG, R_MIX_SB, R_MIX_RW = 0, 512, 1536


def rmsnorm1(P, cx, x, gcol, out, sqs, rstd, TT, dn, eps, psb, extra=1.0):
    P.act(sqs[:, :TT], x, AF.Square)
    P.mm(psb[:, :TT], cx.ones_bf[:], sqs[:, :TT], start=True, stop=True)
    P.act(rstd[:, :TT], psb[:, :TT], AF.Ln, bias=float(eps), scale=1.0 / dn)
    P.act(rstd[:, :TT], rstd[:, :TT], AF.Exp, bias=float(math.log(extra)), scale=-0.5)
    P.stt(out, x, gcol, rstd[:, :TT], ALU.mult, ALU.mult)


def make_masks(cx, es):
    nc = cx.nc
    sb = lambda n, s, d: es.enter_context(nc.sbuf_tensor(n, s, d))
    cx.ones_w = sb("ones_w", [128, 896], BF16)
    cx.tri_incl = sb("tri_incl", [128, 128], BF16)
    cx.mw = sb("mw", [128, 896], BF16)
    cx.m_le = sb("m_le", [128, 128], BF16)
    cx.m_lt_f = sb("m_lt_f", [128, 128], F32)
    cx.m_le_f = sb("m_le_f", [128, 128], F32)
    cx.m_gt_f = sb("m_gt_f", [128, 128], F32)
    P = Phase(cx.st)
    P.memset(cx.ones_w[:], 1.0)

    def sel(out, in_, pat, cm, base, op):
        P.add("pool", lambda e: e.affine_select(out, in_, pattern=pat, compare_op=op, fill=0.0, base=base,
                                                 channel_multiplier=cm), [in_], [out])
    sel(cx.tri_incl[:], cx.ones_w[:, 0:128], [[-1, 128]], 1, 0, ALU.is_ge)
    sel(cx.mw[:], cx.ones_w[:], [[1, 896]], -1, -384, ALU.is_gt)
    sel(cx.m_le[:], cx.ones_w[:, 0:128], [[1, 128]], -1, 0, ALU.is_ge)
    sel(cx.m_lt_f[:], cx.ones_f[:], [[1, 128]], -1, 0, ALU.is_gt)
    sel(cx.m_le_f[:], cx.ones_f[:], [[1, 128]], -1, 0, ALU.is_ge)
    sel(cx.m_gt_f[:], cx.ones_f[:], [[-1, 128]], 1, 0, ALU.is_gt)
    P.flush()


def sb_phase(cx, pfm, ptm, gains, mixT, L):
    nc = cx.nc
    NT = L // 128
    with ExitStack() as es:
        cx.uid[0] += 1
        tg = "_%d" % cx.uid[0]
        sb = lambda n, s, d: es.enter_context(nc.sbuf_tensor(n + tg, s, d))
        gn = sb("s_gn", [128, 3], F32)
        qn = [sb("s_qn%d" % i, [128, L], BF16) for i in range(2)]
        kn = [sb("s_kn%d" % i, [128, L], BF16) for i in range(2)]
        vh = [sb("s_v%d" % i, [128, NT, 128], BF16) for i in range(2)]
        xin = [sb("s_x%d" % i, [128, 512], F32) for i in range(2)]
        sqs = sb("s_sq", [128, 512], BF16)
        rstd = sb("s_rstd", [128, 512], F32)
        E = [sb("s_E%d" % i, [128, 512], F32) for i in range(2)]
        Lb = [sb("s_L%d" % i, [128, 512], BF16) for i in range(2)]
        T1 = [sb("s_T1%d" % i, [128, 512], F32) for i in range(2)]
        T2 = [sb("s_T2%d" % i, [128, 512], F32) for i in range(2)]
        At = [sb("s_A%d" % i, [128, 512], BF16) for i in range(2)]
        Cs = sb("s_Cs", [128, 512], F32)
        oh = sb("s_oh", [128, 512], F32)
        ob = [sb("s_ob%d" % i, [128, 512], BF16) for i in range(2)]
        ps = cx.ps
        P = Phase(cx.st)
        P.dma("sp", gn[:], gains)
        ptv = ptm.rearrange("(n p) c -> p n c", p=128)
        kstep = 0
        for hd in range(SB_HEADS):
            b = hd % 2
            for n0 in range(0, NT, 8):
                n1 = min(NT, n0 + 8)
                P.dma("pool", vh[b][:, n0:n1, :], ptv[:, n0:n1, C_SBV + hd * 128:C_SBV + (hd + 1) * 128])
            i = 0
            for (t0, TT) in tiles_of(L):
                for (row, gi, dst, extra) in ((R_SBQ, 0, qn[b], 128.0 ** -0.5), (R_SBK, 1, kn[b], 1.0)):
                    x = xin[i % 2]
                    i += 1
                    P.dma("sp", x[:, :TT], pfm[row + hd * 128:row + (hd + 1) * 128, t0:t0 + TT])
                    rmsnorm1(P, cx, x[:, :TT], gn[:, gi:gi + 1], dst[:, t0:t0 + TT], sqs, rstd, TT, 128, EPS, ps[7], extra)
            for (t0, TQ) in tiles_of(L):
                sb_max = (t0 + TQ - 1) // 128
                P.memset(Cs[:, :TQ], 0.0, eng="pool")
                po = ps[6]
                for sbk in range(sb_max, -1, -1):
                    w = kstep % 2
                    kstep += 1
                    pa, pb, pc = ps[w * 3], ps[w * 3 + 1], ps[w * 3 + 2]
                    off = sbk * 128 - t0
                    diag = off >= 0
                    P.mm(pa[:, :TQ], kn[b][:, sbk * 128:(sbk + 1) * 128], qn[b][:, t0:t0 + TQ])
                    P.act(E[w][:, :TQ], pa[:, :TQ], AF.Exp)
                    P.act(Lb[w][:, :TQ], E[w][:, :TQ], AF.Ln, bias=1.0)
                    if diag:
                        msk = cx.mw[:, 384 - off:384 - off + TQ]
                        P.tt(Lb[w][:, :TQ], Lb[w][:, :TQ], msk, ALU.mult, eng="pool")
                    P.mm(pb[:, :TQ], cx.tri_incl[:], Lb[w][:, :TQ])
                    P.mm(pc[:, :TQ], cx.ones_bf[:], Lb[w][:, :TQ])
                    P.tt(T1[w][:, :TQ], pb[:, :TQ], Cs[:, :TQ], ALU.add)
                    P.tt(T2[w][:, :TQ], pa[:, :TQ], T1[w][:, :TQ], ALU.subtract)
                    P.act(At[w][:, :TQ], T2[w][:, :TQ], AF.Exp)
                    if diag:
                        P.tt(At[w][:, :TQ], At[w][:, :TQ], msk, ALU.mult, eng="pool")
                    P.tt(Cs[:, :TQ], pc[:, :TQ], Cs[:, :TQ], ALU.add)
                    P.mm(po[:, :TQ], vh[b][:, sbk, :], At[w][:, :TQ], start=(sbk == sb_max), stop=(sbk == 0))
                P.copy(oh[:, :TQ], po[:, :TQ], eng="act")
                o2 = ob[(t0 // 512) % 2]
                rmsnorm1(P, cx, oh[:, :TQ], gn[:, 2:3], o2[:, :TQ], sqs, rstd, TQ, 128, EPS, ps[7])
                P.dma("sp", mixT[R_MIX_SB + hd * 128:R_MIX_SB + (hd + 1) * 128, t0:t0 + TQ], o2[:, :TQ])
        P.flush()


HG_HEADS = 4


def make_lb(cx, es, hg_lb_ap):
    nc = cx.nc
    sb = lambda n, s, d: es.enter_context(nc.sbuf_tensor(n, s, d))
    cx.lb = sb("lb", [128, 4, 4], F32)
    cx.oml = sb("oml", [128, 4, 4], F32)
    cx.noml = sb("noml", [128, 4, 4], F32)
    cx.rmask = sb("rmask", [128, 512], F32)
    with ExitStack() as es2:
        x = es2.enter_context(nc.sbuf_tensor("lb_x", [128, 4, 4], F32))
        e = es2.enter_context(nc.sbuf_tensor("lb_e", [128, 4, 4], F32))
        s = es2.enter_context(nc.sbuf_tensor("lb_s", [128, 4], F32))
        P = Phase(cx.st)
        P.dma("sp", x[:], hg_lb_ap)
        P.act(e[:], x[:], AF.Exp)
        P.tt(s[:], e[:, :, 0], e[:, :, 1], ALU.add)
        P.tt(s[:], s[:], e[:, :, 2], ALU.add)
        P.tt(s[:], s[:], e[:, :, 3], ALU.add)
        P.recip(s[:], s[:])
        P.memset(cx.lb[:, 0, :], 0.0)
        for l in range(1, 4):
            P.tt(e[:, :, l], e[:, :, l], s[:], ALU.mult)
            P.tt(cx.lb[:, l, :], cx.lb[:, l - 1, :], e[:, :, l], ALU.add)
        P.ts(cx.oml[:], cx.lb[:], -1.0, 1.0, op0=ALU.mult, op1=ALU.add)
        P.ts(cx.noml[:], cx.lb[:], 1.0, -1.0, op0=ALU.mult, op1=ALU.add)
        P.memset(cx.rmask[:], 1.0)
        for c in range(8):
            P.memset(cx.rmask[:, c * 64:c * 64 + 1], 0.0)
        P.flush()


def hg_phase(cx, pfm, ptm, layer, gnorm, mixT, L):
    nc = cx.nc
    NT = L // 128
    H = HG_HEADS
    with ExitStack() as es:
        cx.uid[0] += 1
        tg = "_%d" % cx.uid[0]
        sb = lambda n, s, d: es.enter_context(nc.sbuf_tensor(n + tg, s, d))
        gn = sb("g_gn", [128, 1], F32)
        V = [sb("g_v%d" % i, [64, 2 * NT, 128], BF16) for i in range(H)]
        X = [sb("g_x%d" % i, [128, 512], F32) for i in range(3)]
        SG = sb("g_sg", [128, 512], F32)
        FG = sb("g_fg", [128, 512], F32)
        QS = [sb("g_qs%d" % i, [128, 512], F32) for i in range(H)]
        KK = [sb("g_kk%d" % i, [128, 512], F32) for i in range(H)]
        G = [sb("g_G%d" % i, [128, 512], F32) for i in range(H)]
        NG = [sb("g_NG%d" % i, [128, 512], F32) for i in range(H)]
        EG = [sb("g_EG%d" % i, [128, 512], F32) for i in range(H)]
        QP = [sb("g_QP%d" % i, [128, 512], BF16) for i in range(H)]
        OH = [sb("g_OH%d" % i, [128, 512], F32) for i in range(H)]
        S = [sb("g_S%d" % i, [128, 128], F32) for i in range(H)]
        Sbf = [sb("g_Sb%d" % i, [128, 128], BF16) for i in range(H)]
        tmp = [sb("g_t%d" % i, [128, 64], F32) for i in range(6)]
        QT = [sb("g_QT%d" % i, [128, 64], BF16) for i in range(2)]
        KT = [sb("g_KT%d" % i, [128, 64], BF16) for i in range(2)]
        KH = [sb("g_KH%d" % i, [128, 64], F32) for i in range(2)]
        AM = [sb("g_AM%d" % i, [64, 64], BF16) for i in range(2)]
        AF32 = [sb("g_AF%d" % i, [64, 64], F32) for i in range(2)]
        KHt = [sb("g_KHt%d" % i, [64, 128], BF16) for i in range(2)]
        sqs = sb("g_sq", [128, 512], BF16)
        rstd = sb("g_rstd", [128, 512], F32)
        ON = sb("g_on", [128, 512], F32)
        MX = [sb("g_mx%d" % i, [128, 512], BF16) for i in range(2)]
        ps = cx.ps
        P = Phase(cx.st)
        P.dma("sp", gn[:], gnorm)
        ptv = ptm.rearrange("(n p) c -> p n c", p=64)
        for hd in range(H):
            for n0 in range(0, 2 * NT, 16):
                n1 = min(2 * NT, n0 + 16)
                P.dma("pool", V[hd][:, n0:n1, :], ptv[:, n0:n1, C_HGI + hd * 128:C_HGI + (hd + 1) * 128])
            P.memset(S[hd][:], 0.0)
            P.memset(Sbf[hd][:], 0.0)
        k = 0
        nm = 0
        for (t0, TT) in tiles_of(L):
            for hd in range(H):
                lb = cx.lb[:, layer, hd:hd + 1]
                oml = cx.oml[:, layer, hd:hd + 1]
                noml = cx.noml[:, layer, hd:hd + 1]
                P.dma("sp", X[0][:, :TT], pfm[R_HGF + hd * 128:R_HGF + (hd + 1) * 128, t0:t0 + TT])
                P.dma("sp", X[1][:, :TT], pfm[R_HGQ + hd * 128:R_HGQ + (hd + 1) * 128, t0:t0 + TT])
                P.act(SG[:, :TT], X[0][:, :TT], AF.Sigmoid)
                P.ts(FG[:, :TT], SG[:, :TT], oml, lb, op0=ALU.mult, op1=ALU.add)
                P.act(FG[:, :TT], FG[:, :TT], AF.Ln)
                P.ts(KK[hd][:, :TT], SG[:, :TT], noml, oml, op0=ALU.mult, op1=ALU.add)
                P.act(QS[hd][:, :TT], X[1][:, :TT], AF.Silu)
                P.scan(G[hd][:, :TT], cx.rmask[:, :TT], FG[:, :TT], 0.0, ALU.mult, ALU.add)
                P.ts(NG[hd][:, :TT], G[hd][:, :TT], -1.0, None, op0=ALU.mult)
                P.act(EG[hd][:, :TT], G[hd][:, :TT], AF.Exp)
                P.tt(QP[hd][:, :TT], QS[hd][:, :TT], EG[hd][:, :TT], ALU.mult)
            for c in range(TT // 64):
                blk = t0 // 64 + c
                cs = slice(c * 64, (c + 1) * 64)
                mid = c * 64 + 31
                end = c * 64 + 63
                for hd in range(H):
                    w = k % 2
                    k += 1
                    t1, t2, t3 = tmp[w * 3], tmp[w * 3 + 1], tmp[w * 3 + 2]
                    P.act(t1[:], G[hd][:, cs], AF.Exp, bias=NG[hd][:, mid:mid + 1])
                    P.stt(QT[w][:], t1[:], 1e30, QS[hd][:, cs], ALU.min, ALU.mult)
                    P.act(t2[:], G[hd][:, cs], AF.Exp, bias=G[hd][:, mid:mid + 1], scale=-1.0)
                    P.stt(KT[w][:], t2[:], 1e30, KK[hd][:, cs], ALU.min, ALU.mult)
                    P.act(t3[:], G[hd][:, cs], AF.Exp, bias=G[hd][:, end:end + 1], scale=-1.0)
                    P.tt(KH[w][:], KK[hd][:, cs], t3[:], ALU.mult, eng="pool")
                    pa, po, pt, pn = ps[w], ps[2 + w], ps[4 + w], ps[6]
                    P.mm(pa[:64, :64], KT[w][:], QT[w][:])
                    P.ts(AF32[w][:], pa[:64, :64], 1e30, -1e30, op0=ALU.min, op1=ALU.max)
                    P.tt(AM[w][:], AF32[w][:], cx.m_le[:64, :64], ALU.mult)
                    P.mm(po[:, :64], V[hd][:, blk, :], AM[w][:], start=True, stop=False)
                    P.mm(po[:, :64], Sbf[hd][:], QP[hd][:, cs], start=False, stop=True)
                    P.copy(OH[hd][:, cs], po[:, :64], eng="act")
                    P.transpose(pt[:64, :128], KH[w][:], cx.ident_f[:])
                    P.copy(KHt[w][:], pt[:64, :128], eng="dve")
                    P.mm(pn[:, :128], KHt[w][:], V[hd][:, blk, :])
                    P.stt(S[hd][:], S[hd][:], EG[hd][:, end:end + 1], pn[:, :128], ALU.mult, ALU.add)
                    P.copy(Sbf[hd][:], S[hd][:], eng="act")
            for hd in range(H):
                P.dma("sp", X[2][:, :TT], pfm[R_HGG + hd * 128:R_HGG + (hd + 1) * 128, t0:t0 + TT])
                P.act(X[2][:, :TT], X[2][:, :TT], AF.Silu)
                rmsnorm1(P, cx, OH[hd][:, :TT], gn[:, 0:1], ON[:, :TT], sqs, rstd, TT, 128, EPS, ps[7])
                mx = MX[nm % 2]
                nm += 1
                P.tt(mx[:, :TT], ON[:, :TT], X[2][:, :TT], ALU.mult)
                P.dma("sp", mixT[R_MIX_HG + hd * 128:R_MIX_HG + (hd + 1) * 128, t0:t0 + TT], mx[:, :TT])
        P.flush()


RW_H = 8
CW = -0.6065306597126334
RW_LN_EPS = 64e-5


def rw_phase(cx, pfm, ptm, layer, prm, vfirst, mixT, L):
    nc = cx.nc
    NT = L // 128
    H = RW_H
    with ExitStack() as es:
        cx.uid[0] += 1
        tg = "_%d" % cx.uid[0]
        sb = lambda n, s, d=F32: es.enter_context(nc.sbuf_tensor(n + tg, s, d))
        rwp = sb("r_rwp", [64, 7, 8])
        omka = sb("r_omka", [64, 8])
        lop = sb("r_lop", [128, 8])
        w2s = sb("r_w2", [96, 512])
        a2s = sb("r_a2", [96, 512])
        g2s = sb("r_g2", [128, 2, 512])
        v2s = sb("r_v2", [64, 512])
        tmb = sb("r_tmb", [128, 5, 512])
        m_gt4 = sb("r_mgt4", [128, 4, 128])
        m_lt4 = sb("r_mlt4", [128, 4, 128])
        m_le4 = sb("r_mle4", [128, 4, 128])
        id4 = sb("r_id4", [128, 4, 128])
        rmh = sb("r_rmh", [64, 8, 128])
        ST = sb("r_ST", [64, 8, 64])
        fm = {}
        for n in ("Rc", "Rp", "Kc", "Kp", "Rs", "Ks", "SW", "CS", "EP", "EN", "EX", "A", "KKn", "Bv",
                  "K2", "At", "Bt", "Kt", "Rt", "Bh", "Kh"):
            fm[n] = sb("r_f" + n, [64, 8, 128])
        fm["T0"], fm["T1"], fm["KK0"], fm["CX"] = fm["Rp"], fm["Kp"], fm["Rc"], fm["Kc"]
        lo = {}
        for n in ("WLc", "WLp", "ALc", "ALp"):
            lo[n] = sb("r_l" + n, [96, 128])
        for n in ("GLc", "GLp"):
            lo[n] = sb("r_l" + n, [128, 2, 128])
        for n in ("VLc", "VLp"):
            lo[n] = sb("r_l" + n, [64, 128])
        tm = {}
        for n in ("Vc", "Vp", "V", "VF", "SV", "Gt", "NXZ", "SA", "Y", "YN", "BHt", "KHt"):
            tm[n] = sb("r_t" + n, [128, 512])
        tm["CEN"], tm["SQ"], tm["BON"] = tm["Y"], tm["NXZ"], tm["SA"]
        MS = sb("r_MS", [128, 8])
        VS = sb("r_VS", [128, 8])
        RKS = sb("r_RKS", [128, 8])
        big = {}
        for n in ("M0", "M1", "N0", "N1", "PT", "LAK", "MRB", "MRK"):
            big[n] = sb("r_b" + n, [128, 8, 128])
        OB = [sb("r_OB%d" % i, [128, 4, 128], BF16) for i in range(2)]
        ps = cx.ps
        P = Phase(cx.st)
        bank = [0]

        def nb():
            b = ps[bank[0] % 8]
            bank[0] += 1
            return b

        def b3(p_, h=4):
            return p_[:].rearrange("p (h t) -> p h t", h=h)

        P.dma("sp", rwp[:], prm["rwp"])
        P.dma("sp", lop[:], prm["lop"])
        P.dma("sp", w2s[:], prm["w2"])
        P.dma("sp", a2s[:], prm["a2"])
        P.dma("sp", g2s[:], prm["g2"])
        P.dma("sp", tmb[:], prm["tmb"])
        if layer > 0:
            P.dma("sp", v2s[:], prm["v2"])
        P.ts(omka[:], rwp[:, 5, :], -1.0, 1.0, op0=ALU.mult, op1=ALU.add)
        for j in range(4):
            P.copy(m_gt4[:, j, :], cx.m_gt_f[:])
            P.copy(m_lt4[:, j, :], cx.m_lt_f[:])
            P.copy(m_le4[:, j, :], cx.m_le_f[:])
            P.copy(id4[:, j, :], cx.ident_f[:])
        P.memset(rmh[:], 1.0)
        P.memset(rmh[:, :, 0:1], 0.0)
        P.memset(ST[:], 0.0)

        def bc(ap2, n=128):
            return ap2.unsqueeze(2).to_broadcast([ap2.shape[0], ap2.shape[1], n])

        def fmv(row0):
            return pfm[row0:row0 + 512, :].rearrange("(h k) t -> k h t", k=64)

        rv, kv = fmv(R_RWR), fmv(R_RWK)
        mixv = mixT[R_MIX_RW:R_MIX_RW + 512, :].rearrange("(j p) t -> p j t", p=128)
        hs = lambda h: slice(h * 64, (h + 1) * 64)

        def shift_load(cur, prev, src3, t0, three):
            if three:
                P.dma("sp", cur[:], src3[:, :, t0:t0 + 128])
                if t0 == 0:
                    P.memset(prev[:, :, 0:1], 0.0)
                    P.dma("sp", prev[:, :, 1:128], src3[:, :, 0:127])
                else:
                    P.dma("sp", prev[:], src3[:, :, t0 - 1:t0 + 127])
            else:
                P.dma("sp", cur[:], src3[:, t0:t0 + 128])
                if t0 == 0:
                    P.memset(prev[:, 0:1], 0.0)
                    P.dma("sp", prev[:, 1:128], src3[:, 0:127])
                else:
                    P.dma("sp", prev[:], src3[:, t0 - 1:t0 + 127])

        for c in range(NT):
            t0 = c * 128
            f = fm
            shift_load(f["Rc"], f["Rp"], rv, t0, True)
            shift_load(f["Kc"], f["Kp"], kv, t0, True)
            for (cur, prev, out, mi) in ((f["Rc"], f["Rp"], f["Rs"], 0), (f["Kc"], f["Kp"], f["Ks"], 1)):
                P.tt(prev[:], prev[:], cur[:], ALU.subtract)
                P.tt(prev[:], prev[:], bc(rwp[:, mi, :]), ALU.mult)
                P.tt(out[:], prev[:], cur[:], ALU.add)
            shift_load(lo["WLc"], lo["WLp"], pfm[R_WLO:R_WLO + 96, :], t0, False)
            shift_load(lo["ALc"], lo["ALp"], pfm[R_ALO:R_ALO + 96, :], t0, False)
            shift_load(lo["GLc"], lo["GLp"], pfm[R_GLO:R_GLO + 256, :].rearrange("(j p) t -> p j t", p=128), t0, True)
            P.tt(lo["WLp"][:], lo["WLp"][:], lo["WLc"][:], ALU.subtract)
            P.stt(lo["WLc"][:], lo["WLp"][:], lop[:96, 0:1], lo["WLc"][:], ALU.mult, ALU.add)
            P.act(lo["WLc"][:], lo["WLc"][:], AF.Tanh)
            P.tt(lo["ALp"][:], lo["ALp"][:], lo["ALc"][:], ALU.subtract)
            P.stt(lo["ALc"][:], lo["ALp"][:], lop[:96, 1:2], lo["ALc"][:], ALU.mult, ALU.add)
            P.tt(lo["GLp"][:], lo["GLp"][:], lo["GLc"][:], ALU.subtract)
            for j in range(2):
                P.stt(lo["GLc"][:, j, :], lo["GLp"][:, j, :], lop[:, 2 + j:3 + j], lo["GLc"][:, j, :], ALU.mult, ALU.add)
            P.act(lo["GLc"][:], lo["GLc"][:], AF.Sigmoid)
            for (w_s, code, bias_i, out) in ((w2s, lo["WLc"], 2, f["SW"]), (a2s, lo["ALc"], 3, f["A"])):
                for half in range(2):
                    pb = nb()
                    for j in range(4):
                        h = half * 4 + j
                        P.mm(pb[:64, j * 128:(j + 1) * 128], w_s[:, hs(h)], code[:])
                    P.tt(out[:, half * 4:half * 4 + 4, :], b3(pb)[:64], bc(rwp[:, bias_i, half * 4:half * 4 + 4]), ALU.add)
                P.act(out[:], out[:], AF.Sigmoid)
            P.scan(f["CS"][:].rearrange("k h t -> k (h t)"), rmh[:].rearrange("k h t -> k (h t)"),
                   f["SW"][:].rearrange("k h t -> k (h t)"), 0.0, ALU.mult, ALU.add)
            P.tt(f["CX"][:], f["CS"][:], f["SW"][:], ALU.subtract)
            P.act(f["EP"][:], f["CS"][:], AF.Exp, scale=CW)
            P.act(f["EN"][:], f["CS"][:], AF.Exp, scale=-CW)
            P.act(f["EX"][:], f["CX"][:], AF.Exp, scale=CW)
            P.tt(f["KK0"][:], f["Ks"][:], bc(rwp[:, 4, :]), ALU.mult)
            P.tt(f["T0"][:], f["KK0"][:], f["KK0"][:], ALU.mult)
            for half in range(2):
                pb = nb()
                P.mm(pb[:64, :], cx.ones_f[:64, :64], f["T0"][:, half * 4:half * 4 + 4, :].rearrange("k h t -> k (h t)"))
                P.ts(f["T1"][:, half * 4:half * 4 + 4, :], b3(pb)[:64], 1e-16, None, op0=ALU.max)
            P.act(f["T1"][:], f["T1"][:], AF.Ln)
            P.act(f["T1"][:], f["T1"][:], AF.Exp, scale=-0.5)
            P.tt(f["KKn"][:], f["KK0"][:], f["T1"][:], ALU.mult)
            P.tt(f["Bv"][:], f["KKn"][:], f["A"][:], ALU.mult)
            P.tt(f["T0"][:], f["A"][:], bc(rwp[:, 5, :]), ALU.mult)
            P.tt(f["T0"][:], f["T0"][:], bc(omka[:]), ALU.add)
            P.tt(f["K2"][:], f["Ks"][:], f["T0"][:], ALU.mult)
            P.tt(f["At"][:], f["KKn"][:], f["EX"][:], ALU.mult)
            P.tt(f["Bt"][:], f["Bv"][:], f["EN"][:], ALU.mult)
            P.tt(f["Kt"][:], f["K2"][:], f["EN"][:], ALU.mult)
            P.tt(f["Rt"][:], f["Rs"][:], f["EP"][:], ALU.mult)
            eg = f["EP"][:, :, 127:128].to_broadcast([64, 8, 128])
            P.tt(f["Bh"][:], f["Bt"][:], eg, ALU.mult)
            P.tt(f["Kh"][:], f["Kt"][:], eg, ALU.mult)
            P.tt(f["T0"][:], f["Rs"][:], f["K2"][:], ALU.mult)
            P.tt(f["T0"][:], f["T0"][:], bc(rwp[:, 6, :]), ALU.mult)
            pb = nb()
            for h in range(H):
                P.mm(pb[:, h:h + 1], f["T0"][:, h, :], cx.ones_f[:64, 0:1])
            P.copy(RKS[:], pb[:, 0:8], eng="act")
            t = tm
            P.dma("sp", t["Vc"][:], ptm[t0:t0 + 128, C_RWV:C_RWV + 512])
            if t0 == 0:
                P.memset(t["Vp"][0:1, :], 0.0)
                P.dma("sp", t["Vp"][1:128, :], ptm[0:127, C_RWV:C_RWV + 512])
            else:
                P.dma("sp", t["Vp"][:], ptm[t0 - 1:t0 + 127, C_RWV:C_RWV + 512])
            P.tt(t["Vp"][:], t["Vp"][:], t["Vc"][:], ALU.subtract)
            P.tt(t["Vp"][:], t["Vp"][:], tmb[:, 0, :], ALU.mult)
            if layer == 0:
                P.tt(t["V"][:], t["Vp"][:], t["Vc"][:], ALU.add)
                P.dma("sp", vfirst[t0:t0 + 128, :], t["V"][:])
            else:
                P.tt(t["Vc"][:], t["Vp"][:], t["Vc"][:], ALU.add)
                shift_load(lo["VLc"], lo["VLp"], pfm[R_VLO:R_VLO + 64, :], t0, False)
                P.tt(lo["VLp"][:], lo["VLp"][:], lo["VLc"][:], ALU.subtract)
                P.stt(lo["VLc"][:], lo["VLp"][:], lop[:64, 4:5], lo["VLc"][:], ALU.mult, ALU.add)
                pb = nb()
                P.mm(pb[:, :], lo["VLc"][:], v2s[:])
                P.tt(t["SV"][:], pb[:, :], tmb[:, 1, :], ALU.add)
                P.act(t["SV"][:], t["SV"][:], AF.Sigmoid)
                P.dma("sp", t["VF"][:], vfirst[t0:t0 + 128, :])
                P.tt(t["VF"][:], t["VF"][:], t["Vc"][:], ALU.subtract)
                P.tt(t["VF"][:], t["VF"][:], t["SV"][:], ALU.mult)
                P.tt(t["V"][:], t["VF"][:], t["Vc"][:], ALU.add)
            pb = nb()
            for j in range(2):
                P.mm(pb[:, :], lo["GLc"][:, j, :], g2s[:, j, :], start=(j == 0), stop=(j == 1))
            P.copy(t["Gt"][:], pb[:, :], eng="act")
            M, N, PT = big["M0"], big["N0"], big["PT"]
            M2, N2 = big["M1"], big["N1"]
            for half in range(2):
                pa, pb = nb(), nb()
                for j in range(4):
                    h = half * 4 + j
                    P.mm(pa[:, j * 128:(j + 1) * 128], f["At"][:, h, :], f["Bt"][:, h, :])
                    P.mm(pb[:, j * 128:(j + 1) * 128], f["Bt"][:, h, :], f["At"][:, h, :])
                hh = slice(half * 4, half * 4 + 4)
                P.stt(M[:, hh, :], b3(pa), -1.0, m_gt4[:], ALU.mult, ALU.mult)
                P.stt(N[:, hh, :], b3(pb), -1.0, m_lt4[:], ALU.mult, ALU.mult)
                P.tt(PT[:, hh, :], N[:, hh, :], id4[:], ALU.add, eng="pool")
            for step in range(6):
                last = step == 5
                for half in range(2):
                    hh = slice(half * 4, half * 4 + 4)
                    pa = nb()
                    for j in range(4):
                        h = half * 4 + j
                        P.mm(pa[:, j * 128:(j + 1) * 128], N[:, h, :], M[:, h, :])
                    P.copy(M2[:, hh, :], b3(pa), eng="act")
                    if not last:
                        pb = nb()
                        for j in range(4):
                            h = half * 4 + j
                            P.mm(pb[:, j * 128:(j + 1) * 128], M[:, h, :], N[:, h, :])
                        P.copy(N2[:, hh, :], b3(pb), eng="dve")
                    pc = nb()
                    for j in range(4):
                        h = half * 4 + j
                        P.mm(pc[:, j * 128:(j + 1) * 128], M2[:, h, :], PT[:, h, :])
                    P.tt(PT[:, hh, :], b3(pc), PT[:, hh, :], ALU.add)
                M, M2 = M2, M
                N, N2 = N2, N
            for (dst, lt, rt, msk) in ((big["LAK"], f["Kt"], f["At"], m_lt4), (big["MRB"], f["Bt"], f["Rt"], m_le4),
                                       (big["MRK"], f["Kt"], f["Rt"], m_le4)):
                for half in range(2):
                    pa = nb()
                    for j in range(4):
                        h = half * 4 + j
                        P.mm(pa[:, j * 128:(j + 1) * 128], lt[:, h, :], rt[:, h, :])
                    P.tt(dst[:, half * 4:half * 4 + 4, :], b3(pa), msk[:], ALU.mult)
            for (src, dst) in ((f["Bh"], t["BHt"]), (f["Kh"], t["KHt"])):
                pa = nb()
                for h in range(H):
                    P.transpose(pa[:, hs(h)], src[:, h, :], cx.ident_f[:64, :64])
                P.copy(dst[:], pa[:, :], eng="act")
            pa = nb()
            for h in range(H):
                P.mm(pa[:, hs(h)], big["LAK"][:, h, :], t["V"][:, hs(h)], start=True, stop=False)
                P.mm(pa[:, hs(h)], f["At"][:, h, :], ST[:, h, :], start=False, stop=True)
            P.ts(t["NXZ"][:], pa[:, :], -1.0, None, op0=ALU.mult)
            pa = nb()
            for h in range(H):
                P.mm(pa[:, hs(h)], PT[:, h, :], t["NXZ"][:, hs(h)])
            P.copy(t["SA"][:], pa[:, :], eng="act")
            pa = nb()
            for h in range(H):
                P.mm(pa[:, hs(h)], f["Rt"][:, h, :], ST[:, h, :], start=True, stop=False)
                P.mm(pa[:, hs(h)], big["MRB"][:, h, :], t["SA"][:, hs(h)], start=False, stop=False)
                P.mm(pa[:, hs(h)], big["MRK"][:, h, :], t["V"][:, hs(h)], start=False, stop=True)
            P.copy(t["Y"][:], pa[:, :], eng="act")
            pa = nb()
            for h in range(H):
                P.mm(pa[:64, hs(h)], t["BHt"][:, hs(h)], t["SA"][:, hs(h)], start=True, stop=False)
                P.mm(pa[:64, hs(h)], t["KHt"][:, hs(h)], t["V"][:, hs(h)], start=False, stop=True)
            P.tt(ST[:], ST[:], f["EP"][:, :, 127:128].to_broadcast([64, 8, 64]), ALU.mult)
            P.tt(ST[:], ST[:], pa[:64, :].rearrange("k (h v) -> k h v", h=8), ALU.add)
            Y3 = t["Y"][:].rearrange("p (h v) -> p h v", h=8)
            C3 = t["CEN"][:].rearrange("p (h v) -> p h v", h=8)
            S3 = t["SQ"][:].rearrange("p (h v) -> p h v", h=8)
            N3 = t["YN"][:].rearrange("p (h v) -> p h v", h=8)
            V3 = t["V"][:].rearrange("p (h v) -> p h v", h=8)
            B3 = t["BON"][:].rearrange("p (h v) -> p h v", h=8)
            P.add("dve", lambda e: e.tensor_reduce(MS[:], Y3, AX.X, ALU.add), [t["Y"][:]], [MS[:]])
            P.ts(MS[:], MS[:], 1.0 / 64, None, op0=ALU.mult)
            P.tt(C3, Y3, MS[:].unsqueeze(2).to_broadcast([128, 8, 64]), ALU.subtract)
            P.tt(S3, C3, C3, ALU.mult, eng="pool")
            P.add("dve", lambda e: e.tensor_reduce(VS[:], S3, AX.X, ALU.add), [t["SQ"][:]], [VS[:]])
            P.act(VS[:], VS[:], AF.Ln, bias=RW_LN_EPS, scale=1.0 / 64)
            P.act(VS[:], VS[:], AF.Exp, scale=-0.5)
            P.tt(N3, C3, VS[:].unsqueeze(2).to_broadcast([128, 8, 64]), ALU.mult)
            P.tt(t["YN"][:], t["YN"][:], tmb[:, 2, :], ALU.mult)
            P.tt(t["YN"][:], t["YN"][:], tmb[:, 3, :], ALU.add)
            P.tt(B3, V3, RKS[:].unsqueeze(2).to_broadcast([128, 8, 64]), ALU.mult, eng="pool")
            P.tt(t["YN"][:], t["YN"][:], t["BON"][:], ALU.add)
            P.tt(t["YN"][:], t["YN"][:], t["Gt"][:], ALU.mult)
            pa = nb()
            for j in range(4):
                P.transpose(pa[:, j * 128:(j + 1) * 128], t["YN"][:, j * 128:(j + 1) * 128], cx.ident_f[:])
            ob = OB[c % 2]
            P.copy(ob[:], b3(pa), eng="act")
            P.dma("sp", mixv[:, :, t0:t0 + 128], ob[:])
        P.flush()


from concourse.bass_utils import run_bass_kernel_spmd

DEPTH = 4
N_META = 16
RW_SHIFT_N = 1984


def build_program(L, depth=DEPTH, Lr=None):
    nc = bass.Bass("TRN2", target_bir_lowering=False)
    dt = lambda n, s, k="ExternalInput", d=F32: nc.dram_tensor(n, s, d, kind=k).ap()
    h0 = dt("h0", [D, L])
    gains = dt("gains", [depth, 3, 128, DC])
    f1wi = dt("ffn1_wi", [depth, D, 2 * DFF])
    f1wo = dt("ffn1_wo", [depth, DFF, D])
    f2wi = dt("ffn2_wi", [depth, D, 2 * DFF])
    f2wo = dt("ffn2_wo", [depth, DFF, D])
    w_in = dt("w_in", [depth, D, 7104])
    w_in_v = dt("w_in_v", [depth - 1, D, 64]) if depth > 1 else None
    w_out = dt("w_out", [depth, D, D])
    hglb = dt("hglb", [128, 4, 4])
    hgn = dt("hgn", [depth, 128, 1])
    sbg = dt("sbg", [depth, 128, 3])
    rwp = dt("rwp", [depth, 64, 7, 8])
    lop = dt("lop", [depth, 128, 8])
    w2 = dt("rw_w2", [depth, 96, 512])
    a2 = dt("rw_a2", [depth, 96, 512])
    g2 = dt("rw_g2", [depth, 128, 2, 512])
    v2 = dt("rw_v2", [depth, 64, 512])
    tmb = dt("tmb", [depth, 128, 5, 512])
    hT = dt("hT", [D, L], "ExternalOutput")
    pfm = dt("pfm", [NFM, L], "Internal")
    ptm = dt("ptm", [L, NTM], "Internal")
    vfirst = dt("vfirst", [L, 512], "Internal")
    mixT = dt("mixT", [D, L], "Internal", BF16)
    with ExitStack() as es:
        cx = make_ctx(nc, es)
        cx.eps_ap = lambda e: float(e)
        init_consts(cx)
        make_masks(cx, es)
        make_lb(cx, es, hglb)
        P = Phase(cx.st)
        P.dma("sp", hT, h0)
        P.flush()
        for l in range(depth):
            ffn_phase(cx, hT, f1wi[l], f1wo[l], gains[l, 0], L, Lr=Lr)
            proj_phase(cx, hT, w_in[l], (w_in_v[l - 1] if l > 0 else None), gains[l, 1], pfm, ptm, L)
            hg_phase(cx, pfm, ptm, l, hgn[l], mixT, L)
            sb_phase(cx, pfm, ptm, sbg[l], mixT, L)
            prm = {"rwp": rwp[l], "lop": lop[l], "w2": w2[l], "a2": a2[l], "g2": g2[l], "v2": v2[l], "tmb": tmb[l]}
            rw_phase(cx, pfm, ptm, l, prm, vfirst, mixT, L)
            ffn_phase(cx, hT, f2wi[l], f2wo[l], gains[l, 2], L, pre=(mixT, w_out[l]), Lr=Lr)
    return nc, cx.st.nops


def _c(a):
    return np.ascontiguousarray(a, dtype=np.float32)


def layout_params(inp, depth=DEPTH):
    fm16 = lambda v: v.reshape(DC, 128).T
    fmh = lambda v: v.reshape(8, 64).T
    out = {}
    out["gains"] = _c(np.stack([np.stack([fm16(inp[k][l]) for k in ("norm_ffn1", "norm_mix", "norm_ffn2")]) for l in range(depth)]))
    for k in ("ffn1_wi", "ffn1_wo", "ffn2_wi", "ffn2_wo", "w_in", "w_out"):
        out[k] = _c(inp[k][:depth])
    if depth > 1:
        out["w_in_v"] = _c(inp["w_in_v"][:depth - 1])
    out["hglb"] = _c(inp["hg_lb"].reshape(4, 4, 128).transpose(2, 1, 0))
    out["hgn"] = _c(inp["hg_norm"][:depth].reshape(depth, 128, 1))
    out["sbg"] = _c(np.stack([np.stack([inp["sb_qn"][l], inp["sb_kn"][l], inp["sb_on"][l]], axis=1) for l in range(depth)]))
    rwp, lop, v2, tmb = [], [], [], []
    for l in range(depth):
        mu = inp["rw_mu"][l]
        rwp.append(np.stack([fmh(mu[0:512]), fmh(mu[512:1024]), fmh(inp["rw_w0"][l]), fmh(inp["rw_a0"][l]), fmh(inp["rw_kk"][l]),
                             fmh(inp["rw_ka"][l]), fmh(inp["rw_rk"][l].reshape(-1))], axis=1))
        lp = np.zeros((128, 8), np.float32)
        lp[:96, 0] = mu[1536:1632]
        lp[:96, 1] = mu[1632:1728]
        lp[:, 2] = mu[1728:1856]
        lp[:, 3] = mu[1856:1984]
        tb = np.zeros((128, 5, 512), np.float32)
        tb[:, 0] = mu[1024:1536][None]
        tb[:, 2] = inp["rw_ln_w"][l][None]
        tb[:, 3] = inp["rw_ln_b"][l][None]
        vv = np.zeros((64, 512), np.float32)
        if l > 0:
            lp[:64, 4] = inp["rw_mu_v"][l - 1]
            tb[:, 1] = inp["rw_v0"][l - 1][None]
            vv = inp["rw_v2"][l - 1]
        lop.append(lp)
        tmb.append(tb)
        v2.append(vv)
    out["rwp"] = _c(np.stack(rwp))
    out["lop"] = _c(np.stack(lop))
    out["tmb"] = _c(np.stack(tmb))
    out["rw_v2"] = _c(np.stack(v2))
    out["rw_w2"] = _c(inp["rw_w2"][:depth])
    out["rw_a2"] = _c(inp["rw_a2"][:depth])
    out["rw_g2"] = _c(np.stack([inp["rw_g2"][l].reshape(2, 128, 512).transpose(1, 0, 2) for l in range(depth)]))
    return out


def kernel(**inputs):
    inp = {k: np.asarray(v) for k, v in inputs.items()}
    x = inp["x"]
    B, S, _ = x.shape
    depth = inp["norm_ffn1"].shape[0]
    L_real = N_META + S
    L = ((L_real + 127) // 128) * 128
    nc, nops = build_program(L, depth, L_real)
    shared = layout_params(inp, depth)
    n_cores = 8 if B == 4 else B
    in_maps = []
    for c in range(n_cores):
        b = c % B
        h0 = np.zeros((L, D), np.float32)
        h0[:N_META] = inp["meta"]
        h0[N_META:L_real] = x[b]
        m = dict(shared)
        m["h0"] = _c(h0.T)
        in_maps.append(m)
    res = run_bass_kernel_spmd(nc, in_maps, core_ids=list(range(n_cores)))
    out = np.stack([np.ascontiguousarray(res.results[b]["hT"].T[N_META:L_real]) for b in range(B)])
    return out.astype(np.float32)
```

```python
from contextlib import ExitStack
import math
import numpy as np
import concourse.bass as bass
import concourse.mybir as mybir

F32 = mybir.dt.float32
BF16 = mybir.dt.bfloat16
AF = mybir.ActivationFunctionType
ALU = mybir.AluOpType
AX = mybir.AxisListType

ENGS = ("pe", "act", "dve", "pool", "sp")
NDMA = 12


def region(ap):
    t = ap.tensor
    shp = list(t.shape)
    rowlen = 1
    for s in shp[1:]:
        rowlen *= s
    off = int(ap.offset)
    r0 = off // rowlen
    c0 = off % rowlen
    rext = 0
    cext = 0
    for step, cnt in ap.ap:
        step = abs(int(step))
        cnt = int(cnt)
        if cnt <= 1 or step == 0:
            continue
        if step >= rowlen and step % rowlen == 0:
            rext += (cnt - 1) * (step // rowlen)
        else:
            cext += (cnt - 1) * step
    c1 = c0 + cext + 1
    if c1 > rowlen:
        extra = (c1 - 1) // rowlen
        rext += extra
        c0, c1 = 0, rowlen
    return (ap.name, r0, r0 + rext + 1, c0, c1)


class State:
    def __init__(self, nc, es):
        self.nc = nc
        self.sem = {}
        self.cnt = {}
        for e in ENGS:
            self.sem[e] = es.enter_context(nc.semaphore("s_" + e))
            self.cnt[e] = 0
        self.dsem = {}
        self.dcnt = {}
        self.dnext = {}
        for q in ("sp", "pool", "act"):
            self.dsem[q] = [es.enter_context(nc.semaphore("d_%s%d" % (q, i))) for i in range(NDMA)]
            self.dcnt[q] = [0] * NDMA
            self.dnext[q] = 0
        self.waited = {e: {} for e in ENGS}
        self.nops = 0


class Phase:
    def __init__(self, st):
        self.st = st
        self.nc = st.nc
        self.ops = {e: [] for e in ENGS}
        self.recs = {}
        self.order = 0

    def _deps(self, reads, writes):
        deps = {}

        def add(done):
            k = done[0]
            if k not in deps or deps[k][2] < done[2]:
                deps[k] = done

        for ap in reads:
            nm, r0, r1, c0, c1 = region(ap)
            for rec in self.recs.get(nm, ()):
                if rec[5] and rec[0] < r1 and r0 < rec[1] and rec[2] < c1 and c0 < rec[3]:
                    add(rec[4])
        for ap in writes:
            nm, r0, r1, c0, c1 = region(ap)
            for rec in self.recs.get(nm, ()):
                if rec[0] < r1 and r0 < rec[1] and rec[2] < c1 and c0 < rec[3]:
                    add(rec[4])
        return deps

    def _record(self, reads, writes, done):
        for ap in writes:
            nm, r0, r1, c0, c1 = region(ap)
            lst = self.recs.setdefault(nm, [])
            lst[:] = [rc for rc in lst if not (r0 <= rc[0] and rc[1] <= r1 and c0 <= rc[2] and rc[3] <= c1)]
            lst.append([r0, r1, c0, c1, done, True])
        for ap in reads:
            nm, r0, r1, c0, c1 = region(ap)
            lst = self.recs.setdefault(nm, [])
            lst[:] = [rc for rc in lst if not ((not rc[5]) and rc[4][0] == done[0] and r0 <= rc[0] and rc[1] <= r1 and c0 <= rc[2] and rc[3] <= c1)]
            lst.append([r0, r1, c0, c1, done, False])

    def add(self, eng, fn, reads, writes, pe_skip=True):
        st = self.st
        deps = self._deps(reads, writes)
        st.cnt[eng] += 1
        done = ("e_" + eng, st.sem[eng], st.cnt[eng])
        waits = []
        for k, d in deps.items():
            if eng == "pe" and k == "e_pe":
                continue
            if st.waited[eng].get(k, 0) >= d[2]:
                continue
            st.waited[eng][k] = d[2]
            waits.append((d[1], d[2]))
        self.ops[eng].append((fn, waits, (st.sem[eng], 1)))
        self._record(reads, writes, done)
        st.nops += 1

    def dma(self, q, out, in_, **kw):
        st = self.st
        reads, writes = [in_], [out]
        deps = self._deps(reads, writes)
        i = st.dnext[q]
        st.dnext[q] = (i + 1) % NDMA
        sem = st.dsem[q][i]
        key = "d_%s%d" % (q, i)
        waits = []
        if st.dcnt[q][i] > 0 and st.waited[q].get(key, 0) < st.dcnt[q][i]:
            waits.append((sem, st.dcnt[q][i]))
            st.waited[q][key] = st.dcnt[q][i]
        st.dcnt[q][i] += 16
        done = (key, sem, st.dcnt[q][i])
        for k, d in deps.items():
            if st.waited[q].get(k, 0) >= d[2]:
                continue
            st.waited[q][k] = d[2]
            waits.append((d[1], d[2]))
        self.ops[q].append((lambda e: e.dma_start(out=out, in_=in_, **kw), waits, (sem, 16)))
        self._record(reads, writes, done)
        st.nops += 1

    def mm(self, out, lhsT, rhs, start=True, stop=True):
        self.add("pe", lambda e: e.matmul(out, lhsT, rhs, start=start, stop=stop), [lhsT, rhs], [out])

    def transpose(self, out, in_, ident):
        self.add("pe", lambda e: e.transpose(out, in_, ident), [in_, ident], [out])

    def act(self, out, in_, func, bias=None, scale=1.0, eng="act"):
        rd = [in_]
        kw = {}
        if bias is not None:
            kw["bias"] = bias
            if not isinstance(bias, (int, float)):
                rd.append(bias)
        if not isinstance(scale, (int, float)):
            rd.append(scale)
        self.add("act", lambda e: e.activation(out, in_, func, scale=scale, **kw), rd, [out])

    def tt(self, out, in0, in1, op, eng="dve"):
        self.add(eng, lambda e: e.tensor_tensor(out, in0, in1, op), [in0, in1], [out])

    def ts(self, out, in0, s1, s2=None, op0=ALU.mult, op1=ALU.bypass, eng="dve"):
        rd = [in0]
        for s in (s1, s2):
            if s is not None and not isinstance(s, (int, float)):
                rd.append(s)
        if s2 is None:
            self.add(eng, lambda e: e.tensor_scalar(out, in0, s1, None, op0), rd, [out])
        else:
            self.add(eng, lambda e: e.tensor_scalar(out, in0, s1, s2, op0, op1), rd, [out])

    def stt(self, out, in0, scalar, in1, op0, op1):
        rd = [in0, in1]
        if not isinstance(scalar, (int, float)):
            rd.append(scalar)
        self.add("dve", lambda e: e.scalar_tensor_tensor(out, in0, scalar, in1, op0, op1), rd, [out])

    def copy(self, out, in_, eng="dve"):
        if eng == "act":
            self.add("act", lambda e: e.copy(out, in_), [in_], [out])
        else:
            self.add(eng, lambda e: e.tensor_copy(out, in_), [in_], [out])

    def memset(self, out, val, eng="dve"):
        self.add(eng, lambda e: e.memset(out, val), [], [out])

    def recip(self, out, in_):
        self.add("dve", lambda e: e.reciprocal(out, in_), [in_], [out])

    def scan(self, out, d0, d1, init, op0, op1):
        rd = [d0, d1]
        if not isinstance(init, (int, float)):
            rd.append(init)
        self.add("dve", lambda e: e.tensor_tensor_scan(out, d0, d1, init, op0, op1), rd, [out])

    def flush(self, final=False):
        st = self.st
        nc = self.nc
        finals = []
        for e in ENGS:
            if st.cnt[e] > 0:
                finals.append(("e_" + e, st.sem[e], st.cnt[e]))
        for q in st.dsem:
            for i in range(NDMA):
                if st.dcnt[q][i] > 0:
                    finals.append(("d_%s%d" % (q, i), st.dsem[q][i], st.dcnt[q][i]))
        ops = self.ops
        emap = {"pe": "tensor", "act": "scalar", "dve": "vector", "pool": "gpsimd", "sp": "sync"}

        def mk(ename):
            def body(eng):
                for fn, waits, inc in ops[ename]:
                    for s, v in waits:
                        eng.wait_ge(s, v)
                    ins = fn(eng)
                    ins.then_inc(inc[0], inc[1])
                for k, s, v in finals:
                    if st.waited[ename].get(k, 0) >= v:
                        continue
                    st.waited[ename][k] = v
                    eng.wait_ge(s, v)
            return body

        with nc.Block() as block:
            for ename in ENGS:
                getattr(block, emap[ename])(mk(ename))
        self.ops = {e: [] for e in ENGS}
        self.recs = {}


def _coll(self, in_ap, out_ap, groups):
    st = self.st
    q = "pool"
    reads, writes = [in_ap], [out_ap]
    deps = self._deps(reads, writes)
    i = st.dnext[q]
    st.dnext[q] = (i + 1) % NDMA
    sem = st.dsem[q][i]
    key = "d_%s%d" % (q, i)
    waits = []
    if st.dcnt[q][i] > 0 and st.waited[q].get(key, 0) < st.dcnt[q][i]:
        waits.append((sem, st.dcnt[q][i]))
        st.waited[q][key] = st.dcnt[q][i]
    st.dcnt[q][i] += 16
    done = (key, sem, st.dcnt[q][i])
    for k, d in deps.items():
        if st.waited[q].get(k, 0) >= d[2]:
            continue
        st.waited[q][k] = d[2]
        waits.append((d[1], d[2]))
    self.ops[q].append((lambda e: e.collective_compute("AllGather", ALU.bypass, replica_groups=groups, ins=[in_ap], outs=[out_ap]), waits, (sem, 16)))
    self._record(reads, writes, done)
    st.nops += 1


Phase.allgather = _coll


D = 2048
DC = 16
DFF = 5632
FC = 44
EPS = 1e-6


def tiles_of(L, TT=512):
    out = []
    t = 0
    while t < L:
        n = min(TT, L - t)
        out.append((t, n))
        t += n
    return out


class Ctx:
    pass


def make_ctx(nc, es):
    cx = Ctx()
    cx.nc = nc
    cx.uid = [0]
    cx.st = State(nc, es)
    cx.ps = [es.enter_context(nc.psum_tensor("ps%d" % i, [128, 512], F32)) for i in range(8)]
    cx.ones_bf = es.enter_context(nc.sbuf_tensor("ones_bf", [128, 128], BF16))
    cx.ones_f = es.enter_context(nc.sbuf_tensor("ones_f", [128, 128], F32))
    cx.ident_f = es.enter_context(nc.sbuf_tensor("ident_f", [128, 128], F32))
    return cx


def init_consts(cx):
    P = Phase(cx.st)
    P.memset(cx.ones_bf[:], 1.0)
    P.memset(cx.ones_f[:], 1.0)
    nc = cx.nc
    P.add("pool", lambda e: e.affine_select(cx.ident_f[:], cx.ones_f[:], pattern=[[1, 128]], compare_op=ALU.is_equal,
                                             fill=0.0, base=0, channel_multiplier=-1), [cx.ones_f[:]], [cx.ident_f[:]])
    P.flush()


def rmsnorm_fm(P, cx, h, g, u, sq, rstd, ncn, TT, dn, eps, psb, extra_scale=1.0):
    for c in range(ncn):
        P.act(sq[:, c, :TT], h[:, c, :TT], AF.Square)
    for c in range(ncn):
        P.mm(psb[:, :TT], cx.ones_bf[:], sq[:, c, :TT], start=(c == 0), stop=(c == ncn - 1))
    P.act(rstd[:, :TT], psb[:, :TT], AF.Sqrt, bias=cx.eps_ap(eps), scale=1.0 / dn)
    P.recip(rstd[:, :TT], rstd[:, :TT])
    if extra_scale != 1.0:
        P.ts(rstd[:, :TT], rstd[:, :TT], float(extra_scale), None, op0=ALU.mult)
    for c in range(ncn):
        P.stt(u[:, c, :TT], h[:, c, :TT], g[:, c:c + 1], rstd[:, :TT], ALU.mult, ALU.mult)


def ffn_phase(cx, hT, wi, wo, gvec, L, pre=None, Lr=None):
    nc = cx.nc
    with ExitStack() as es:
        cx.uid[0] += 1
        tg = "_%d" % cx.uid[0]
        sb = lambda n, s, d: es.enter_context(nc.sbuf_tensor(n + tg, s, d))
        h = sb("f_h", [128, DC, 512], F32)
        u = sb("f_u", [128, DC, 512], BF16)
        hid = sb("f_hid", [128, FC, 512], BF16)
        rstd = sb("f_rstd", [128, 512], F32)
        g = sb("f_g", [128, DC], F32)
        wg = [sb("f_wg%d" % i, [128, DC, 256], BF16) for i in range(2)]
        wu = [sb("f_wu%d" % i, [128, DC, 256], BF16) for i in range(2)]
        wos = [sb("f_wo%d" % i, [128, FC, 128], BF16) for i in range(2)]
        tmp = [sb("f_tmp%d" % i, [128, 512], F32) for i in range(2)]
        P = Phase(cx.st)
        P.dma("sp", g[:], gvec)
        hv = hT.rearrange("(c p) t -> p c t", p=128)
        ps = cx.ps
        for (t0, TT) in tiles_of(L):
            P.dma("sp", h[:, :, :TT], hv[:, :, t0:t0 + TT])
            if pre is not None:
                mixT, w_out = pre
                mv = mixT.rearrange("(c p) t -> p c t", p=128)
                P.dma("sp", u[:, :, :TT], mv[:, :, t0:t0 + TT])
                for ds in range(4):
                    slab = hid[:, (ds % 2) * 16:(ds % 2) * 16 + 16, :]
                    P.dma("pool", slab, w_out[ds])
                    for dj in range(4):
                        dc = ds * 4 + dj
                        po = ps[4 + dc % 2]
                        for mc in range(DC):
                            P.mm(po[:, :TT], slab[:, mc, dj * 128:(dj + 1) * 128], u[:, mc, :TT], start=(mc == 0), stop=(mc == DC - 1))
                        P.tt(h[:, dc, :TT], po[:, :TT], h[:, dc, :TT], ALU.add)
            rmsnorm_fm(P, cx, h, g, u, hid, rstd, DC, TT, D, EPS, ps[7])
            k = 0
            for j2 in range(FC // 2):
                b = j2 % 2
                P.dma("pool", wg[b][:], wi[j2])
                P.dma("pool", wu[b][:], wi[FC // 2 + j2])
                for jj in range(2):
                    j = 2 * j2 + jj
                    pa = ps[(k % 2) * 2]
                    pb = ps[(k % 2) * 2 + 1]
                    tm = tmp[k % 2]
                    k += 1
                    for c in range(DC):
                        P.mm(pa[:, :TT], wg[b][:, c, jj * 128:(jj + 1) * 128], u[:, c, :TT], start=(c == 0), stop=(c == DC - 1))
                    for c in range(DC):
                        P.mm(pb[:, :TT], wu[b][:, c, jj * 128:(jj + 1) * 128], u[:, c, :TT], start=(c == 0), stop=(c == DC - 1))
                    P.act(tm[:, :TT], pa[:, :TT], AF.Silu)
                    P.tt(hid[:, j, :TT], tm[:, :TT], pb[:, :TT], ALU.mult)
            for dc in range(DC):
                b = dc % 2
                P.dma("pool", wos[b][:], wo[dc])
                po = ps[4 + b]
                for j in range(FC):
                    P.mm(po[:, :TT], wos[b][:, j, :], hid[:, j, :TT], start=(j == 0), stop=(j == FC - 1))
                P.stt(h[:, dc, :TT], po[:, :TT], 0.5, h[:, dc, :TT], ALU.mult, ALU.add)
            if Lr is not None and t0 + TT > Lr:
                P.memset(h[:, :, max(0, Lr - t0):TT], 0.0)
            P.dma("sp", hv[:, :, t0:t0 + TT], h[:, :, :TT])
        P.flush()


R_HGQ, R_HGF, R_HGG = 0, 512, 1024
R_SBQ, R_SBK = 1536, 2560
R_RWR, R_RWK = 3584, 4096
R_WLO, R_ALO, R_GLO, R_VLO = 4608, 4736, 4864, 5120
NFM = 5248
C_HGI, C_SBV, C_RWV = 0, 512, 1536
NTM = 2048

SLABS = [
    (0, 512, "fm", R_HGQ), (512, 512, "fm", R_HGF), (1536, 512, "fm", R_HGG),
    (2048, 512, "fm", R_SBQ), (2560, 512, "fm", R_SBQ + 512),
    (3072, 512, "fm", R_SBK), (3584, 512, "fm", R_SBK + 512),
    (5120, 512, "fm", R_RWR), (5632, 512, "fm", R_RWK),
    (6656, 448, "lo", 0),
    (1024, 512, "tm", C_HGI), (4096, 512, "tm", C_SBV), (4608, 512, "tm", C_SBV + 512), (6144, 512, "tm", C_RWV),
]


def proj_phase(cx, hT, w_in, w_in_v, gvec, pfm, ptm, L):
    nc = cx.nc
    with ExitStack() as es:
        cx.uid[0] += 1
        tg = "_%d" % cx.uid[0]
        sb = lambda n, s, d: es.enter_context(nc.sbuf_tensor(n + tg, s, d))
        h = sb("p_h", [128, DC, 512], F32)
        u = sb("p_u", [128, DC, 512], BF16)
        sq = sb("p_sq", [128, DC, 512], BF16)
        rstd = sb("p_rstd", [128, 512], F32)
        g = sb("p_g", [128, DC], F32)
        ws = [sb("p_w%d" % i, [128, DC, 512], BF16) for i in range(2)]
        wv = sb("p_wv", [128, DC, 64], BF16)
        stg = [sb("p_stg%d" % i, [128, 512], F32) for i in range(4)]
        P = Phase(cx.st)
        P.dma("sp", g[:], gvec)
        hv = hT.rearrange("(c p) t -> p c t", p=128)
        if w_in_v is not None:
            P.dma("pool", wv[:], w_in_v)
        ps = cx.ps
        k = 0
        nslab = 0
        for (t0, TT) in tiles_of(L):
            P.dma("sp", h[:, :, :TT], hv[:, :, t0:t0 + TT])
            rmsnorm_fm(P, cx, h, g, u, sq, rstd, DC, TT, D, EPS, ps[7])

            def fm_chunk(wt, cs, M, drow):
                nonlocal k
                pb = ps[k % 4]
                sg = stg[k % 4]
                for c in range(DC):
                    P.mm(pb[:M, :TT], wt[:, c, cs:cs + M], u[:, c, :TT], start=(c == 0), stop=(c == DC - 1))
                if k % 2 == 0:
                    P.copy(sg[:M, :TT], pb[:M, :TT], eng="act")
                else:
                    P.copy(sg[:M, :TT], pb[:M, :TT], eng="dve")
                P.dma("sp", pfm[drow:drow + M, t0:t0 + TT], sg[:M, :TT])
                k += 1

            for (c0, ncol, kind, dst) in SLABS:
                wt = ws[nslab % 2]
                nslab += 1
                P.dma("pool", wt[:], w_in[c0 // 512])
                if kind == "fm":
                    for j in range(ncol // 128):
                        fm_chunk(wt, j * 128, 128, dst + j * 128)
                elif kind == "lo":
                    fm_chunk(wt, 0, 96, R_WLO)
                    fm_chunk(wt, 96, 96, R_ALO)
                    fm_chunk(wt, 192, 128, R_GLO)
                    fm_chunk(wt, 320, 128, R_GLO + 128)
                else:
                    for tb in range(TT // 128):
                        pb = ps[k % 4]
                        sg = stg[k % 4]
                        for c in range(DC):
                            P.mm(pb[:, :ncol], u[:, c, tb * 128:(tb + 1) * 128], wt[:, c, :ncol], start=(c == 0), stop=(c == DC - 1))
                        if k % 2 == 0:
                            P.copy(sg[:, :ncol], pb[:, :ncol], eng="act")
                        else:
                            P.copy(sg[:, :ncol], pb[:, :ncol], eng="dve")
                        P.dma("sp", ptm[t0 + tb * 128:t0 + (tb + 1) * 128, dst:dst + ncol], sg[:, :ncol])
                        k += 1
            if w_in_v is not None:
                fm_chunk(wv, 0, 64, R_VLO)
        P.flush()


SB_HEADS = 8
R_MIX_HG, R_MIX_SB, R_MIX_RW = 0, 512, 1536


def rmsnorm1(P, cx, x, gcol, out, sqs, rstd, TT, dn, eps, psb, extra=1.0):
    P.act(sqs[:, :TT], x, AF.Square)
    P.mm(psb[:, :TT], cx.ones_bf[:], sqs[:, :TT], start=True, stop=True)
    P.act(rstd[:, :TT], psb[:, :TT], AF.Ln, bias=float(eps), scale=1.0 / dn)
    P.act(rstd[:, :TT], rstd[:, :TT], AF.Exp, bias=float(math.log(extra)), scale=-0.5)
    P.stt(out, x, gcol, rstd[:, :TT], ALU.mult, ALU.mult)


def make_masks(cx, es):
    nc = cx.nc
    sb = lambda n, s, d: es.enter_context(nc.sbuf_tensor(n, s, d))
    cx.ones_w = sb("ones_w", [128, 896], BF16)
    cx.tri_incl = sb("tri_incl", [128, 128], BF16)
    cx.mw = sb("mw", [128, 896], BF16)
    cx.m_le = sb("m_le", [128, 128], BF16)
    cx.m_lt_f = sb("m_lt_f", [128, 128], F32)
    cx.m_le_f = sb("m_le_f", [128, 128], F32)
    cx.m_gt_f = sb("m_gt_f", [128, 128], F32)
    P = Phase(cx.st)
    P.memset(cx.ones_w[:], 1.0)

    def sel(out, in_, pat, cm, base, op):
        P.add("pool", lambda e: e.affine_select(out, in_, pattern=pat, compare_op=op, fill=0.0, base=base,
                                                 channel_multiplier=cm), [in_], [out])
    sel(cx.tri_incl[:], cx.ones_w[:, 0:128], [[-1, 128]], 1, 0, ALU.is_ge)
    sel(cx.mw[:], cx.ones_w[:], [[1, 896]], -1, -384, ALU.is_gt)
    sel(cx.m_le[:], cx.ones_w[:, 0:128], [[1, 128]], -1, 0, ALU.is_ge)
    sel(cx.m_lt_f[:], cx.ones_f[:], [[1, 128]], -1, 0, ALU.is_gt)
    sel(cx.m_le_f[:], cx.ones_f[:], [[1, 128]], -1, 0, ALU.is_ge)
    sel(cx.m_gt_f[:], cx.ones_f[:], [[-1, 128]], 1, 0, ALU.is_gt)
    P.flush()


def sb_phase(cx, pfm, ptm, gains, mixT, L):
    nc = cx.nc
    NT = L // 128
    with ExitStack() as es:
        cx.uid[0] += 1
        tg = "_%d" % cx.uid[0]
        sb = lambda n, s, d: es.enter_context(nc.sbuf_tensor(n + tg, s, d))
        gn = sb("s_gn", [128, 3], F32)
        qn = [sb("s_qn%d" % i, [128, L], BF16) for i in range(2)]
        kn = [sb("s_kn%d" % i, [128, L], BF16) for i in range(2)]
        vh = [sb("s_v%d" % i, [128, NT, 128], BF16) for i in range(2)]
        xin = [sb("s_x%d" % i, [128, 512], F32) for i in range(2)]
        sqs = sb("s_sq", [128, 512], BF16)
        rstd = sb("s_rstd", [128, 512], F32)
        E = [sb("s_E%d" % i, [128, 512], F32) for i in range(2)]
        Lb = [sb("s_L%d" % i, [128, 512], BF16) for i in range(2)]
        T1 = [sb("s_T1%d" % i, [128, 512], F32) for i in range(2)]
        T2 = [sb("s_T2%d" % i, [128, 512], F32) for i in range(2)]
        At = [sb("s_A%d" % i, [128, 512], BF16) for i in range(2)]
        Cs = sb("s_Cs", [128, 512], F32)
        oh = sb("s_oh", [128, 512], F32)
        ob = [sb("s_ob%d" % i, [128, 512], BF16) for i in range(2)]
        ps = cx.ps
        P = Phase(cx.st)
        P.dma("sp", gn[:], gains)
        ptv = ptm.rearrange("(n p) c -> p n c", p=128)
        kstep = 0
        for hd in range(SB_HEADS):
            b = hd % 2
            for n0 in range(0, NT, 8):
                n1 = min(NT, n0 + 8)
                P.dma("pool", vh[b][:, n0:n1, :], ptv[:, n0:n1, C_SBV + hd * 128:C_SBV + (hd + 1) * 128])
            i = 0
            for (t0, TT) in tiles_of(L):
                for (row, gi, dst, extra) in ((R_SBQ, 0, qn[b], 128.0 ** -0.5), (R_SBK, 1, kn[b], 1.0)):
                    x = xin[i % 2]
                    i += 1
                    P.dma("sp", x[:, :TT], pfm[row + hd * 128:row + (hd + 1) * 128, t0:t0 + TT])
                    rmsnorm1(P, cx, x[:, :TT], gn[:, gi:gi + 1], dst[:, t0:t0 + TT], sqs, rstd, TT, 128, EPS, ps[7], extra)
            for (t0, TQ) in tiles_of(L):
                sb_max = (t0 + TQ - 1) // 128
                P.memset(Cs[:, :TQ], 0.0, eng="pool")
                po = ps[6]
                for sbk in range(sb_max, -1, -1):
                    w = kstep % 2
                    kstep += 1
                    pa, pb, pc = ps[w * 3], ps[w * 3 + 1], ps[w * 3 + 2]
                    off = sbk * 128 - t0
                    diag = off >= 0
                    P.mm(pa[:, :TQ], kn[b][:, sbk * 128:(sbk + 1) * 128], qn[b][:, t0:t0 + TQ])
                    P.act(E[w][:, :TQ], pa[:, :TQ], AF.Exp)
                    P.act(Lb[w][:, :TQ], E[w][:, :TQ], AF.Ln, bias=1.0)
                    if diag:
                        msk = cx.mw[:, 384 - off:384 - off + TQ]
                        P.tt(Lb[w][:, :TQ], Lb[w][:, :TQ], msk, ALU.mult, eng="pool")
                    P.mm(pb[:, :TQ], cx.tri_incl[:], Lb[w][:, :TQ])
                    P.mm(pc[:, :TQ], cx.ones_bf[:], Lb[w][:, :TQ])
                    P.tt(T1[w][:, :TQ], pb[:, :TQ], Cs[:, :TQ], ALU.add)
                    P.tt(T2[w][:, :TQ], pa[:, :TQ], T1[w][:, :TQ], ALU.subtract)
                    P.act(At[w][:, :TQ], T2[w][:, :TQ], AF.Exp)
                    if diag:
                        P.tt(At[w][:, :TQ], At[w][:, :TQ], msk, ALU.mult, eng="pool")
                    P.tt(Cs[:, :TQ], pc[:, :TQ], Cs[:, :TQ], ALU.add)
                    P.mm(po[:, :TQ], vh[b][:, sbk, :], At[w][:, :TQ], start=(sbk == sb_max), stop=(sbk == 0))
                P.copy(oh[:, :TQ], po[:, :TQ], eng="act")
                o2 = ob[(t0 // 512) % 2]
                rmsnorm1(P, cx, oh[:, :TQ], gn[:, 2:3], o2[:, :TQ], sqs, rstd, TQ, 128, EPS, ps[7])
                P.dma("sp", mixT[R_MIX_SB + hd * 128:R_MIX_SB + (hd + 1) * 128, t0:t0 + TQ], o2[:, :TQ])
        P.flush()


HG_HEADS = 4


def make_lb(cx, es, hg_lb_ap):
    nc = cx.nc
    sb = lambda n, s, d: es.enter_context(nc.sbuf_tensor(n, s, d))
    cx.lb = sb("lb", [128, 4, 4], F32)
    cx.oml = sb("oml", [128, 4, 4], F32)
    cx.noml = sb("noml", [128, 4, 4], F32)
    cx.rmask = sb("rmask", [128, 512], F32)
    with ExitStack() as es2:
        x = es2.enter_context(nc.sbuf_tensor("lb_x", [128, 4, 4], F32))
        e = es2.enter_context(nc.sbuf_tensor("lb_e", [128, 4, 4], F32))
        s = es2.enter_context(nc.sbuf_tensor("lb_s", [128, 4], F32))
        P = Phase(cx.st)
        P.dma("sp", x[:], hg_lb_ap)
        P.act(e[:], x[:], AF.Exp)
        P.tt(s[:], e[:, :, 0], e[:, :, 1], ALU.add)
        P.tt(s[:], s[:], e[:, :, 2], ALU.add)
        P.tt(s[:], s[:], e[:, :, 3], ALU.add)
        P.recip(s[:], s[:])
        P.memset(cx.lb[:, 0, :], 0.0)
        for l in range(1, 4):
            P.tt(e[:, :, l], e[:, :, l], s[:], ALU.mult)
            P.tt(cx.lb[:, l, :], cx.lb[:, l - 1, :], e[:, :, l], ALU.add)
        P.ts(cx.oml[:], cx.lb[:], -1.0, 1.0, op0=ALU.mult, op1=ALU.add)
        P.ts(cx.noml[:], cx.lb[:], 1.0, -1.0, op0=ALU.mult, op1=ALU.add)
        P.memset(cx.rmask[:], 1.0)
        for c in range(8):
            P.memset(cx.rmask[:, c * 64:c * 64 + 1], 0.0)
        P.flush()


def hg_phase(cx, pfm, ptm, layer, gnorm, mixT, L):
    nc = cx.nc
    NT = L // 128
    H = HG_HEADS
    with ExitStack() as es:
        cx.uid[0] += 1
        tg = "_%d" % cx.uid[0]
        sb = lambda n, s, d: es.enter_context(nc.sbuf_tensor(n + tg, s, d))
        gn = sb("g_gn", [128, 1], F32)
        V = [sb("g_v%d" % i, [64, 2 * NT, 128], BF16) for i in range(H)]
        X = [sb("g_x%d" % i, [128, 512], F32) for i in range(3)]
        SG = sb("g_sg", [128, 512], F32)
        FG = sb("g_fg", [128, 512], F32)
        QS = [sb("g_qs%d" % i, [128, 512], F32) for i in range(H)]
        KK = [sb("g_kk%d" % i, [128, 512], F32) for i in range(H)]
        G = [sb("g_G%d" % i, [128, 512], F32) for i in range(H)]
        NG = [sb("g_NG%d" % i, [128, 512], F32) for i in range(H)]
        EG = [sb("g_EG%d" % i, [128, 512], F32) for i in range(H)]
        QP = [sb("g_QP%d" % i, [128, 512], BF16) for i in range(H)]
        OH = [sb("g_OH%d" % i, [128, 512], F32) for i in range(H)]
        S = [sb("g_S%d" % i, [128, 128], F32) for i in range(H)]
        Sbf = [sb("g_Sb%d" % i, [128, 128], BF16) for i in range(H)]
        tmp = [sb("g_t%d" % i, [128, 64], F32) for i in range(6)]
        QT = [sb("g_QT%d" % i, [128, 64], BF16) for i in range(2)]
        KT = [sb("g_KT%d" % i, [128, 64], BF16) for i in range(2)]
        KH = [sb("g_KH%d" % i, [128, 64], F32) for i in range(2)]
        AM = [sb("g_AM%d" % i, [64, 64], BF16) for i in range(2)]
        AF32 = [sb("g_AF%d" % i, [64, 64], F32) for i in range(2)]
        KHt = [sb("g_KHt%d" % i, [64, 128], BF16) for i in range(2)]
        sqs = sb("g_sq", [128, 512], BF16)
        rstd = sb("g_rstd", [128, 512], F32)
        ON = sb("g_on", [128, 512], F32)
        MX = [sb("g_mx%d" % i, [128, 512], BF16) for i in range(2)]
        ps = cx.ps
        P = Phase(cx.st)
        P.dma("sp", gn[:], gnorm)
        ptv = ptm.rearrange("(n p) c -> p n c", p=64)
        for hd in range(H):
            for n0 in range(0, 2 * NT, 16):
                n1 = min(2 * NT, n0 + 16)
                P.dma("pool", V[hd][:, n0:n1, :], ptv[:, n0:n1, C_HGI + hd * 128:C_HGI + (hd + 1) * 128])
            P.memset(S[hd][:], 0.0)
            P.memset(Sbf[hd][:], 0.0)
        k = 0
        nm = 0
        for (t0, TT) in tiles_of(L):
            for hd in range(H):
                lb = cx.lb[:, layer, hd:hd + 1]
                oml = cx.oml[:, layer, hd:hd + 1]
                noml = cx.noml[:, layer, hd:hd + 1]
                P.dma("sp", X[0][:, :TT], pfm[R_HGF + hd * 128:R_HGF + (hd + 1) * 128, t0:t0 + TT])
                P.dma("sp", X[1][:, :TT], pfm[R_HGQ + hd * 128:R_HGQ + (hd + 1) * 128, t0:t0 + TT])
                P.act(SG[:, :TT], X[0][:, :TT], AF.Sigmoid)
                P.ts(FG[:, :TT], SG[:, :TT], oml, lb, op0=ALU.mult, op1=ALU.add)
                P.act(FG[:, :TT], FG[:, :TT], AF.Ln)
                P.ts(KK[hd][:, :TT], SG[:, :TT], noml, oml, op0=ALU.mult, op1=ALU.add)
                P.act(QS[hd][:, :TT], X[1][:, :TT], AF.Silu)
                P.scan(G[hd][:, :TT], cx.rmask[:, :TT], FG[:, :TT], 0.0, ALU.mult, ALU.add)
                P.ts(NG[hd][:, :TT], G[hd][:, :TT], -1.0, None, op0=ALU.mult)
                P.act(EG[hd][:, :TT], G[hd][:, :TT], AF.Exp)
                P.tt(QP[hd][:, :TT], QS[hd][:, :TT], EG[hd][:, :TT], ALU.mult)
            for c in range(TT // 64):
                blk = t0 // 64 + c
                cs = slice(c * 64, (c + 1) * 64)
                mid = c * 64 + 31
                end = c * 64 + 63
                for hd in range(H):
                    w = k % 2
                    k += 1
                    t1, t2, t3 = tmp[w * 3], tmp[w * 3 + 1], tmp[w * 3 + 2]
                    P.act(t1[:], G[hd][:, cs], AF.Exp, bias=NG[hd][:, mid:mid + 1])
                    P.stt(QT[w][:], t1[:], 1e30, QS[hd][:, cs], ALU.min, ALU.mult)
                    P.act(t2[:], G[hd][:, cs], AF.Exp, bias=G[hd][:, mid:mid + 1], scale=-1.0)
                    P.stt(KT[w][:], t2[:], 1e30, KK[hd][:, cs], ALU.min, ALU.mult)
                    P.act(t3[:], G[hd][:, cs], AF.Exp, bias=G[hd][:, end:end + 1], scale=-1.0)
                    P.tt(KH[w][:], KK[hd][:, cs], t3[:], ALU.mult, eng="pool")
                    pa, po, pt, pn = ps[w], ps[2 + w], ps[4 + w], ps[6]
                    P.mm(pa[:64, :64], KT[w][:], QT[w][:])
                    P.ts(AF32[w][:], pa[:64, :64], 1e30, -1e30, op0=ALU.min, op1=ALU.max)
                    P.tt(AM[w][:], AF32[w][:], cx.m_le[:64, :64], ALU.mult)
                    P.mm(po[:, :64], V[hd][:, blk, :], AM[w][:], start=True, stop=False)
                    P.mm(po[:, :64], Sbf[hd][:], QP[hd][:, cs], start=False, stop=True)
                    P.copy(OH[hd][:, cs], po[:, :64], eng="act")
                    P.transpose(pt[:64, :128], KH[w][:], cx.ident_f[:])
                    P.copy(KHt[w][:], pt[:64, :128], eng="dve")
                    P.mm(pn[:, :128], KHt[w][:], V[hd][:, blk, :])
                    P.stt(S[hd][:], S[hd][:], EG[hd][:, end:end + 1], pn[:, :128], ALU.mult, ALU.add)
                    P.copy(Sbf[hd][:], S[hd][:], eng="act")
            for hd in range(H):
                P.dma("sp", X[2][:, :TT], pfm[R_HGG + hd * 128:R_HGG + (hd + 1) * 128, t0:t0 + TT])
                P.act(X[2][:, :TT], X[2][:, :TT], AF.Silu)
                rmsnorm1(P, cx, OH[hd][:, :TT], gn[:, 0:1], ON[:, :TT], sqs, rstd, TT, 128, EPS, ps[7])
                mx = MX[nm % 2]
                nm += 1
                P.tt(mx[:, :TT], ON[:, :TT], X[2][:, :TT], ALU.mult)
                P.dma("sp", mixT[R_MIX_HG + hd * 128:R_MIX_HG + (hd + 1) * 128, t0:t0 + TT], mx[:, :TT])
        P.flush()


RW_H = 8
CW = -0.6065306597126334
RW_LN_EPS = 64e-5


def rw_phase(cx, pfm, ptm, layer, prm, vfirst, mixT, L):
    nc = cx.nc
    NT = L // 128
    H = RW_H
    with ExitStack() as es:
        cx.uid[0] += 1
        tg = "_%d" % cx.uid[0]
        sb = lambda n, s, d=F32: es.enter_context(nc.sbuf_tensor(n + tg, s, d))
        rwp = sb("r_rwp", [64, 7, 8])
        omka = sb("r_omka", [64, 8])
        lop = sb("r_lop", [128, 8])
        w2s = sb("r_w2", [96, 512])
        a2s = sb("r_a2", [96, 512])
        g2s = sb("r_g2", [128, 2, 512])
        v2s = sb("r_v2", [64, 512])
        tmb = sb("r_tmb", [128, 5, 512])
        m_gt4 = sb("r_mgt4", [128, 4, 128])
        m_lt4 = sb("r_mlt4", [128, 4, 128])
        m_le4 = sb("r_mle4", [128, 4, 128])
        id4 = sb("r_id4", [128, 4, 128])
        rmh = sb("r_rmh", [64, 8, 128])
        ST = sb("r_ST", [64, 8, 64])
        fm = {}
        for n in ("Rc", "Rp", "Kc", "Kp", "Rs", "Ks", "SW", "CS", "EP", "EN", "EX", "A", "KKn", "Bv",
                  "K2", "At", "Bt", "Kt", "Rt", "Bh", "Kh"):
            fm[n] = sb("r_f" + n, [64, 8, 128])
        fm["T0"], fm["T1"], fm["KK0"], fm["CX"] = fm["Rp"], fm["Kp"], fm["Rc"], fm["Kc"]
        lo = {}
        for n in ("WLc", "WLp", "ALc", "ALp"):
            lo[n] = sb("r_l" + n, [96, 128])
        for n in ("GLc", "GLp"):
            lo[n] = sb("r_l" + n, [128, 2, 128])
        for n in ("VLc", "VLp"):
            lo[n] = sb("r_l" + n, [64, 128])
        tm = {}
        for n in ("Vc", "Vp", "V", "VF", "SV", "Gt", "NXZ", "SA", "Y", "YN", "BHt", "KHt"):
            tm[n] = sb("r_t" + n, [128, 512])
        tm["CEN"], tm["SQ"], tm["BON"] = tm["Y"], tm["NXZ"], tm["SA"]
        MS = sb("r_MS", [128, 8])
        VS = sb("r_VS", [128, 8])
        RKS = sb("r_RKS", [128, 8])
        big = {}
        for n in ("M0", "M1", "N0", "N1", "PT", "LAK", "MRB", "MRK"):
            big[n] = sb("r_b" + n, [128, 8, 128])
        OB = [sb("r_OB%d" % i, [128, 4, 128], BF16) for i in range(2)]
        ps = cx.ps
        P = Phase(cx.st)
        bank = [0]

        def nb():
            b = ps[bank[0] % 8]
            bank[0] += 1
            return b

        def b3(p_, h=4):
            return p_[:].rearrange("p (h t) -> p h t", h=h)

        P.dma("sp", rwp[:], prm["rwp"])
        P.dma("sp", lop[:], prm["lop"])
        P.dma("sp", w2s[:], prm["w2"])
        P.dma("sp", a2s[:], prm["a2"])
        P.dma("sp", g2s[:], prm["g2"])
        P.dma("sp", tmb[:], prm["tmb"])
        if layer > 0:
            P.dma("sp", v2s[:], prm["v2"])
        P.ts(omka[:], rwp[:, 5, :], -1.0, 1.0, op0=ALU.mult, op1=ALU.add)
        for j in range(4):
            P.copy(m_gt4[:, j, :], cx.m_gt_f[:])
            P.copy(m_lt4[:, j, :], cx.m_lt_f[:])
            P.copy(m_le4[:, j, :], cx.m_le_f[:])
            P.copy(id4[:, j, :], cx.ident_f[:])
        P.memset(rmh[:], 1.0)
        P.memset(rmh[:, :, 0:1], 0.0)
        P.memset(ST[:], 0.0)

        def bc(ap2, n=128):
            return ap2.unsqueeze(2).to_broadcast([ap2.shape[0], ap2.shape[1], n])

        def fmv(row0):
            return pfm[row0:row0 + 512, :].rearrange("(h k) t -> k h t", k=64)

        rv, kv = fmv(R_RWR), fmv(R_RWK)
        mixv = mixT[R_MIX_RW:R_MIX_RW + 512, :].rearrange("(j p) t -> p j t", p=128)
        hs = lambda h: slice(h * 64, (h + 1) * 64)

        def shift_load(cur, prev, src3, t0, three):
            if three:
                P.dma("sp", cur[:], src3[:, :, t0:t0 + 128])
                if t0 == 0:
                    P.memset(prev[:, :, 0:1], 0.0)
                    P.dma("sp", prev[:, :, 1:128], src3[:, :, 0:127])
                else:
                    P.dma("sp", prev[:], src3[:, :, t0 - 1:t0 + 127])
            else:
                P.dma("sp", cur[:], src3[:, t0:t0 + 128])
                if t0 == 0:
                    P.memset(prev[:, 0:1], 0.0)
                    P.dma("sp", prev[:, 1:128], src3[:, 0:127])
                else:
                    P.dma("sp", prev[:], src3[:, t0 - 1:t0 + 127])

        for c in range(NT):
            t0 = c * 128
            f = fm
            shift_load(f["Rc"], f["Rp"], rv, t0, True)
            shift_load(f["Kc"], f["Kp"], kv, t0, True)
            for (cur, prev, out, mi) in ((f["Rc"], f["Rp"], f["Rs"], 0), (f["Kc"], f["Kp"], f["Ks"], 1)):
                P.tt(prev[:], prev[:], cur[:], ALU.subtract)
                P.tt(prev[:], prev[:], bc(rwp[:, mi, :]), ALU.mult)
                P.tt(out[:], prev[:], cur[:], ALU.add)
            shift_load(lo["WLc"], lo["WLp"], pfm[R_WLO:R_WLO + 96, :], t0, False)
            shift_load(lo["ALc"], lo["ALp"], pfm[R_ALO:R_ALO + 96, :], t0, False)
            shift_load(lo["GLc"], lo["GLp"], pfm[R_GLO:R_GLO + 256, :].rearrange("(j p) t -> p j t", p=128), t0, True)
            P.tt(lo["WLp"][:], lo["WLp"][:], lo["WLc"][:], ALU.subtract)
            P.stt(lo["WLc"][:], lo["WLp"][:], lop[:96, 0:1], lo["WLc"][:], ALU.mult, ALU.add)
            P.act(lo["WLc"][:], lo["WLc"][:], AF.Tanh)
            P.tt(lo["ALp"][:], lo["ALp"][:], lo["ALc"][:], ALU.subtract)
            P.stt(lo["ALc"][:], lo["ALp"][:], lop[:96, 1:2], lo["ALc"][:], ALU.mult, ALU.add)
            P.tt(lo["GLp"][:], lo["GLp"][:], lo["GLc"][:], ALU.subtract)
            for j in range(2):
                P.stt(lo["GLc"][:, j, :], lo["GLp"][:, j, :], lop[:, 2 + j:3 + j], lo["GLc"][:, j, :], ALU.mult, ALU.add)
            P.act(lo["GLc"][:], lo["GLc"][:], AF.Sigmoid)
            for (w_s, code, bias_i, out) in ((w2s, lo["WLc"], 2, f["SW"]), (a2s, lo["ALc"], 3, f["A"])):
                for half in range(2):
                    pb = nb()
                    for j in range(4):
                        h = half * 4 + j
                        P.mm(pb[:64, j * 128:(j + 1) * 128], w_s[:, hs(h)], code[:])
                    P.tt(out[:, half * 4:half * 4 + 4, :], b3(pb)[:64], bc(rwp[:, bias_i, half * 4:half * 4 + 4]), ALU.add)
                P.act(out[:], out[:], AF.Sigmoid)
            P.scan(f["CS"][:].rearrange("k h t -> k (h t)"), rmh[:].rearrange("k h t -> k (h t)"),
                   f["SW"][:].rearrange("k h t -> k (h t)"), 0.0, ALU.mult, ALU.add)
            P.tt(f["CX"][:], f["CS"][:], f["SW"][:], ALU.subtract)
            P.act(f["EP"][:], f["CS"][:], AF.Exp, scale=CW)
            P.act(f["EN"][:], f["CS"][:], AF.Exp, scale=-CW)
            P.act(f["EX"][:], f["CX"][:], AF.Exp, scale=CW)
            P.tt(f["KK0"][:], f["Ks"][:], bc(rwp[:, 4, :]), ALU.mult)
            P.tt(f["T0"][:], f["KK0"][:], f["KK0"][:], ALU.mult)
            for half in range(2):
                pb = nb()
                P.mm(pb[:64, :], cx.ones_f[:64, :64], f["T0"][:, half * 4:half * 4 + 4, :].rearrange("k h t -> k (h t)"))
                P.ts(f["T1"][:, half * 4:half * 4 + 4, :], b3(pb)[:64], 1e-16, None, op0=ALU.max)
            P.act(f["T1"][:], f["T1"][:], AF.Ln)
            P.act(f["T1"][:], f["T1"][:], AF.Exp, scale=-0.5)
            P.tt(f["KKn"][:], f["KK0"][:], f["T1"][:], ALU.mult)
            P.tt(f["Bv"][:], f["KKn"][:], f["A"][:], ALU.mult)
            P.tt(f["T0"][:], f["A"][:], bc(rwp[:, 5, :]), ALU.mult)
            P.tt(f["T0"][:], f["T0"][:], bc(omka[:]), ALU.add)
            P.tt(f["K2"][:], f["Ks"][:], f["T0"][:], ALU.mult)
            P.tt(f["At"][:], f["KKn"][:], f["EX"][:], ALU.mult)
            P.tt(f["Bt"][:], f["Bv"][:], f["EN"][:], ALU.mult)
            P.tt(f["Kt"][:], f["K2"][:], f["EN"][:], ALU.mult)
            P.tt(f["Rt"][:], f["Rs"][:], f["EP"][:], ALU.mult)
            eg = f["EP"][:, :, 127:128].to_broadcast([64, 8, 128])
            P.tt(f["Bh"][:], f["Bt"][:], eg, ALU.mult)
            P.tt(f["Kh"][:], f["Kt"][:], eg, ALU.mult)
            P.tt(f["T0"][:], f["Rs"][:], f["K2"][:], ALU.mult)
            P.tt(f["T0"][:], f["T0"][:], bc(rwp[:, 6, :]), ALU.mult)
            pb = nb()
            for h in range(H):
                P.mm(pb[:, h:h + 1], f["T0"][:, h, :], cx.ones_f[:64, 0:1])
            P.copy(RKS[:], pb[:, 0:8], eng="act")
            t = tm
            P.dma("sp", t["Vc"][:], ptm[t0:t0 + 128, C_RWV:C_RWV + 512])
            if t0 == 0:
                P.memset(t["Vp"][0:1, :], 0.0)
                P.dma("sp", t["Vp"][1:128, :], ptm[0:127, C_RWV:C_RWV + 512])
            else:
                P.dma("sp", t["Vp"][:], ptm[t0 - 1:t0 + 127, C_RWV:C_RWV + 512])
            P.tt(t["Vp"][:], t["Vp"][:], t["Vc"][:], ALU.subtract)
            P.tt(t["Vp"][:], t["Vp"][:], tmb[:, 0, :], ALU.mult)
            if layer == 0:
                P.tt(t["V"][:], t["Vp"][:], t["Vc"][:], ALU.add)
                P.dma("sp", vfirst[t0:t0 + 128, :], t["V"][:])
            else:
                P.tt(t["Vc"][:], t["Vp"][:], t["Vc"][:], ALU.add)
                shift_load(lo["VLc"], lo["VLp"], pfm[R_VLO:R_VLO + 64, :], t0, False)
                P.tt(lo["VLp"][:], lo["VLp"][:], lo["VLc"][:], ALU.subtract)
                P.stt(lo["VLc"][:], lo["VLp"][:], lop[:64, 4:5], lo["VLc"][:], ALU.mult, ALU.add)
                pb = nb()
                P.mm(pb[:, :], lo["VLc"][:], v2s[:])
                P.tt(t["SV"][:], pb[:, :], tmb[:, 1, :], ALU.add)
                P.act(t["SV"][:], t["SV"][:], AF.Sigmoid)
                P.dma("sp", t["VF"][:], vfirst[t0:t0 + 128, :])
                P.tt(t["VF"][:], t["VF"][:], t["Vc"][:], ALU.subtract)
                P.tt(t["VF"][:], t["VF"][:], t["SV"][:], ALU.mult)
                P.tt(t["V"][:], t["VF"][:], t["Vc"][:], ALU.add)
            pb = nb()
            for j in range(2):
                P.mm(pb[:, :], lo["GLc"][:, j, :], g2s[:, j, :], start=(j == 0), stop=(j == 1))
            P.copy(t["Gt"][:], pb[:, :], eng="act")
            M, N, PT = big["M0"], big["N0"], big["PT"]
            M2, N2 = big["M1"], big["N1"]
            for half in range(2):
                pa, pb = nb(), nb()
                for j in range(4):
                    h = half * 4 + j
                    P.mm(pa[:, j * 128:(j + 1) * 128], f["At"][:, h, :], f["Bt"][:, h, :])
                    P.mm(pb[:, j * 128:(j + 1) * 128], f["Bt"][:, h, :], f["At"][:, h, :])
                hh = slice(half * 4, half * 4 + 4)
                P.stt(M[:, hh, :], b3(pa), -1.0, m_gt4[:], ALU.mult, ALU.mult)
                P.stt(N[:, hh, :], b3(pb), -1.0, m_lt4[:], ALU.mult, ALU.mult)
                P.tt(PT[:, hh, :], N[:, hh, :], id4[:], ALU.add, eng="pool")
            for step in range(6):
                last = step == 5
                for half in range(2):
                    hh = slice(half * 4, half * 4 + 4)
                    pa = nb()
                    for j in range(4):
                        h = half * 4 + j
                        P.mm(pa[:, j * 128:(j + 1) * 128], N[:, h, :], M[:, h, :])
                    P.copy(M2[:, hh, :], b3(pa), eng="act")
                    if not last:
                        pb = nb()
                        for j in range(4):
                            h = half * 4 + j
                            P.mm(pb[:, j * 128:(j + 1) * 128], M[:, h, :], N[:, h, :])
                        P.copy(N2[:, hh, :], b3(pb), eng="dve")
                    pc = nb()
                    for j in range(4):
                        h = half * 4 + j
                        P.mm(pc[:, j * 128:(j + 1) * 128], M2[:, h, :], PT[:, h, :])
                    P.tt(PT[:, hh, :], b3(pc), PT[:, hh, :], ALU.add)
                M, M2 = M2, M
                N, N2 = N2, N
            for (dst, lt, rt, msk) in ((big["LAK"], f["Kt"], f["At"], m_lt4), (big["MRB"], f["Bt"], f["Rt"], m_le4),
                                       (big["MRK"], f["Kt"], f["Rt"], m_le4)):
                for half in range(2):
                    pa = nb()
                    for j in range(4):
                        h = half * 4 + j
                        P.mm(pa[:, j * 128:(j + 1) * 128], lt[:, h, :], rt[:, h, :])
                    P.tt(dst[:, half * 4:half * 4 + 4, :], b3(pa), msk[:], ALU.mult)
            for (src, dst) in ((f["Bh"], t["BHt"]), (f["Kh"], t["KHt"])):
                pa = nb()
                for h in range(H):
                    P.transpose(pa[:, hs(h)], src[:, h, :], cx.ident_f[:64, :64])
                P.copy(dst[:], pa[:, :], eng="act")
            pa = nb()
            for h in range(H):
                P.mm(pa[:, hs(h)], big["LAK"][:, h, :], t["V"][:, hs(h)], start=True, stop=False)
                P.mm(pa[:, hs(h)], f["At"][:, h, :], ST[:, h, :], start=False, stop=True)
            P.ts(t["NXZ"][:], pa[:, :], -1.0, None, op0=ALU.mult)
            pa = nb()
            for h in range(H):
                P.mm(pa[:, hs(h)], PT[:, h, :], t["NXZ"][:, hs(h)])
            P.copy(t["SA"][:], pa[:, :], eng="act")
            pa = nb()
            for h in range(H):
                P.mm(pa[:, hs(h)], f["Rt"][:, h, :], ST[:, h, :], start=True, stop=False)
                P.mm(pa[:, hs(h)], big["MRB"][:, h, :], t["SA"][:, hs(h)], start=False, stop=False)
                P.mm(pa[:, hs(h)], big["MRK"][:, h, :], t["V"][:, hs(h)], start=False, stop=True)
            P.copy(t["Y"][:], pa[:, :], eng="act")
            pa = nb()
            for h in range(H):
                P.mm(pa[:64, hs(h)], t["BHt"][:, hs(h)], t["SA"][:, hs(h)], start=True, stop=False)
                P.mm(pa[:64, hs(h)], t["KHt"][:, hs(h)], t["V"][:, hs(h)], start=False, stop=True)
            P.tt(ST[:], ST[:], f["EP"][:, :, 127:128].to_broadcast([64, 8, 64]), ALU.mult)
            P.tt(ST[:], ST[:], pa[:64, :].rearrange("k (h v) -> k h v", h=8), ALU.add)
            Y3 = t["Y"][:].rearrange("p (h v) -> p h v", h=8)
            C3 = t["CEN"][:].rearrange("p (h v) -> p h v", h=8)
            S3 = t["SQ"][:].rearrange("p (h v) -> p h v", h=8)
            N3 = t["YN"][:].rearrange("p (h v) -> p h v", h=8)
            V3 = t["V"][:].rearrange("p (h v) -> p h v", h=8)
            B3 = t["BON"][:].rearrange("p (h v) -> p h v", h=8)
            P.add("dve", lambda e: e.tensor_reduce(MS[:], Y3, AX.X, ALU.add), [t["Y"][:]], [MS[:]])
            P.ts(MS[:], MS[:], 1.0 / 64, None, op0=ALU.mult)
            P.tt(C3, Y3, MS[:].unsqueeze(2).to_broadcast([128, 8, 64]), ALU.subtract)
            P.tt(S3, C3, C3, ALU.mult, eng="pool")
            P.add("dve", lambda e: e.tensor_reduce(VS[:], S3, AX.X, ALU.add), [t["SQ"][:]], [VS[:]])
            P.act(VS[:], VS[:], AF.Ln, bias=RW_LN_EPS, scale=1.0 / 64)
            P.act(VS[:], VS[:], AF.Exp, scale=-0.5)
            P.tt(N3, C3, VS[:].unsqueeze(2).to_broadcast([128, 8, 64]), ALU.mult)
            P.tt(t["YN"][:], t["YN"][:], tmb[:, 2, :], ALU.mult)
            P.tt(t["YN"][:], t["YN"][:], tmb[:, 3, :], ALU.add)
            P.tt(B3, V3, RKS[:].unsqueeze(2).to_broadcast([128, 8, 64]), ALU.mult, eng="pool")
            P.tt(t["YN"][:], t["YN"][:], t["BON"][:], ALU.add)
            P.tt(t["YN"][:], t["YN"][:], t["Gt"][:], ALU.mult)
            pa = nb()
            for j in range(4):
                P.transpose(pa[:, j * 128:(j + 1) * 128], t["YN"][:, j * 128:(j + 1) * 128], cx.ident_f[:])
            ob = OB[c % 2]
            P.copy(ob[:], b3(pa), eng="act")
            P.dma("sp", mixv[:, :, t0:t0 + 128], ob[:])
        P.flush()


from concourse.bass_utils import run_bass_kernel_spmd

DEPTH = 4
N_META = 16
RW_SHIFT_N = 1984


def build_program(L, depth=DEPTH, Lr=None):
    nc = bass.Bass("TRN2", target_bir_lowering=False)
    dt = lambda n, s, k="ExternalInput", d=F32: nc.dram_tensor(n, s, d, kind=k).ap()
    h0 = dt("h0", [D, L])
    gains = dt("gains", [depth, 3, 128, DC])
    f1wi = dt("ffn1_wi", [depth, FC, 128, DC, 256])
    f1wo = dt("ffn1_wo", [depth, DC, 128, FC, 128])
    f2wi = dt("ffn2_wi", [depth, FC, 128, DC, 256])
    f2wo = dt("ffn2_wo", [depth, DC, 128, FC, 128])
    w_in = dt("w_in", [depth, 14, 128, DC, 512])
    w_in_v = dt("w_in_v", [depth - 1, 128, DC, 64]) if depth > 1 else None
    w_out = dt("w_out", [depth, 4, 128, DC, 512])
    hglb = dt("hglb", [128, 4, 4])
    hgn = dt("hgn", [depth, 128, 1])
    sbg = dt("sbg", [depth, 128, 3])
    rwp = dt("rwp", [depth, 64, 7, 8])
    lop = dt("lop", [depth, 128, 8])
    w2 = dt("rw_w2", [depth, 96, 512])
    a2 = dt("rw_a2", [depth, 96, 512])
    g2 = dt("rw_g2", [depth, 128, 2, 512])
    v2 = dt("rw_v2", [depth, 64, 512])
    tmb = dt("tmb", [depth, 128, 5, 512])
    hT = dt("hT", [D, L], "ExternalOutput")
    pfm = dt("pfm", [NFM, L], "Internal")
    ptm = dt("ptm", [L, NTM], "Internal")
    vfirst = dt("vfirst", [L, 512], "Internal")
    mixT = dt("mixT", [D, L], "Internal", BF16)
    with ExitStack() as es:
        cx = make_ctx(nc, es)
        cx.eps_ap = lambda e: float(e)
        init_consts(cx)
        make_masks(cx, es)
        make_lb(cx, es, hglb)
        P = Phase(cx.st)
        P.dma("sp", hT, h0)
        P.flush()
        for l in range(depth):
            ffn_phase(cx, hT, f1wi[l], f1wo[l], gains[l, 0], L, Lr=Lr)
            proj_phase(cx, hT, w_in[l], (w_in_v[l - 1] if l > 0 else None), gains[l, 1], pfm, ptm, L)
            hg_phase(cx, pfm, ptm, l, hgn[l], mixT, L)
            sb_phase(cx, pfm, ptm, sbg[l], mixT, L)
            prm = {"rwp": rwp[l], "lop": lop[l], "w2": w2[l], "a2": a2[l], "g2": g2[l], "v2": v2[l], "tmb": tmb[l]}
            rw_phase(cx, pfm, ptm, l, prm, vfirst, mixT, L)
            ffn_phase(cx, hT, f2wi[l], f2wo[l], gains[l, 2], L, pre=(mixT, w_out[l]), Lr=Lr)
    return nc, cx.st.nops


def _c(a):
    return np.ascontiguousarray(a, dtype=np.float32)


def layout_params(inp, depth=DEPTH):
    fm16 = lambda v: v.reshape(DC, 128).T
    fmh = lambda v: v.reshape(8, 64).T
    out = {}
    out["gains"] = _c(np.stack([np.stack([fm16(inp[k][l]) for k in ("norm_ffn1", "norm_mix", "norm_ffn2")]) for l in range(depth)]))
    for k in ("ffn1_wi", "ffn2_wi"):
        out[k] = _c(inp[k][:depth].reshape(depth, DC, 128, FC, 256).transpose(0, 3, 2, 1, 4))
    for k in ("ffn1_wo", "ffn2_wo"):
        out[k] = _c(inp[k][:depth].reshape(depth, FC, 128, DC, 128).transpose(0, 3, 2, 1, 4))
    wpad = np.zeros((depth, D, 7168), np.float32)
    wpad[:, :, :7104] = inp["w_in"][:depth]
    out["w_in"] = _c(wpad.reshape(depth, DC, 128, 14, 512).transpose(0, 3, 2, 1, 4))
    out["w_out"] = _c(inp["w_out"][:depth].reshape(depth, DC, 128, 4, 512).transpose(0, 3, 2, 1, 4))
    if depth > 1:
        out["w_in_v"] = _c(inp["w_in_v"][:depth - 1].reshape(depth - 1, DC, 128, 64).transpose(0, 2, 1, 3))
    out["hglb"] = _c(inp["hg_lb"].reshape(4, 4, 128).transpose(2, 1, 0))
    out["hgn"] = _c(inp["hg_norm"][:depth].reshape(depth, 128, 1))
    out["sbg"] = _c(np.stack([np.stack([inp["sb_qn"][l], inp["sb_kn"][l], inp["sb_on"][l]], axis=1) for l in range(depth)]))
    rwp, lop, v2, tmb = [], [], [], []
    for l in range(depth):
        mu = inp["rw_mu"][l]
        rwp.append(np.stack([fmh(mu[0:512]), fmh(mu[512:1024]), fmh(inp["rw_w0"][l]), fmh(inp["rw_a0"][l]), fmh(inp["rw_kk"][l]),
                             fmh(inp["rw_ka"][l]), fmh(inp["rw_rk"][l].reshape(-1))], axis=1))
        lp = np.zeros((128, 8), np.float32)
        lp[:96, 0] = mu[1536:1632]
        lp[:96, 1] = mu[1632:1728]
        lp[:, 2] = mu[1728:1856]
        lp[:, 3] = mu[1856:1984]
        tb = np.zeros((128, 5, 512), np.float32)
        tb[:, 0] = mu[1024:1536][None]
        tb[:, 2] = inp["rw_ln_w"][l][None]
        tb[:, 3] = inp["rw_ln_b"][l][None]
        vv = np.zeros((64, 512), np.float32)
        if l > 0:
            lp[:64, 4] = inp["rw_mu_v"][l - 1]
            tb[:, 1] = inp["rw_v0"][l - 1][None]
            vv = inp["rw_v2"][l - 1]
        lop.append(lp)
        tmb.append(tb)
        v2.append(vv)
    out["rwp"] = _c(np.stack(rwp))
    out["lop"] = _c(np.stack(lop))
    out["tmb"] = _c(np.stack(tmb))
    out["rw_v2"] = _c(np.stack(v2))
    out["rw_w2"] = _c(inp["rw_w2"][:depth])
    out["rw_a2"] = _c(inp["rw_a2"][:depth])
    out["rw_g2"] = _c(np.stack([inp["rw_g2"][l].reshape(2, 128, 512).transpose(1, 0, 2) for l in range(depth)]))
    return out


def kernel(**inputs):
    inp = {k: np.asarray(v) for k, v in inputs.items()}
    x = inp["x"]
    B, S, _ = x.shape
    depth = inp["norm_ffn1"].shape[0]
    L_real = N_META + S
    L = ((L_real + 127) // 128) * 128
    nc, nops = build_program(L, depth, L_real)
    shared = layout_params(inp, depth)
    n_cores = 8 if B == 4 else B
    in_maps = []
    for c in range(n_cores):
        b = c % B
        h0 = np.zeros((L, D), np.float32)
        h0[:N_META] = inp["meta"]
        h0[N_META:L_real] = x[b]
        m = dict(shared)
        m["h0"] = _c(h0.T)
        in_maps.append(m)
    res = run_bass_kernel_spmd(nc, in_maps, core_ids=list(range(n_cores)))
    out = np.stack([np.ascontiguousarray(res.results[b]["hT"].T[N_META:L_real]) for b in range(B)])
    return out.astype(np.float32)
```

```python
from contextlib import ExitStack
import math
import numpy as np
import concourse.bass as bass
import concourse.mybir as mybir

F32 = mybir.dt.float32
BF16 = mybir.dt.bfloat16
AF = mybir.ActivationFunctionType
ALU = mybir.AluOpType
AX = mybir.AxisListType

ENGS = ("pe", "act", "dve", "pool", "sp")
NDMA = 12


def region(ap):
    t = ap.tensor
    shp = list(t.shape)
    rowlen = 1
    for s in shp[1:]:
        rowlen *= s
    off = int(ap.offset)
    r0 = off // rowlen
    c0 = off % rowlen
    rext = 0
    cext = 0
    for step, cnt in ap.ap:
        step = abs(int(step))
        cnt = int(cnt)
        if cnt <= 1 or step == 0:
            continue
        if step >= rowlen and step % rowlen == 0:
            rext += (cnt - 1) * (step // rowlen)
        else:
            cext += (cnt - 1) * step
    c1 = c0 + cext + 1
    if c1 > rowlen:
        extra = (c1 - 1) // rowlen
        rext += extra
        c0, c1 = 0, rowlen
    return (ap.name, r0, r0 + rext + 1, c0, c1)


class State:
    def __init__(self, nc, es):
        self.nc = nc
        self.sem = {}
        self.cnt = {}
        for e in ENGS:
            self.sem[e] = es.enter_context(nc.semaphore("s_" + e))
            self.cnt[e] = 0
        self.dsem = {}
        self.dcnt = {}
        self.dnext = {}
        for q in ("sp", "pool", "act"):
            self.dsem[q] = [es.enter_context(nc.semaphore("d_%s%d" % (q, i))) for i in range(NDMA)]
            self.dcnt[q] = [0] * NDMA
            self.dnext[q] = 0
        self.waited = {e: {} for e in ENGS}
        self.nops = 0


class Phase:
    def __init__(self, st):
        self.st = st
        self.nc = st.nc
        self.ops = {e: [] for e in ENGS}
        self.recs = {}
        self.order = 0

    def _deps(self, reads, writes):
        deps = {}

        def add(done):
            k = done[0]
            if k not in deps or deps[k][2] < done[2]:
                deps[k] = done

        for ap in reads:
            nm, r0, r1, c0, c1 = region(ap)
            for rec in self.recs.get(nm, ()):
                if rec[5] and rec[0] < r1 and r0 < rec[1] and rec[2] < c1 and c0 < rec[3]:
                    add(rec[4])
        for ap in writes:
            nm, r0, r1, c0, c1 = region(ap)
            for rec in self.recs.get(nm, ()):
                if rec[0] < r1 and r0 < rec[1] and rec[2] < c1 and c0 < rec[3]:
                    add(rec[4])
        return deps

    def _record(self, reads, writes, done):
        for ap in writes:
            nm, r0, r1, c0, c1 = region(ap)
            lst = self.recs.setdefault(nm, [])
            lst[:] = [rc for rc in lst if not (r0 <= rc[0] and rc[1] <= r1 and c0 <= rc[2] and rc[3] <= c1)]
            lst.append([r0, r1, c0, c1, done, True])
        for ap in reads:
            nm, r0, r1, c0, c1 = region(ap)
            lst = self.recs.setdefault(nm, [])
            lst[:] = [rc for rc in lst if not ((not rc[5]) and rc[4][0] == done[0] and r0 <= rc[0] and rc[1] <= r1 and c0 <= rc[2] and rc[3] <= c1)]
            lst.append([r0, r1, c0, c1, done, False])

    def add(self, eng, fn, reads, writes, pe_skip=True):
        st = self.st
        deps = self._deps(reads, writes)
        st.cnt[eng] += 1
        done = ("e_" + eng, st.sem[eng], st.cnt[eng])
        waits = []
        for k, d in deps.items():
            if eng == "pe" and k == "e_pe":
                continue
            if st.waited[eng].get(k, 0) >= d[2]:
                continue
            st.waited[eng][k] = d[2]
            waits.append((d[1], d[2]))
        self.ops[eng].append((fn, waits, (st.sem[eng], 1)))
        self._record(reads, writes, done)
        st.nops += 1

    def dma(self, q, out, in_, **kw):
        st = self.st
        reads, writes = [in_], [out]
        deps = self._deps(reads, writes)
        i = st.dnext[q]
        st.dnext[q] = (i + 1) % NDMA
        sem = st.dsem[q][i]
        key = "d_%s%d" % (q, i)
        waits = []
        if st.dcnt[q][i] > 0 and st.waited[q].get(key, 0) < st.dcnt[q][i]:
            waits.append((sem, st.dcnt[q][i]))
            st.waited[q][key] = st.dcnt[q][i]
        st.dcnt[q][i] += 16
        done = (key, sem, st.dcnt[q][i])
        for k, d in deps.items():
            if st.waited[q].get(k, 0) >= d[2]:
                continue
            st.waited[q][k] = d[2]
            waits.append((d[1], d[2]))
        self.ops[q].append((lambda e: e.dma_start(out=out, in_=in_, **kw), waits, (sem, 16)))
        self._record(reads, writes, done)
        st.nops += 1

    def mm(self, out, lhsT, rhs, start=True, stop=True, sgc=False):
        if sgc:
            self.add("pe", lambda e: e.matmul(out, lhsT, rhs, start=start, stop=stop, skip_group_check=True), [lhsT, rhs], [out])
        else:
            self.add("pe", lambda e: e.matmul(out, lhsT, rhs, start=start, stop=stop), [lhsT, rhs], [out])

    def transpose(self, out, in_, ident):
        self.add("pe", lambda e: e.transpose(out, in_, ident), [in_, ident], [out])

    def act(self, out, in_, func, bias=None, scale=1.0, eng="act"):
        rd = [in_]
        kw = {}
        if bias is not None:
            kw["bias"] = bias
            if not isinstance(bias, (int, float)):
                rd.append(bias)
        if not isinstance(scale, (int, float)):
            rd.append(scale)
        self.add("act", lambda e: e.activation(out, in_, func, scale=scale, **kw), rd, [out])

    def tt(self, out, in0, in1, op, eng="dve"):
        self.add(eng, lambda e: e.tensor_tensor(out, in0, in1, op), [in0, in1], [out])

    def ts(self, out, in0, s1, s2=None, op0=ALU.mult, op1=ALU.bypass, eng="dve"):
        rd = [in0]
        for s in (s1, s2):
            if s is not None and not isinstance(s, (int, float)):
                rd.append(s)
        if s2 is None:
            self.add(eng, lambda e: e.tensor_scalar(out, in0, s1, None, op0), rd, [out])
        else:
            self.add(eng, lambda e: e.tensor_scalar(out, in0, s1, s2, op0, op1), rd, [out])

    def stt(self, out, in0, scalar, in1, op0, op1):
        rd = [in0, in1]
        if not isinstance(scalar, (int, float)):
            rd.append(scalar)
        self.add("dve", lambda e: e.scalar_tensor_tensor(out, in0, scalar, in1, op0, op1), rd, [out])

    def copy(self, out, in_, eng="dve"):
        if eng == "act":
            self.add("act", lambda e: e.copy(out, in_), [in_], [out])
        else:
            self.add(eng, lambda e: e.tensor_copy(out, in_), [in_], [out])

    def memset(self, out, val, eng="dve"):
        self.add(eng, lambda e: e.memset(out, val), [], [out])

    def recip(self, out, in_):
        self.add("dve", lambda e: e.reciprocal(out, in_), [in_], [out])

    def scan(self, out, d0, d1, init, op0, op1):
        rd = [d0, d1]
        if not isinstance(init, (int, float)):
            rd.append(init)
        self.add("dve", lambda e: e.tensor_tensor_scan(out, d0, d1, init, op0, op1), rd, [out])

    def flush(self, final=False):
        st = self.st
        nc = self.nc
        finals = []
        for e in ENGS:
            if st.cnt[e] > 0:
                finals.append(("e_" + e, st.sem[e], st.cnt[e]))
        for q in st.dsem:
            for i in range(NDMA):
                if st.dcnt[q][i] > 0:
                    finals.append(("d_%s%d" % (q, i), st.dsem[q][i], st.dcnt[q][i]))
        ops = self.ops
        emap = {"pe": "tensor", "act": "scalar", "dve": "vector", "pool": "gpsimd", "sp": "sync"}

        def mk(ename):
            def body(eng):
                for fn, waits, inc in ops[ename]:
                    for s, v in waits:
                        eng.wait_ge(s, v)
                    ins = fn(eng)
                    ins.then_inc(inc[0], inc[1])
                for k, s, v in finals:
                    if st.waited[ename].get(k, 0) >= v:
                        continue
                    st.waited[ename][k] = v
                    eng.wait_ge(s, v)
            return body

        with nc.Block() as block:
            for ename in ENGS:
                getattr(block, emap[ename])(mk(ename))
        self.ops = {e: [] for e in ENGS}
        self.recs = {}


def _coll(self, in_ap, out_ap, groups):
    st = self.st
    q = "pool"
    reads, writes = [in_ap], [out_ap]
    deps = self._deps(reads, writes)
    i = st.dnext[q]
    st.dnext[q] = (i + 1) % NDMA
    sem = st.dsem[q][i]
    key = "d_%s%d" % (q, i)
    waits = []
    if st.dcnt[q][i] > 0 and st.waited[q].get(key, 0) < st.dcnt[q][i]:
        waits.append((sem, st.dcnt[q][i]))
        st.waited[q][key] = st.dcnt[q][i]
    st.dcnt[q][i] += 16
    done = (key, sem, st.dcnt[q][i])
    for k, d in deps.items():
        if st.waited[q].get(k, 0) >= d[2]:
            continue
        st.waited[q][k] = d[2]
        waits.append((d[1], d[2]))
    self.ops[q].append((lambda e: e.collective_compute("AllGather", ALU.bypass, replica_groups=groups, ins=[in_ap], outs=[out_ap]), waits, (sem, 16)))
    self._record(reads, writes, done)
    st.nops += 1


Phase.allgather = _coll


D = 2048
DC = 16
DFF = 5632
FC = 44
EPS = 1e-6


def tiles_of(L, TT=512):
    out = []
    t = 0
    while t < L:
        n = min(TT, L - t)
        out.append((t, n))
        t += n
    return out


class Ctx:
    pass


def make_ctx(nc, es):
    cx = Ctx()
    cx.nc = nc
    cx.uid = [0]
    cx.st = State(nc, es)
    cx.ps = [es.enter_context(nc.psum_tensor("ps%d" % i, [128, 512], F32)) for i in range(8)]
    cx.ones_bf = es.enter_context(nc.sbuf_tensor("ones_bf", [128, 128], BF16))
    cx.ones_f = es.enter_context(nc.sbuf_tensor("ones_f", [128, 128], F32))
    cx.ident_f = es.enter_context(nc.sbuf_tensor("ident_f", [128, 128], F32))
    return cx


def init_consts(cx):
    P = Phase(cx.st)
    P.memset(cx.ones_bf[:], 1.0)
    P.memset(cx.ones_f[:], 1.0)
    nc = cx.nc
    P.add("pool", lambda e: e.affine_select(cx.ident_f[:], cx.ones_f[:], pattern=[[1, 128]], compare_op=ALU.is_equal,
                                             fill=0.0, base=0, channel_multiplier=-1), [cx.ones_f[:]], [cx.ident_f[:]])
    P.flush()


def rmsnorm_fm(P, cx, h, g, u, sq, rstd, ncn, TT, dn, eps, psb, extra_scale=1.0):
    for c in range(ncn):
        P.act(sq[:, c, :TT], h[:, c, :TT], AF.Square)
    for c in range(ncn):
        P.mm(psb[:, :TT], cx.ones_bf[:], sq[:, c, :TT], start=(c == 0), stop=(c == ncn - 1))
    P.act(rstd[:, :TT], psb[:, :TT], AF.Sqrt, bias=cx.eps_ap(eps), scale=1.0 / dn)
    P.recip(rstd[:, :TT], rstd[:, :TT])
    if extra_scale != 1.0:
        P.ts(rstd[:, :TT], rstd[:, :TT], float(extra_scale), None, op0=ALU.mult)
    for c in range(ncn):
        P.stt(u[:, c, :TT], h[:, c, :TT], g[:, c:c + 1], rstd[:, :TT], ALU.mult, ALU.mult)


def ffn_phase(cx, hT, wi, wo, gvec, L, pre=None, Lr=None):
    nc = cx.nc
    with ExitStack() as es:
        cx.uid[0] += 1
        tg = "_%d" % cx.uid[0]
        sb = lambda n, s, d: es.enter_context(nc.sbuf_tensor(n + tg, s, d))
        h = sb("f_h", [128, DC, 512], F32)
        u = sb("f_u", [128, DC, 512], BF16)
        hid = sb("f_hid", [128, FC, 512], BF16)
        rstd = sb("f_rstd", [128, 512], F32)
        g = sb("f_g", [128, DC], F32)
        wg = [sb("f_wg%d" % i, [128, DC, 256], BF16) for i in range(2)]
        wu = [sb("f_wu%d" % i, [128, DC, 256], BF16) for i in range(2)]
        wos = [sb("f_wo%d" % i, [128, FC, 128], BF16) for i in range(2)]
        tmp = [sb("f_tmp%d" % i, [128, 512], F32) for i in range(2)]
        P = Phase(cx.st)
        P.dma("sp", g[:], gvec)
        hv = hT.rearrange("(c p) t -> p c t", p=128)
        ps = cx.ps
        for (t0, TT) in tiles_of(L):
            P.dma("sp", h[:, :, :TT], hv[:, :, t0:t0 + TT])
            if pre is not None:
                mixT, w_out = pre
                mv = mixT.rearrange("(c p) t -> p c t", p=128)
                P.dma("sp", u[:, :, :TT], mv[:, :, t0:t0 + TT])
                for ds in range(4):
                    slab = hid[:, (ds % 2) * 16:(ds % 2) * 16 + 16, :]
                    P.dma("pool", slab, w_out[ds])
                    for dj in range(4):
                        dc = ds * 4 + dj
                        po = ps[4 + dc % 2]
                        for mc in range(DC):
                            P.mm(po[:, :TT], slab[:, mc, dj * 128:(dj + 1) * 128], u[:, mc, :TT], start=(mc == 0), stop=(mc == DC - 1))
                        P.tt(h[:, dc, :TT], po[:, :TT], h[:, dc, :TT], ALU.add)
            rmsnorm_fm(P, cx, h, g, u, hid, rstd, DC, TT, D, EPS, ps[7])
            k = 0
            for j2 in range(FC // 2):
                b = j2 % 2
                P.dma("pool", wg[b][:], wi[j2])
                P.dma("pool", wu[b][:], wi[FC // 2 + j2])
                for jj in range(2):
                    j = 2 * j2 + jj
                    pa = ps[(k % 2) * 2]
                    pb = ps[(k % 2) * 2 + 1]
                    tm = tmp[k % 2]
                    k += 1
                    for c in range(DC):
                        P.mm(pa[:, :TT], wg[b][:, c, jj * 128:(jj + 1) * 128], u[:, c, :TT], start=(c == 0), stop=(c == DC - 1))
                    for c in range(DC):
                        P.mm(pb[:, :TT], wu[b][:, c, jj * 128:(jj + 1) * 128], u[:, c, :TT], start=(c == 0), stop=(c == DC - 1))
                    P.act(tm[:, :TT], pa[:, :TT], AF.Silu)
                    P.tt(hid[:, j, :TT], tm[:, :TT], pb[:, :TT], ALU.mult)
            for dc in range(DC):
                b = dc % 2
                P.dma("pool", wos[b][:], wo[dc])
                po = ps[4 + b]
                for j in range(FC):
                    P.mm(po[:, :TT], wos[b][:, j, :], hid[:, j, :TT], start=(j == 0), stop=(j == FC - 1))
                P.stt(h[:, dc, :TT], po[:, :TT], 0.5, h[:, dc, :TT], ALU.mult, ALU.add)
            if Lr is not None and t0 + TT > Lr:
                P.memset(h[:, :, max(0, Lr - t0):TT], 0.0)
            P.dma("sp", hv[:, :, t0:t0 + TT], h[:, :, :TT])
        P.flush()


R_HGQ, R_HGF, R_HGG = 0, 512, 1024
R_SBQ, R_SBK = 1536, 2560
R_RWR, R_RWK = 3584, 4096
R_WLO, R_ALO, R_GLO, R_VLO = 4608, 4736, 4864, 5120
NFM = 5248
C_HGI, C_SBV, C_RWV = 0, 512, 1536
NTM = 2048

SLABS = [
    (0, 512, "fm", R_HGQ), (512, 512, "fm", R_HGF), (1536, 512, "fm", R_HGG),
    (2048, 512, "fm", R_SBQ), (2560, 512, "fm", R_SBQ + 512),
    (3072, 512, "fm", R_SBK), (3584, 512, "fm", R_SBK + 512),
    (5120, 512, "fm", R_RWR), (5632, 512, "fm", R_RWK),
    (6656, 448, "lo", 0),
    (1024, 512, "tm", C_HGI), (4096, 512, "tm", C_SBV), (4608, 512, "tm", C_SBV + 512), (6144, 512, "tm", C_RWV),
]


def proj_phase(cx, hT, w_in, w_in_v, gvec, pfm, ptm, L):
    nc = cx.nc
    with ExitStack() as es:
        cx.uid[0] += 1
        tg = "_%d" % cx.uid[0]
        sb = lambda n, s, d: es.enter_context(nc.sbuf_tensor(n + tg, s, d))
        h = sb("p_h", [128, DC, 512], F32)
        u = sb("p_u", [128, DC, 512], BF16)
        sq = sb("p_sq", [128, DC, 512], BF16)
        rstd = sb("p_rstd", [128, 512], F32)
        g = sb("p_g", [128, DC], F32)
        ws = [sb("p_w%d" % i, [128, DC, 512], BF16) for i in range(2)]
        wv = sb("p_wv", [128, DC, 64], BF16)
        stg = [sb("p_stg%d" % i, [128, 512], F32) for i in range(4)]
        P = Phase(cx.st)
        P.dma("sp", g[:], gvec)
        hv = hT.rearrange("(c p) t -> p c t", p=128)
        if w_in_v is not None:
            P.dma("pool", wv[:], w_in_v)
        ps = cx.ps
        k = 0
        nslab = 0
        for (t0, TT) in tiles_of(L):
            P.dma("sp", h[:, :, :TT], hv[:, :, t0:t0 + TT])
            rmsnorm_fm(P, cx, h, g, u, sq, rstd, DC, TT, D, EPS, ps[7])

            def fm_chunk(wt, cs, M, drow):
                nonlocal k
                pb = ps[k % 4]
                sg = stg[k % 4]
                for c in range(DC):
                    P.mm(pb[:M, :TT], wt[:, c, cs:cs + M], u[:, c, :TT], start=(c == 0), stop=(c == DC - 1))
                if k % 2 == 0:
                    P.copy(sg[:M, :TT], pb[:M, :TT], eng="act")
                else:
                    P.copy(sg[:M, :TT], pb[:M, :TT], eng="dve")
                P.dma("sp", pfm[drow:drow + M, t0:t0 + TT], sg[:M, :TT])
                k += 1

            for (c0, ncol, kind, dst) in SLABS:
                wt = ws[nslab % 2]
                nslab += 1
                P.dma("pool", wt[:], w_in[c0 // 512])
                if kind == "fm":
                    for j in range(ncol // 128):
                        fm_chunk(wt, j * 128, 128, dst + j * 128)
                elif kind == "lo":
                    fm_chunk(wt, 0, 96, R_WLO)
                    fm_chunk(wt, 96, 96, R_ALO)
                    fm_chunk(wt, 192, 128, R_GLO)
                    fm_chunk(wt, 320, 128, R_GLO + 128)
                else:
                    for tb in range(TT // 128):
                        pb = ps[k % 4]
                        sg = stg[k % 4]
                        for c in range(DC):
                            P.mm(pb[:, :ncol], u[:, c, tb * 128:(tb + 1) * 128], wt[:, c, :ncol], start=(c == 0), stop=(c == DC - 1))
                        if k % 2 == 0:
                            P.copy(sg[:, :ncol], pb[:, :ncol], eng="act")
                        else:
                            P.copy(sg[:, :ncol], pb[:, :ncol], eng="dve")
                        P.dma("sp", ptm[t0 + tb * 128:t0 + (tb + 1) * 128, dst:dst + ncol], sg[:, :ncol])
                        k += 1
            if w_in_v is not None:
                fm_chunk(wv, 0, 64, R_VLO)
        P.flush()


SB_HEADS = 8
R_MIX_HG, R_MIX_SB, R_MIX_RW = 0, 512, 1536


def rmsnorm1(P, cx, x, gcol, out, sqs, rstd, TT, dn, eps, psb, extra=1.0):
    P.act(sqs[:, :TT], x, AF.Square)
    P.mm(psb[:, :TT], cx.ones_bf[:], sqs[:, :TT], start=True, stop=True)
    P.act(rstd[:, :TT], psb[:, :TT], AF.Ln, bias=float(eps), scale=1.0 / dn)
    P.act(rstd[:, :TT], rstd[:, :TT], AF.Exp, bias=float(math.log(extra)), scale=-0.5)
    P.stt(out, x, gcol, rstd[:, :TT], ALU.mult, ALU.mult)


def make_masks(cx, es):
    nc = cx.nc
    sb = lambda n, s, d: es.enter_context(nc.sbuf_tensor(n, s, d))
    cx.ones_w = sb("ones_w", [128, 896], BF16)
    cx.tri_incl = sb("tri_incl", [128, 128], BF16)
    cx.tri_ls = sb("tri_ls", [128, 128], BF16)
    cx.mw = sb("mw", [128, 896], BF16)
    cx.m_le = sb("m_le", [128, 128], BF16)
    cx.m_lt_f = sb("m_lt_f", [128, 128], F32)
    cx.m_le_f = sb("m_le_f", [128, 128], F32)
    cx.m_gt_f = sb("m_gt_f", [128, 128], F32)
    P = Phase(cx.st)
    P.memset(cx.ones_w[:], 1.0)

    def sel(out, in_, pat, cm, base, op):
        P.add("pool", lambda e: e.affine_select(out, in_, pattern=pat, compare_op=op, fill=0.0, base=base,
                                                 channel_multiplier=cm), [in_], [out])
    sel(cx.tri_incl[:], cx.ones_w[:, 0:128], [[-1, 128]], 1, 0, ALU.is_ge)
    sel(cx.tri_ls[:], cx.ones_w[:, 0:128], [[1, 128]], -1, 0, ALU.is_gt)
    sel(cx.mw[:], cx.ones_w[:], [[1, 896]], -1, -384, ALU.is_gt)
    sel(cx.m_le[:], cx.ones_w[:, 0:128], [[1, 128]], -1, 0, ALU.is_ge)
    sel(cx.m_lt_f[:], cx.ones_f[:], [[1, 128]], -1, 0, ALU.is_gt)
    sel(cx.m_le_f[:], cx.ones_f[:], [[1, 128]], -1, 0, ALU.is_ge)
    sel(cx.m_gt_f[:], cx.ones_f[:], [[-1, 128]], 1, 0, ALU.is_gt)
    P.flush()


def sb_phase(cx, pfm, ptm, gains, mixT, L):
    nc = cx.nc
    NT = L // 128
    with ExitStack() as es:
        cx.uid[0] += 1
        tg = "_%d" % cx.uid[0]
        sb = lambda n, s, d: es.enter_context(nc.sbuf_tensor(n + tg, s, d))
        gn = sb("s_gn", [128, 3], F32)
        qn = [sb("s_qn%d" % i, [128, L], BF16) for i in range(2)]
        kn = [sb("s_kn%d" % i, [128, L], BF16) for i in range(2)]
        vh = [sb("s_v%d" % i, [128, NT, 128], BF16) for i in range(2)]
        xin = [sb("s_x%d" % i, [128, 512], F32) for i in range(2)]
        sqs = sb("s_sq", [128, 512], BF16)
        rstd = sb("s_rstd", [128, 512], F32)
        E = [sb("s_E%d" % i, [128, 512], F32) for i in range(2)]
        Lb = [sb("s_L%d" % i, [128, 512], BF16) for i in range(2)]
        T1 = [sb("s_T1%d" % i, [128, 512], F32) for i in range(2)]
        T2 = [sb("s_T2%d" % i, [128, 512], F32) for i in range(2)]
        At = [sb("s_A%d" % i, [128, 512], BF16) for i in range(2)]
        Cs = sb("s_Cs", [128, 512], F32)
        oh = sb("s_oh", [128, 512], F32)
        ob = [sb("s_ob%d" % i, [128, 512], BF16) for i in range(2)]
        ps = cx.ps
        P = Phase(cx.st)
        P.dma("sp", gn[:], gains)
        ptv = ptm.rearrange("(n p) c -> p n c", p=128)
        kstep = 0
        for hd in range(SB_HEADS):
            b = hd % 2
            for n0 in range(0, NT, 8):
                n1 = min(NT, n0 + 8)
                P.dma("pool", vh[b][:, n0:n1, :], ptv[:, n0:n1, C_SBV + hd * 128:C_SBV + (hd + 1) * 128])
            i = 0
            for (t0, TT) in tiles_of(L):
                for (row, gi, dst, extra) in ((R_SBQ, 0, qn[b], 128.0 ** -0.5), (R_SBK, 1, kn[b], 1.0)):
                    x = xin[i % 2]
                    i += 1
                    P.dma("sp", x[:, :TT], pfm[row + hd * 128:row + (hd + 1) * 128, t0:t0 + TT])
                    rmsnorm1(P, cx, x[:, :TT], gn[:, gi:gi + 1], dst[:, t0:t0 + TT], sqs, rstd, TT, 128, EPS, ps[7], extra)
            for (t0, TQ) in tiles_of(L):
                sb_max = (t0 + TQ - 1) // 128
                P.memset(Cs[:, :TQ], 0.0, eng="pool")
                po = ps[6]
                steps = list(range(sb_max, -1, -1))

                def stageA(sbk, w):
                    pa, pc = ps[w * 2], ps[w * 2 + 1]
                    off = sbk * 128 - t0
                    P.mm(pa[:, :TQ], kn[b][:, sbk * 128:(sbk + 1) * 128], qn[b][:, t0:t0 + TQ], start=True, stop=True)
                    P.act(E[w][:, :TQ], pa[:, :TQ], AF.Exp)
                    P.act(Lb[w][:, :TQ], E[w][:, :TQ], AF.Ln, bias=1.0)
                    if off >= 0:
                        P.tt(Lb[w][:, :TQ], Lb[w][:, :TQ], cx.mw[:, 384 - off:384 - off + TQ], ALU.mult, eng="pool")
                    P.mm(pa[:, :TQ], cx.tri_ls[:], Lb[w][:, :TQ], start=False, stop=True, sgc=True)
                    P.mm(pc[:, :TQ], cx.ones_bf[:], Lb[w][:, :TQ])

                def stageB(sbk, w):
                    pa, pc = ps[w * 2], ps[w * 2 + 1]
                    off = sbk * 128 - t0
                    P.tt(Cs[:, :TQ], pc[:, :TQ], Cs[:, :TQ], ALU.add)
                    P.tt(T2[w][:, :TQ], pa[:, :TQ], Cs[:, :TQ], ALU.subtract)
                    P.act(At[w][:, :TQ], T2[w][:, :TQ], AF.Exp)
                    if off >= 0:
                        P.tt(At[w][:, :TQ], At[w][:, :TQ], cx.mw[:, 384 - off:384 - off + TQ], ALU.mult, eng="pool")
                    P.mm(po[:, :TQ], vh[b][:, sbk, :], At[w][:, :TQ], start=(sbk == sb_max), stop=(sbk == 0))

                stageA(steps[0], kstep % 2)
                for i_s, sbk in enumerate(steps):
                    w = kstep % 2
                    kstep += 1
                    if i_s + 1 < len(steps):
                        stageA(steps[i_s + 1], kstep % 2)
                    stageB(sbk, w)
                P.copy(oh[:, :TQ], po[:, :TQ], eng="act")
                o2 = ob[(t0 // 512) % 2]
                rmsnorm1(P, cx, oh[:, :TQ], gn[:, 2:3], o2[:, :TQ], sqs, rstd, TQ, 128, EPS, ps[7])
                P.dma("sp", mixT[R_MIX_SB + hd * 128:R_MIX_SB + (hd + 1) * 128, t0:t0 + TQ], o2[:, :TQ])
        P.flush()


HG_HEADS = 4


def make_lb(cx, es, hg_lb_ap):
    nc = cx.nc
    sb = lambda n, s, d: es.enter_context(nc.sbuf_tensor(n, s, d))
    cx.lb = sb("lb", [128, 4, 4], F32)
    cx.oml = sb("oml", [128, 4, 4], F32)
    cx.noml = sb("noml", [128, 4, 4], F32)
    cx.rmask = sb("rmask", [128, 512], F32)
    with ExitStack() as es2:
        x = es2.enter_context(nc.sbuf_tensor("lb_x", [128, 4, 4], F32))
        e = es2.enter_context(nc.sbuf_tensor("lb_e", [128, 4, 4], F32))
        s = es2.enter_context(nc.sbuf_tensor("lb_s", [128, 4], F32))
        P = Phase(cx.st)
        P.dma("sp", x[:], hg_lb_ap)
        P.act(e[:], x[:], AF.Exp)
        P.tt(s[:], e[:, :, 0], e[:, :, 1], ALU.add)
        P.tt(s[:], s[:], e[:, :, 2], ALU.add)
        P.tt(s[:], s[:], e[:, :, 3], ALU.add)
        P.recip(s[:], s[:])
        P.memset(cx.lb[:, 0, :], 0.0)
        for l in range(1, 4):
            P.tt(e[:, :, l], e[:, :, l], s[:], ALU.mult)
            P.tt(cx.lb[:, l, :], cx.lb[:, l - 1, :], e[:, :, l], ALU.add)
        P.ts(cx.oml[:], cx.lb[:], -1.0, 1.0, op0=ALU.mult, op1=ALU.add)
        P.ts(cx.noml[:], cx.lb[:], 1.0, -1.0, op0=ALU.mult, op1=ALU.add)
        P.memset(cx.rmask[:], 1.0)
        for c in range(8):
            P.memset(cx.rmask[:, c * 64:c * 64 + 1], 0.0)
        P.flush()


def hg_phase(cx, pfm, ptm, layer, gnorm, mixT, L):
    nc = cx.nc
    NT = L // 128
    H = HG_HEADS
    with ExitStack() as es:
        cx.uid[0] += 1
        tg = "_%d" % cx.uid[0]
        sb = lambda n, s, d: es.enter_context(nc.sbuf_tensor(n + tg, s, d))
        gn = sb("g_gn", [128, 1], F32)
        V = [sb("g_v%d" % i, [64, 2 * NT, 128], BF16) for i in range(H)]
        X = [sb("g_x%d" % i, [128, 512], F32) for i in range(3)]
        SG = sb("g_sg", [128, 512], F32)
        FG = sb("g_fg", [128, 512], F32)
        QS = [sb("g_qs%d" % i, [128, 512], F32) for i in range(H)]
        KK = [sb("g_kk%d" % i, [128, 512], F32) for i in range(H)]
        G = [sb("g_G%d" % i, [128, 512], F32) for i in range(H)]
        NG = [sb("g_NG%d" % i, [128, 512], F32) for i in range(H)]
        EG = [sb("g_EG%d" % i, [128, 512], F32) for i in range(H)]
        QP = [sb("g_QP%d" % i, [128, 512], BF16) for i in range(H)]
        OH = [sb("g_OH%d" % i, [128, 512], F32) for i in range(H)]
        S = [sb("g_S%d" % i, [128, 128], F32) for i in range(H)]
        Sbf = [sb("g_Sb%d" % i, [128, 128], BF16) for i in range(H)]
        tmp = [sb("g_t%d" % i, [128, 64], F32) for i in range(6)]
        QT = [sb("g_QT%d" % i, [128, 64], BF16) for i in range(2)]
        KT = [sb("g_KT%d" % i, [128, 64], BF16) for i in range(2)]
        KH = [sb("g_KH%d" % i, [128, 64], F32) for i in range(2)]
        AM = [sb("g_AM%d" % i, [64, 64], BF16) for i in range(2)]
        AF32 = [sb("g_AF%d" % i, [64, 64], F32) for i in range(2)]
        KHt = [sb("g_KHt%d" % i, [64, 128], BF16) for i in range(2)]
        sqs = sb("g_sq", [128, 512], BF16)
        rstd = sb("g_rstd", [128, 512], F32)
        ON = sb("g_on", [128, 512], F32)
        MX = [sb("g_mx%d" % i, [128, 512], BF16) for i in range(2)]
        ps = cx.ps
        P = Phase(cx.st)
        P.dma("sp", gn[:], gnorm)
        ptv = ptm.rearrange("(n p) c -> p n c", p=64)
        for hd in range(H):
            for n0 in range(0, 2 * NT, 16):
                n1 = min(2 * NT, n0 + 16)
                P.dma("pool", V[hd][:, n0:n1, :], ptv[:, n0:n1, C_HGI + hd * 128:C_HGI + (hd + 1) * 128])
            P.memset(S[hd][:], 0.0)
            P.memset(Sbf[hd][:], 0.0)
        k = 0
        nm = 0
        for (t0, TT) in tiles_of(L):
            for hd in range(H):
                lb = cx.lb[:, layer, hd:hd + 1]
                oml = cx.oml[:, layer, hd:hd + 1]
                noml = cx.noml[:, layer, hd:hd + 1]
                P.dma("sp", X[0][:, :TT], pfm[R_HGF + hd * 128:R_HGF + (hd + 1) * 128, t0:t0 + TT])
                P.dma("sp", X[1][:, :TT], pfm[R_HGQ + hd * 128:R_HGQ + (hd + 1) * 128, t0:t0 + TT])
                P.act(SG[:, :TT], X[0][:, :TT], AF.Sigmoid)
                P.ts(FG[:, :TT], SG[:, :TT], oml, lb, op0=ALU.mult, op1=ALU.add)
                P.act(FG[:, :TT], FG[:, :TT], AF.Ln)
                P.ts(KK[hd][:, :TT], SG[:, :TT], noml, oml, op0=ALU.mult, op1=ALU.add)
                P.act(QS[hd][:, :TT], X[1][:, :TT], AF.Silu)
                P.scan(G[hd][:, :TT], cx.rmask[:, :TT], FG[:, :TT], 0.0, ALU.mult, ALU.add)
                P.ts(NG[hd][:, :TT], G[hd][:, :TT], -1.0, None, op0=ALU.mult)
                P.act(EG[hd][:, :TT], G[hd][:, :TT], AF.Exp)
                P.tt(QP[hd][:, :TT], QS[hd][:, :TT], EG[hd][:, :TT], ALU.mult)
            for c in range(TT // 64):
                blk = t0 // 64 + c
                cs = slice(c * 64, (c + 1) * 64)
                mid = c * 64 + 31
                end = c * 64 + 63
                for hd in range(H):
                    w = k % 2
                    k += 1
                    t1, t2, t3 = tmp[w * 3], tmp[w * 3 + 1], tmp[w * 3 + 2]
                    P.act(t1[:], G[hd][:, cs], AF.Exp, bias=NG[hd][:, mid:mid + 1])
                    P.stt(QT[w][:], t1[:], 1e30, QS[hd][:, cs], ALU.min, ALU.mult)
                    P.act(t2[:], G[hd][:, cs], AF.Exp, bias=G[hd][:, mid:mid + 1], scale=-1.0)
                    P.stt(KT[w][:], t2[:], 1e30, KK[hd][:, cs], ALU.min, ALU.mult)
                    P.act(t3[:], G[hd][:, cs], AF.Exp, bias=G[hd][:, end:end + 1], scale=-1.0)
                    P.tt(KH[w][:], KK[hd][:, cs], t3[:], ALU.mult, eng="pool")
                    pa, po, pt, pn = ps[w], ps[2 + w], ps[4 + w], ps[6]
                    P.mm(pa[:64, :64], KT[w][:], QT[w][:])
                    P.ts(AF32[w][:], pa[:64, :64], 1e30, -1e30, op0=ALU.min, op1=ALU.max)
                    P.tt(AM[w][:], AF32[w][:], cx.m_le[:64, :64], ALU.mult)
                    P.mm(po[:, :64], V[hd][:, blk, :], AM[w][:], start=True, stop=False)
                    P.mm(po[:, :64], Sbf[hd][:], QP[hd][:, cs], start=False, stop=True)
                    P.copy(OH[hd][:, cs], po[:, :64], eng="act")
                    P.transpose(pt[:64, :128], KH[w][:], cx.ident_f[:])
                    P.copy(KHt[w][:], pt[:64, :128], eng="dve")
                    P.mm(pn[:, :128], KHt[w][:], V[hd][:, blk, :])
                    P.stt(S[hd][:], S[hd][:], EG[hd][:, end:end + 1], pn[:, :128], ALU.mult, ALU.add)
                    P.copy(Sbf[hd][:], S[hd][:], eng="act")
            for hd in range(H):
                P.dma("sp", X[2][:, :TT], pfm[R_HGG + hd * 128:R_HGG + (hd + 1) * 128, t0:t0 + TT])
                P.act(X[2][:, :TT], X[2][:, :TT], AF.Silu)
                rmsnorm1(P, cx, OH[hd][:, :TT], gn[:, 0:1], ON[:, :TT], sqs, rstd, TT, 128, EPS, ps[7])
                mx = MX[nm % 2]
                nm += 1
                P.tt(mx[:, :TT], ON[:, :TT], X[2][:, :TT], ALU.mult)
                P.dma("sp", mixT[R_MIX_HG + hd * 128:R_MIX_HG + (hd + 1) * 128, t0:t0 + TT], mx[:, :TT])
        P.flush()


RW_H = 8
CW = -0.6065306597126334
RW_LN_EPS = 64e-5


def rw_phase(cx, pfm, ptm, layer, prm, vfirst, mixT, L):
    nc = cx.nc
    NT = L // 128
    H = RW_H
    with ExitStack() as es:
        cx.uid[0] += 1
        tg = "_%d" % cx.uid[0]
        sb = lambda n, s, d=F32: es.enter_context(nc.sbuf_tensor(n + tg, s, d))
        rwp = sb("r_rwp", [64, 7, 8])
        omka = sb("r_omka", [64, 8])
        lop = sb("r_lop", [128, 8])
        w2s = sb("r_w2", [96, 512])
        a2s = sb("r_a2", [96, 512])
        g2s = sb("r_g2", [128, 2, 512])
        v2s = sb("r_v2", [64, 512])
        tmb = sb("r_tmb", [128, 5, 512])
        m_gt4 = sb("r_mgt4", [128, 4, 128])
        m_lt4 = sb("r_mlt4", [128, 4, 128])
        m_le4 = sb("r_mle4", [128, 4, 128])
        id4 = sb("r_id4", [128, 4, 128])
        rmh = sb("r_rmh", [64, 8, 128])
        ST = sb("r_ST", [64, 8, 64])
        fm = {}
        for n in ("Rc", "Rp", "Kc", "Kp", "Rs", "Ks", "SW", "CS", "EP", "EN", "EX", "A", "KKn", "Bv",
                  "K2", "At", "Bt", "Kt", "Rt", "Bh", "Kh"):
            fm[n] = sb("r_f" + n, [64, 8, 128])
        fm["T0"], fm["T1"], fm["KK0"], fm["CX"] = fm["Rp"], fm["Kp"], fm["Rc"], fm["Kc"]
        lo = {}
        for n in ("WLc", "WLp", "ALc", "ALp"):
            lo[n] = sb("r_l" + n, [96, 128])
        for n in ("GLc", "GLp"):
            lo[n] = sb("r_l" + n, [128, 2, 128])
        for n in ("VLc", "VLp"):
            lo[n] = sb("r_l" + n, [64, 128])
        tm = {}
        for n in ("Vc", "Vp", "V", "VF", "SV", "Gt", "NXZ", "SA", "Y", "YN", "BHt", "KHt"):
            tm[n] = sb("r_t" + n, [128, 512])
        tm["CEN"], tm["SQ"], tm["BON"] = tm["Y"], tm["NXZ"], tm["SA"]
        MS = sb("r_MS", [128, 8])
        VS = sb("r_VS", [128, 8])
        RKS = sb("r_RKS", [128, 8])
        big = {}
        for n in ("M0", "M1", "N0", "N1", "PT", "LAK", "MRB", "MRK"):
            big[n] = sb("r_b" + n, [128, 8, 128])
        OB = [sb("r_OB%d" % i, [128, 4, 128], BF16) for i in range(2)]
        ps = cx.ps
        P = Phase(cx.st)
        bank = [0]

        def nb():
            b = ps[bank[0] % 8]
            bank[0] += 1
            return b

        def b3(p_, h=4):
            return p_[:].rearrange("p (h t) -> p h t", h=h)

        P.dma("sp", rwp[:], prm["rwp"])
        P.dma("sp", lop[:], prm["lop"])
        P.dma("sp", w2s[:], prm["w2"])
        P.dma("sp", a2s[:], prm["a2"])
        P.dma("sp", g2s[:], prm["g2"])
        P.dma("sp", tmb[:], prm["tmb"])
        if layer > 0:
            P.dma("sp", v2s[:], prm["v2"])
        P.ts(omka[:], rwp[:, 5, :], -1.0, 1.0, op0=ALU.mult, op1=ALU.add)
        for j in range(4):
            P.copy(m_gt4[:, j, :], cx.m_gt_f[:])
            P.copy(m_lt4[:, j, :], cx.m_lt_f[:])
            P.copy(m_le4[:, j, :], cx.m_le_f[:])
            P.copy(id4[:, j, :], cx.ident_f[:])
        P.memset(rmh[:], 1.0)
        P.memset(rmh[:, :, 0:1], 0.0)
        P.memset(ST[:], 0.0)

        def bc(ap2, n=128):
            return ap2.unsqueeze(2).to_broadcast([ap2.shape[0], ap2.shape[1], n])

        def fmv(row0):
            return pfm[row0:row0 + 512, :].rearrange("(h k) t -> k h t", k=64)

        rv, kv = fmv(R_RWR), fmv(R_RWK)
        mixv = mixT[R_MIX_RW:R_MIX_RW + 512, :].rearrange("(j p) t -> p j t", p=128)
        hs = lambda h: slice(h * 64, (h + 1) * 64)

        def shift_load(cur, prev, src3, t0, three):
            if three:
                P.dma("sp", cur[:], src3[:, :, t0:t0 + 128])
                if t0 == 0:
                    P.memset(prev[:, :, 0:1], 0.0)
                    P.dma("sp", prev[:, :, 1:128], src3[:, :, 0:127])
                else:
                    P.dma("sp", prev[:], src3[:, :, t0 - 1:t0 + 127])
            else:
                P.dma("sp", cur[:], src3[:, t0:t0 + 128])
                if t0 == 0:
                    P.memset(prev[:, 0:1], 0.0)
                    P.dma("sp", prev[:, 1:128], src3[:, 0:127])
                else:
                    P.dma("sp", prev[:], src3[:, t0 - 1:t0 + 127])

        for c in range(NT):
            t0 = c * 128
            f = fm
            shift_load(f["Rc"], f["Rp"], rv, t0, True)
            shift_load(f["Kc"], f["Kp"], kv, t0, True)
            for (cur, prev, out, mi, en) in ((f["Rc"], f["Rp"], f["Rs"], 0, "dve"), (f["Kc"], f["Kp"], f["Ks"], 1, "pool")):
                P.tt(prev[:], prev[:], cur[:], ALU.subtract, eng=en)
                P.tt(prev[:], prev[:], bc(rwp[:, mi, :]), ALU.mult, eng=en)
                P.tt(out[:], prev[:], cur[:], ALU.add, eng=en)
            shift_load(lo["WLc"], lo["WLp"], pfm[R_WLO:R_WLO + 96, :], t0, False)
            shift_load(lo["ALc"], lo["ALp"], pfm[R_ALO:R_ALO + 96, :], t0, False)
            shift_load(lo["GLc"], lo["GLp"], pfm[R_GLO:R_GLO + 256, :].rearrange("(j p) t -> p j t", p=128), t0, True)
            P.tt(lo["WLp"][:], lo["WLp"][:], lo["WLc"][:], ALU.subtract)
            P.stt(lo["WLc"][:], lo["WLp"][:], lop[:96, 0:1], lo["WLc"][:], ALU.mult, ALU.add)
            P.act(lo["WLc"][:], lo["WLc"][:], AF.Tanh)
            P.tt(lo["ALp"][:], lo["ALp"][:], lo["ALc"][:], ALU.subtract)
            P.stt(lo["ALc"][:], lo["ALp"][:], lop[:96, 1:2], lo["ALc"][:], ALU.mult, ALU.add)
            P.tt(lo["GLp"][:], lo["GLp"][:], lo["GLc"][:], ALU.subtract)
            for j in range(2):
                P.stt(lo["GLc"][:, j, :], lo["GLp"][:, j, :], lop[:, 2 + j:3 + j], lo["GLc"][:, j, :], ALU.mult, ALU.add)
            P.act(lo["GLc"][:], lo["GLc"][:], AF.Sigmoid)
            for (w_s, code, bias_i, out) in ((w2s, lo["WLc"], 2, f["SW"]), (a2s, lo["ALc"], 3, f["A"])):
                for half in range(2):
                    pb = nb()
                    for j in range(4):
                        h = half * 4 + j
                        P.mm(pb[:64, j * 128:(j + 1) * 128], w_s[:, hs(h)], code[:])
                    P.tt(out[:, half * 4:half * 4 + 4, :], b3(pb)[:64], bc(rwp[:, bias_i, half * 4:half * 4 + 4]), ALU.add)
                P.act(out[:], out[:], AF.Sigmoid)
            P.scan(f["CS"][:].rearrange("k h t -> k (h t)"), rmh[:].rearrange("k h t -> k (h t)"),
                   f["SW"][:].rearrange("k h t -> k (h t)"), 0.0, ALU.mult, ALU.add)
            P.tt(f["CX"][:], f["CS"][:], f["SW"][:], ALU.subtract, eng="pool")
            P.act(f["EP"][:], f["CS"][:], AF.Exp, scale=CW)
            P.act(f["EN"][:], f["CS"][:], AF.Exp, scale=-CW)
            P.act(f["EX"][:], f["CX"][:], AF.Exp, scale=CW)
            P.tt(f["KK0"][:], f["Ks"][:], bc(rwp[:, 4, :]), ALU.mult, eng="pool")
            P.tt(f["T0"][:], f["KK0"][:], f["KK0"][:], ALU.mult, eng="pool")
            for half in range(2):
                pb = nb()
                P.mm(pb[:64, :], cx.ones_f[:64, :64], f["T0"][:, half * 4:half * 4 + 4, :].rearrange("k h t -> k (h t)"))
                P.ts(f["T1"][:, half * 4:half * 4 + 4, :], b3(pb)[:64], 1e-16, None, op0=ALU.max)
            P.act(f["T1"][:], f["T1"][:], AF.Ln)
            P.act(f["T1"][:], f["T1"][:], AF.Exp, scale=-0.5)
            P.tt(f["KKn"][:], f["KK0"][:], f["T1"][:], ALU.mult)
            P.tt(f["Bv"][:], f["KKn"][:], f["A"][:], ALU.mult, eng="pool")
            P.tt(f["T0"][:], f["A"][:], bc(rwp[:, 5, :]), ALU.mult)
            P.tt(f["T0"][:], f["T0"][:], bc(omka[:]), ALU.add)
            P.tt(f["K2"][:], f["Ks"][:], f["T0"][:], ALU.mult)
            P.tt(f["At"][:], f["KKn"][:], f["EX"][:], ALU.mult)
            P.tt(f["Bt"][:], f["Bv"][:], f["EN"][:], ALU.mult)
            P.tt(f["Kt"][:], f["K2"][:], f["EN"][:], ALU.mult, eng="pool")
            P.tt(f["Rt"][:], f["Rs"][:], f["EP"][:], ALU.mult, eng="pool")
            eg = f["EP"][:, :, 127:128].to_broadcast([64, 8, 128])
            P.tt(f["Bh"][:], f["Bt"][:], eg, ALU.mult)
            P.tt(f["Kh"][:], f["Kt"][:], eg, ALU.mult, eng="pool")
            P.tt(f["T0"][:], f["Rs"][:], f["K2"][:], ALU.mult, eng="pool")
            P.tt(f["T0"][:], f["T0"][:], bc(rwp[:, 6, :]), ALU.mult, eng="pool")
            pb = nb()
            for h in range(H):
                P.mm(pb[:, h:h + 1], f["T0"][:, h, :], cx.ones_f[:64, 0:1])
            P.copy(RKS[:], pb[:, 0:8], eng="act")
            t = tm
            P.dma("sp", t["Vc"][:], ptm[t0:t0 + 128, C_RWV:C_RWV + 512])
            if t0 == 0:
                P.memset(t["Vp"][0:1, :], 0.0)
                P.dma("sp", t["Vp"][1:128, :], ptm[0:127, C_RWV:C_RWV + 512])
            else:
                P.dma("sp", t["Vp"][:], ptm[t0 - 1:t0 + 127, C_RWV:C_RWV + 512])
            P.tt(t["Vp"][:], t["Vp"][:], t["Vc"][:], ALU.subtract, eng="pool")
            P.tt(t["Vp"][:], t["Vp"][:], tmb[:, 0, :], ALU.mult, eng="pool")
            if layer == 0:
                P.tt(t["V"][:], t["Vp"][:], t["Vc"][:], ALU.add)
                P.dma("sp", vfirst[t0:t0 + 128, :], t["V"][:])
            else:
                P.tt(t["Vc"][:], t["Vp"][:], t["Vc"][:], ALU.add)
                shift_load(lo["VLc"], lo["VLp"], pfm[R_VLO:R_VLO + 64, :], t0, False)
                P.tt(lo["VLp"][:], lo["VLp"][:], lo["VLc"][:], ALU.subtract)
                P.stt(lo["VLc"][:], lo["VLp"][:], lop[:64, 4:5], lo["VLc"][:], ALU.mult, ALU.add)
                pb = nb()
                P.mm(pb[:, :], lo["VLc"][:], v2s[:])
                P.tt(t["SV"][:], pb[:, :], tmb[:, 1, :], ALU.add)
                P.act(t["SV"][:], t["SV"][:], AF.Sigmoid)
                P.dma("sp", t["VF"][:], vfirst[t0:t0 + 128, :])
                P.tt(t["VF"][:], t["VF"][:], t["Vc"][:], ALU.subtract)
                P.tt(t["VF"][:], t["VF"][:], t["SV"][:], ALU.mult)
                P.tt(t["V"][:], t["VF"][:], t["Vc"][:], ALU.add)
            pb = nb()
            for j in range(2):
                P.mm(pb[:, :], lo["GLc"][:, j, :], g2s[:, j, :], start=(j == 0), stop=(j == 1))
            P.copy(t["Gt"][:], pb[:, :], eng="act")
            M, N, PT = big["M0"], big["N0"], big["PT"]
            M2, N2 = big["M1"], big["N1"]
            for half in range(2):
                pa, pb = nb(), nb()
                for j in range(4):
                    h = half * 4 + j
                    P.mm(pa[:, j * 128:(j + 1) * 128], f["At"][:, h, :], f["Bt"][:, h, :])
                    P.mm(pb[:, j * 128:(j + 1) * 128], f["Bt"][:, h, :], f["At"][:, h, :])
                hh = slice(half * 4, half * 4 + 4)
                P.stt(M[:, hh, :], b3(pa), -1.0, m_gt4[:], ALU.mult, ALU.mult)
                P.stt(N[:, hh, :], b3(pb), -1.0, m_lt4[:], ALU.mult, ALU.mult)
                P.tt(PT[:, hh, :], N[:, hh, :], id4[:], ALU.add, eng="pool")
            for step in range(6):
                last = step == 5
                for half in range(2):
                    hh = slice(half * 4, half * 4 + 4)
                    pa = nb()
                    for j in range(4):
                        h = half * 4 + j
                        P.mm(pa[:, j * 128:(j + 1) * 128], N[:, h, :], M[:, h, :])
                    P.copy(M2[:, hh, :], b3(pa), eng="act")
                    if not last:
                        pb = nb()
                        for j in range(4):
                            h = half * 4 + j
                            P.mm(pb[:, j * 128:(j + 1) * 128], M[:, h, :], N[:, h, :])
                        P.copy(N2[:, hh, :], b3(pb), eng="dve")
                    pc = nb()
                    for j in range(4):
                        h = half * 4 + j
                        P.mm(pc[:, j * 128:(j + 1) * 128], M2[:, h, :], PT[:, h, :])
                    P.tt(PT[:, hh, :], b3(pc), PT[:, hh, :], ALU.add)
                M, M2 = M2, M
                N, N2 = N2, N
            for (dst, lt, rt, msk) in ((big["LAK"], f["Kt"], f["At"], m_lt4), (big["MRB"], f["Bt"], f["Rt"], m_le4),
                                       (big["MRK"], f["Kt"], f["Rt"], m_le4)):
                for half in range(2):
                    pa = nb()
                    for j in range(4):
                        h = half * 4 + j
                        P.mm(pa[:, j * 128:(j + 1) * 128], lt[:, h, :], rt[:, h, :])
                    P.tt(dst[:, half * 4:half * 4 + 4, :], b3(pa), msk[:], ALU.mult)
            for (src, dst) in ((f["Bh"], t["BHt"]), (f["Kh"], t["KHt"])):
                pa = nb()
                for h in range(H):
                    P.transpose(pa[:, hs(h)], src[:, h, :], cx.ident_f[:64, :64])
                P.copy(dst[:], pa[:, :], eng="act")
            pa = nb()
            for h in range(H):
                P.mm(pa[:, hs(h)], big["LAK"][:, h, :], t["V"][:, hs(h)], start=True, stop=False)
                P.mm(pa[:, hs(h)], f["At"][:, h, :], ST[:, h, :], start=False, stop=True)
            P.ts(t["NXZ"][:], pa[:, :], -1.0, None, op0=ALU.mult)
            pa = nb()
            for h in range(H):
                P.mm(pa[:, hs(h)], PT[:, h, :], t["NXZ"][:, hs(h)])
            P.copy(t["SA"][:], pa[:, :], eng="act")
            pa = nb()
            for h in range(H):
                P.mm(pa[:, hs(h)], f["Rt"][:, h, :], ST[:, h, :], start=True, stop=False)
                P.mm(pa[:, hs(h)], big["MRB"][:, h, :], t["SA"][:, hs(h)], start=False, stop=False)
                P.mm(pa[:, hs(h)], big["MRK"][:, h, :], t["V"][:, hs(h)], start=False, stop=True)
            P.copy(t["Y"][:], pa[:, :], eng="act")
            pa = nb()
            for h in range(H):
                P.mm(pa[:64, hs(h)], t["BHt"][:, hs(h)], t["SA"][:, hs(h)], start=True, stop=False)
                P.mm(pa[:64, hs(h)], t["KHt"][:, hs(h)], t["V"][:, hs(h)], start=False, stop=True)
            P.tt(ST[:], ST[:], f["EP"][:, :, 127:128].to_broadcast([64, 8, 64]), ALU.mult)
            P.tt(ST[:], ST[:], pa[:64, :].rearrange("k (h v) -> k h v", h=8), ALU.add)
            Y3 = t["Y"][:].rearrange("p (h v) -> p h v", h=8)
            C3 = t["CEN"][:].rearrange("p (h v) -> p h v", h=8)
            S3 = t["SQ"][:].rearrange("p (h v) -> p h v", h=8)
            N3 = t["YN"][:].rearrange("p (h v) -> p h v", h=8)
            V3 = t["V"][:].rearrange("p (h v) -> p h v", h=8)
            B3 = t["BON"][:].rearrange("p (h v) -> p h v", h=8)
            P.add("dve", lambda e: e.tensor_reduce(MS[:], Y3, AX.X, ALU.add), [t["Y"][:]], [MS[:]])
            P.ts(MS[:], MS[:], 1.0 / 64, None, op0=ALU.mult)
            P.tt(C3, Y3, MS[:].unsqueeze(2).to_broadcast([128, 8, 64]), ALU.subtract)
            P.tt(S3, C3, C3, ALU.mult, eng="pool")
            P.add("dve", lambda e: e.tensor_reduce(VS[:], S3, AX.X, ALU.add), [t["SQ"][:]], [VS[:]])
            P.act(VS[:], VS[:], AF.Ln, bias=RW_LN_EPS, scale=1.0 / 64)
            P.act(VS[:], VS[:], AF.Exp, scale=-0.5)
            P.tt(N3, C3, VS[:].unsqueeze(2).to_broadcast([128, 8, 64]), ALU.mult)
            P.tt(t["YN"][:], t["YN"][:], tmb[:, 2, :], ALU.mult)
            P.tt(t["YN"][:], t["YN"][:], tmb[:, 3, :], ALU.add)
            P.tt(B3, V3, RKS[:].unsqueeze(2).to_broadcast([128, 8, 64]), ALU.mult, eng="pool")
            P.tt(t["YN"][:], t["YN"][:], t["BON"][:], ALU.add)
            P.tt(t["YN"][:], t["YN"][:], t["Gt"][:], ALU.mult)
            pa = nb()
            for j in range(4):
                P.transpose(pa[:, j * 128:(j + 1) * 128], t["YN"][:, j * 128:(j + 1) * 128], cx.ident_f[:])
            ob = OB[c % 2]
            P.copy(ob[:], b3(pa), eng="act")
            P.dma("sp", mixv[:, :, t0:t0 + 128], ob[:])
        P.flush()


from concourse.bass_utils import run_bass_kernel_spmd

DEPTH = 4
N_META = 16
RW_SHIFT_N = 1984


def build_program(L, depth=DEPTH, Lr=None):
    nc = bass.Bass("TRN2", target_bir_lowering=False)
    dt = lambda n, s, k="ExternalInput", d=F32: nc.dram_tensor(n, s, d, kind=k).ap()
    h0 = dt("h0", [D, L])
    gains = dt("gains", [depth, 3, 128, DC])
    f1wi = dt("ffn1_wi", [depth, FC, 128, DC, 256])
    f1wo = dt("ffn1_wo", [depth, DC, 128, FC, 128])
    f2wi = dt("ffn2_wi", [depth, FC, 128, DC, 256])
    f2wo = dt("ffn2_wo", [depth, DC, 128, FC, 128])
    w_in = dt("w_in", [depth, 14, 128, DC, 512])
    w_in_v = dt("w_in_v", [depth - 1, 128, DC, 64]) if depth > 1 else None
    w_out = dt("w_out", [depth, 4, 128, DC, 512])
    hglb = dt("hglb", [128, 4, 4])
    hgn = dt("hgn", [depth, 128, 1])
    sbg = dt("sbg", [depth, 128, 3])
    rwp = dt("rwp", [depth, 64, 7, 8])
    lop = dt("lop", [depth, 128, 8])
    w2 = dt("rw_w2", [depth, 96, 512])
    a2 = dt("rw_a2", [depth, 96, 512])
    g2 = dt("rw_g2", [depth, 128, 2, 512])
    v2 = dt("rw_v2", [depth, 64, 512])
    tmb = dt("tmb", [depth, 128, 5, 512])
    hT = dt("hT", [D, L], "ExternalOutput")
    pfm = dt("pfm", [NFM, L], "Internal")
    ptm = dt("ptm", [L, NTM], "Internal")
    vfirst = dt("vfirst", [L, 512], "Internal")
    mixT = dt("mixT", [D, L], "Internal", BF16)
    with ExitStack() as es:
        cx = make_ctx(nc, es)
        cx.eps_ap = lambda e: float(e)
        init_consts(cx)
        make_masks(cx, es)
        make_lb(cx, es, hglb)
        P = Phase(cx.st)
        P.dma("sp", hT, h0)
        P.flush()
        for l in range(depth):
            ffn_phase(cx, hT, f1wi[l], f1wo[l], gains[l, 0], L, Lr=Lr)
            proj_phase(cx, hT, w_in[l], (w_in_v[l - 1] if l > 0 else None), gains[l, 1], pfm, ptm, L)
            hg_phase(cx, pfm, ptm, l, hgn[l], mixT, L)
            sb_phase(cx, pfm, ptm, sbg[l], mixT, L)
            prm = {"rwp": rwp[l], "lop": lop[l], "w2": w2[l], "a2": a2[l], "g2": g2[l], "v2": v2[l], "tmb": tmb[l]}
            rw_phase(cx, pfm, ptm, l, prm, vfirst, mixT, L)
            ffn_phase(cx, hT, f2wi[l], f2wo[l], gains[l, 2], L, pre=(mixT, w_out[l]), Lr=Lr)
    return nc, cx.st.nops


def _c(a):
    return np.ascontiguousarray(a, dtype=np.float32)


def layout_params(inp, depth=DEPTH):
    fm16 = lambda v: v.reshape(DC, 128).T
    fmh = lambda v: v.reshape(8, 64).T
    out = {}
    out["gains"] = _c(np.stack([np.stack([fm16(inp[k][l]) for k in ("norm_ffn1", "norm_mix", "norm_ffn2")]) for l in range(depth)]))
    for k in ("ffn1_wi", "ffn2_wi"):
        out[k] = _c(inp[k][:depth].reshape(depth, DC, 128, FC, 256).transpose(0, 3, 2, 1, 4))
    for k in ("ffn1_wo", "ffn2_wo"):
        out[k] = _c(inp[k][:depth].reshape(depth, FC, 128, DC, 128).transpose(0, 3, 2, 1, 4))
    wpad = np.zeros((depth, D, 7168), np.float32)
    wpad[:, :, :7104] = inp["w_in"][:depth]
    out["w_in"] = _c(wpad.reshape(depth, DC, 128, 14, 512).transpose(0, 3, 2, 1, 4))
    out["w_out"] = _c(inp["w_out"][:depth].reshape(depth, DC, 128, 4, 512).transpose(0, 3, 2, 1, 4))
    if depth > 1:
        out["w_in_v"] = _c(inp["w_in_v"][:depth - 1].reshape(depth - 1, DC, 128, 64).transpose(0, 2, 1, 3))
    out["hglb"] = _c(inp["hg_lb"].reshape(4, 4, 128).transpose(2, 1, 0))
    out["hgn"] = _c(inp["hg_norm"][:depth].reshape(depth, 128, 1))
    out["sbg"] = _c(np.stack([np.stack([inp["sb_qn"][l], inp["sb_kn"][l], inp["sb_on"][l]], axis=1) for l in range(depth)]))
    rwp, lop, v2, tmb = [], [], [], []
    for l in range(depth):
        mu = inp["rw_mu"][l]
        rwp.append(np.stack([fmh(mu[0:512]), fmh(mu[512:1024]), fmh(inp["rw_w0"][l]), fmh(inp["rw_a0"][l]), fmh(inp["rw_kk"][l]),
                             fmh(inp["rw_ka"][l]), fmh(inp["rw_rk"][l].reshape(-1))], axis=1))
        lp = np.zeros((128, 8), np.float32)
        lp[:96, 0] = mu[1536:1632]
        lp[:96, 1] = mu[1632:1728]
        lp[:, 2] = mu[1728:1856]
        lp[:, 3] = mu[1856:1984]
        tb = np.zeros((128, 5, 512), np.float32)
        tb[:, 0] = mu[1024:1536][None]
        tb[:, 2] = inp["rw_ln_w"][l][None]
        tb[:, 3] = inp["rw_ln_b"][l][None]
        vv = np.zeros((64, 512), np.float32)
        if l > 0:
            lp[:64, 4] = inp["rw_mu_v"][l - 1]
            tb[:, 1] = inp["rw_v0"][l - 1][None]
            vv = inp["rw_v2"][l - 1]
        lop.append(lp)
        tmb.append(tb)
        v2.append(vv)
    out["rwp"] = _c(np.stack(rwp))
    out["lop"] = _c(np.stack(lop))
    out["tmb"] = _c(np.stack(tmb))
    out["rw_v2"] = _c(np.stack(v2))
    out["rw_w2"] = _c(inp["rw_w2"][:depth])
    out["rw_a2"] = _c(inp["rw_a2"][:depth])
    out["rw_g2"] = _c(np.stack([inp["rw_g2"][l].reshape(2, 128, 512).transpose(1, 0, 2) for l in range(depth)]))
    return out


def kernel(**inputs):
    inp = {k: np.asarray(v) for k, v in inputs.items()}
    x = inp["x"]
    B, S, _ = x.shape
    depth = inp["norm_ffn1"].shape[0]
    L_real = N_META + S
    L = ((L_real + 127) // 128) * 128
    nc, nops = build_program(L, depth, L_real)
    shared = layout_params(inp, depth)
    n_cores = 8 if B == 4 else B
    in_maps = []
    for c in range(n_cores):
        b = c % B
        h0 = np.zeros((L, D), np.float32)
        h0[:N_META] = inp["meta"]
        h0[N_META:L_real] = x[b]
        m = dict(shared)
        m["h0"] = _c(h0.T)
        in_maps.append(m)
    res = run_bass_kernel_spmd(nc, in_maps, core_ids=list(range(n_cores)))
    out = np.stack([np.ascontiguousarray(res.results[b]["hT"].T[N_META:L_real]) for b in range(B)])
    return out.astype(np.float32)
```

```python
from contextlib import ExitStack
import math
import numpy as np
import concourse.bass as bass
import concourse.mybir as mybir

F32 = mybir.dt.float32
BF16 = mybir.dt.bfloat16
AF = mybir.ActivationFunctionType
ALU = mybir.AluOpType
AX = mybir.AxisListType

ENGS = ("pe", "act", "dve", "pool", "sp")
NDMA = 12


def region(ap):
    t = ap.tensor
    shp = list(t.shape)
    rowlen = 1
    for s in shp[1:]:
        rowlen *= s
    off = int(ap.offset)
    r0 = off // rowlen
    c0 = off % rowlen
    rext = 0
    cext = 0
    for step, cnt in ap.ap:
        step = abs(int(step))
        cnt = int(cnt)
        if cnt <= 1 or step == 0:
            continue
        if step >= rowlen and step % rowlen == 0:
            rext += (cnt - 1) * (step // rowlen)
        else:
            cext += (cnt - 1) * step
    c1 = c0 + cext + 1
    if c1 > rowlen:
        extra = (c1 - 1) // rowlen
        rext += extra
        c0, c1 = 0, rowlen
    return (ap.name, r0, r0 + rext + 1, c0, c1)


class State:
    def __init__(self, nc, es):
        self.nc = nc
        self.sem = {}
        self.cnt = {}
        for e in ENGS:
            self.sem[e] = es.enter_context(nc.semaphore("s_" + e))
            self.cnt[e] = 0
        self.dsem = {}
        self.dcnt = {}
        self.dnext = {}
        for q in ("sp", "pool", "act"):
            self.dsem[q] = [es.enter_context(nc.semaphore("d_%s%d" % (q, i))) for i in range(NDMA)]
            self.dcnt[q] = [0] * NDMA
            self.dnext[q] = 0
        self.waited = {e: {} for e in ENGS}
        self.nops = 0


class Phase:
    def __init__(self, st):
        self.st = st
        self.nc = st.nc
        self.ops = {e: [] for e in ENGS}
        self.recs = {}
        self.order = 0

    def _deps(self, reads, writes):
        deps = {}

        def add(done):
            k = done[0]
            if k not in deps or deps[k][2] < done[2]:
                deps[k] = done

        for ap in reads:
            nm, r0, r1, c0, c1 = region(ap)
            for rec in self.recs.get(nm, ()):
                if rec[5] and rec[0] < r1 and r0 < rec[1] and rec[2] < c1 and c0 < rec[3]:
                    add(rec[4])
        for ap in writes:
            nm, r0, r1, c0, c1 = region(ap)
            for rec in self.recs.get(nm, ()):
                if rec[0] < r1 and r0 < rec[1] and rec[2] < c1 and c0 < rec[3]:
                    add(rec[4])
        return deps

    def _record(self, reads, writes, done):
        for ap in writes:
            nm, r0, r1, c0, c1 = region(ap)
            lst = self.recs.setdefault(nm, [])
            lst[:] = [rc for rc in lst if not (r0 <= rc[0] and rc[1] <= r1 and c0 <= rc[2] and rc[3] <= c1)]
            lst.append([r0, r1, c0, c1, done, True])
        for ap in reads:
            nm, r0, r1, c0, c1 = region(ap)
            lst = self.recs.setdefault(nm, [])
            lst[:] = [rc for rc in lst if not ((not rc[5]) and rc[4][0] == done[0] and r0 <= rc[0] and rc[1] <= r1 and c0 <= rc[2] and rc[3] <= c1)]
            lst.append([r0, r1, c0, c1, done, False])

    def add(self, eng, fn, reads, writes, pe_skip=True):
        st = self.st
        deps = self._deps(reads, writes)
        st.cnt[eng] += 1
        done = ("e_" + eng, st.sem[eng], st.cnt[eng])
        waits = []
        for k, d in deps.items():
            if eng == "pe" and k == "e_pe":
                continue
            if st.waited[eng].get(k, 0) >= d[2]:
                continue
            st.waited[eng][k] = d[2]
            waits.append((d[1], d[2]))
        self.ops[eng].append((fn, waits, (st.sem[eng], 1)))
        self._record(reads, writes, done)
        st.nops += 1

    def dma(self, q, out, in_, **kw):
        st = self.st
        reads, writes = [in_], [out]
        deps = self._deps(reads, writes)
        i = st.dnext[q]
        st.dnext[q] = (i + 1) % NDMA
        sem = st.dsem[q][i]
        key = "d_%s%d" % (q, i)
        waits = []
        if st.dcnt[q][i] > 0 and st.waited[q].get(key, 0) < st.dcnt[q][i]:
            waits.append((sem, st.dcnt[q][i]))
            st.waited[q][key] = st.dcnt[q][i]
        st.dcnt[q][i] += 16
        done = (key, sem, st.dcnt[q][i])
        for k, d in deps.items():
            if st.waited[q].get(k, 0) >= d[2]:
                continue
            st.waited[q][k] = d[2]
            waits.append((d[1], d[2]))
        self.ops[q].append((lambda e: e.dma_start(out=out, in_=in_, **kw), waits, (sem, 16)))
        self._record(reads, writes, done)
        st.nops += 1

    def mm(self, out, lhsT, rhs, start=True, stop=True, sgc=False):
        if sgc:
            self.add("pe", lambda e: e.matmul(out, lhsT, rhs, start=start, stop=stop, skip_group_check=True), [lhsT, rhs], [out])
        else:
            self.add("pe", lambda e: e.matmul(out, lhsT, rhs, start=start, stop=stop), [lhsT, rhs], [out])

    def transpose(self, out, in_, ident):
        self.add("pe", lambda e: e.transpose(out, in_, ident), [in_, ident], [out])

    def act(self, out, in_, func, bias=None, scale=1.0, eng="act"):
        rd = [in_]
        kw = {}
        if bias is not None:
            kw["bias"] = bias
            if not isinstance(bias, (int, float)):
                rd.append(bias)
        if not isinstance(scale, (int, float)):
            rd.append(scale)
        self.add("act", lambda e: e.activation(out, in_, func, scale=scale, **kw), rd, [out])

    def tt(self, out, in0, in1, op, eng="dve"):
        self.add(eng, lambda e: e.tensor_tensor(out, in0, in1, op), [in0, in1], [out])

    def ts(self, out, in0, s1, s2=None, op0=ALU.mult, op1=ALU.bypass, eng="dve"):
        rd = [in0]
        for s in (s1, s2):
            if s is not None and not isinstance(s, (int, float)):
                rd.append(s)
        if s2 is None:
            self.add(eng, lambda e: e.tensor_scalar(out, in0, s1, None, op0), rd, [out])
        else:
            self.add(eng, lambda e: e.tensor_scalar(out, in0, s1, s2, op0, op1), rd, [out])

    def stt(self, out, in0, scalar, in1, op0, op1):
        rd = [in0, in1]
        if not isinstance(scalar, (int, float)):
            rd.append(scalar)
        self.add("dve", lambda e: e.scalar_tensor_tensor(out, in0, scalar, in1, op0, op1), rd, [out])

    def copy(self, out, in_, eng="dve"):
        if eng == "act":
            self.add("act", lambda e: e.copy(out, in_), [in_], [out])
        else:
            self.add(eng, lambda e: e.tensor_copy(out, in_), [in_], [out])

    def memset(self, out, val, eng="dve"):
        self.add(eng, lambda e: e.memset(out, val), [], [out])

    def recip(self, out, in_):
        self.add("dve", lambda e: e.reciprocal(out, in_), [in_], [out])

    def scan(self, out, d0, d1, init, op0, op1):
        rd = [d0, d1]
        if not isinstance(init, (int, float)):
            rd.append(init)
        self.add("dve", lambda e: e.tensor_tensor_scan(out, d0, d1, init, op0, op1), rd, [out])

    def flush(self, final=False):
        st = self.st
        nc = self.nc
        finals = []
        for e in ENGS:
            if st.cnt[e] > 0:
                finals.append(("e_" + e, st.sem[e], st.cnt[e]))
        for q in st.dsem:
            for i in range(NDMA):
                if st.dcnt[q][i] > 0:
                    finals.append(("d_%s%d" % (q, i), st.dsem[q][i], st.dcnt[q][i]))
        ops = self.ops
        emap = {"pe": "tensor", "act": "scalar", "dve": "vector", "pool": "gpsimd", "sp": "sync"}

        def mk(ename):
            def body(eng):
                for fn, waits, inc in ops[ename]:
                    for s, v in waits:
                        eng.wait_ge(s, v)
                    ins = fn(eng)
                    ins.then_inc(inc[0], inc[1])
                for k, s, v in finals:
                    if st.waited[ename].get(k, 0) >= v:
                        continue
                    st.waited[ename][k] = v
                    eng.wait_ge(s, v)
            return body

        with nc.Block() as block:
            for ename in ENGS:
                getattr(block, emap[ename])(mk(ename))
        self.ops = {e: [] for e in ENGS}
        self.recs = {}


def _coll(self, in_ap, out_ap, groups):
    st = self.st
    q = "pool"
    reads, writes = [in_ap], [out_ap]
    deps = self._deps(reads, writes)
    i = st.dnext[q]
    st.dnext[q] = (i + 1) % NDMA
    sem = st.dsem[q][i]
    key = "d_%s%d" % (q, i)
    waits = []
    if st.dcnt[q][i] > 0 and st.waited[q].get(key, 0) < st.dcnt[q][i]:
        waits.append((sem, st.dcnt[q][i]))
        st.waited[q][key] = st.dcnt[q][i]
    st.dcnt[q][i] += 16
    done = (key, sem, st.dcnt[q][i])
    for k, d in deps.items():
        if st.waited[q].get(k, 0) >= d[2]:
            continue
        st.waited[q][k] = d[2]
        waits.append((d[1], d[2]))
    self.ops[q].append((lambda e: e.collective_compute("AllGather", ALU.bypass, replica_groups=groups, ins=[in_ap], outs=[out_ap]), waits, (sem, 16)))
    self._record(reads, writes, done)
    st.nops += 1


Phase.allgather = _coll


D = 2048
DC = 16
DFF = 5632
FC = 44
EPS = 1e-6


def tiles_of(L, TT=512):
    out = []
    t = 0
    while t < L:
        n = min(TT, L - t)
        out.append((t, n))
        t += n
    return out


class Ctx:
    pass


def make_ctx(nc, es):
    cx = Ctx()
    cx.nc = nc
    cx.uid = [0]
    cx.st = State(nc, es)
    cx.ps = [es.enter_context(nc.psum_tensor("ps%d" % i, [128, 512], F32)) for i in range(8)]
    cx.ones_bf = es.enter_context(nc.sbuf_tensor("ones_bf", [128, 128], BF16))
    cx.ones_f = es.enter_context(nc.sbuf_tensor("ones_f", [128, 128], F32))
    cx.ident_f = es.enter_context(nc.sbuf_tensor("ident_f", [128, 128], F32))
    return cx


def init_consts(cx):
    P = Phase(cx.st)
    P.memset(cx.ones_bf[:], 1.0)
    P.memset(cx.ones_f[:], 1.0)
    nc = cx.nc
    P.add("pool", lambda e: e.affine_select(cx.ident_f[:], cx.ones_f[:], pattern=[[1, 128]], compare_op=ALU.is_equal,
                                             fill=0.0, base=0, channel_multiplier=-1), [cx.ones_f[:]], [cx.ident_f[:]])
    P.flush()


def rmsnorm_fm(P, cx, h, g, u, sq, rstd, ncn, TT, dn, eps, psb, extra_scale=1.0):
    for c in range(ncn):
        P.act(sq[:, c, :TT], h[:, c, :TT], AF.Square)
    for c in range(ncn):
        P.mm(psb[:, :TT], cx.ones_bf[:], sq[:, c, :TT], start=(c == 0), stop=(c == ncn - 1))
    P.act(rstd[:, :TT], psb[:, :TT], AF.Sqrt, bias=cx.eps_ap(eps), scale=1.0 / dn)
    P.recip(rstd[:, :TT], rstd[:, :TT])
    if extra_scale != 1.0:
        P.ts(rstd[:, :TT], rstd[:, :TT], float(extra_scale), None, op0=ALU.mult)
    for c in range(ncn):
        P.stt(u[:, c, :TT], h[:, c, :TT], g[:, c:c + 1], rstd[:, :TT], ALU.mult, ALU.mult)


def ffn_phase(cx, hT, wi, wo, gvec, L, pre=None, Lr=None):
    nc = cx.nc
    with ExitStack() as es:
        cx.uid[0] += 1
        tg = "_%d" % cx.uid[0]
        sb = lambda n, s, d: es.enter_context(nc.sbuf_tensor(n + tg, s, d))
        h = sb("f_h", [128, DC, 512], F32)
        u = sb("f_u", [128, DC, 512], BF16)
        hid = sb("f_hid", [128, FC, 512], BF16)
        rstd = sb("f_rstd", [128, 512], F32)
        g = sb("f_g", [128, DC], F32)
        wg = [sb("f_wg%d" % i, [128, DC, 256], BF16) for i in range(2)]
        wu = [sb("f_wu%d" % i, [128, DC, 256], BF16) for i in range(2)]
        wos = [sb("f_wo%d" % i, [128, FC, 128], BF16) for i in range(2)]
        tmp = [sb("f_tmp%d" % i, [128, 512], F32) for i in range(2)]
        P = Phase(cx.st)
        P.dma("sp", g[:], gvec)
        hv = hT.rearrange("(c p) t -> p c t", p=128)
        ps = cx.ps
        for (t0, TT) in tiles_of(L):
            P.dma("sp", h[:, :, :TT], hv[:, :, t0:t0 + TT])
            if pre is not None:
                mixT, w_out = pre
                mv = mixT.rearrange("(c p) t -> p c t", p=128)
                P.dma("sp", u[:, :, :TT], mv[:, :, t0:t0 + TT])
                for ds in range(4):
                    slab = hid[:, (ds % 2) * 16:(ds % 2) * 16 + 16, :]
                    P.dma("pool", slab, w_out[ds])
                    for dj in range(4):
                        dc = ds * 4 + dj
                        po = ps[4 + dc % 2]
                        for mc in range(DC):
                            P.mm(po[:, :TT], slab[:, mc, dj * 128:(dj + 1) * 128], u[:, mc, :TT], start=(mc == 0), stop=(mc == DC - 1))
                        P.tt(h[:, dc, :TT], po[:, :TT], h[:, dc, :TT], ALU.add)
            rmsnorm_fm(P, cx, h, g, u, hid, rstd, DC, TT, D, EPS, ps[7])
            k = 0
            for j2 in range(FC // 2):
                b = j2 % 2
                P.dma("pool", wg[b][:], wi[j2])
                P.dma("pool", wu[b][:], wi[FC // 2 + j2])
                for jj in range(2):
                    j = 2 * j2 + jj
                    pa = ps[(k % 2) * 2]
                    pb = ps[(k % 2) * 2 + 1]
                    tm = tmp[k % 2]
                    k += 1
                    for c in range(DC):
                        P.mm(pa[:, :TT], wg[b][:, c, jj * 128:(jj + 1) * 128], u[:, c, :TT], start=(c == 0), stop=(c == DC - 1))
                    for c in range(DC):
                        P.mm(pb[:, :TT], wu[b][:, c, jj * 128:(jj + 1) * 128], u[:, c, :TT], start=(c == 0), stop=(c == DC - 1))
                    P.act(tm[:, :TT], pa[:, :TT], AF.Silu)
                    P.tt(hid[:, j, :TT], tm[:, :TT], pb[:, :TT], ALU.mult)
            for dc in range(DC):
                b = dc % 2
                P.dma("pool", wos[b][:], wo[dc])
                po = ps[4 + b]
                for j in range(FC):
                    P.mm(po[:, :TT], wos[b][:, j, :], hid[:, j, :TT], start=(j == 0), stop=(j == FC - 1))
                P.stt(h[:, dc, :TT], po[:, :TT], 0.5, h[:, dc, :TT], ALU.mult, ALU.add)
            if Lr is not None and t0 + TT > Lr:
                P.memset(h[:, :, max(0, Lr - t0):TT], 0.0)
            P.dma("sp", hv[:, :, t0:t0 + TT], h[:, :, :TT])
        P.flush()


R_HGQ, R_HGF, R_HGG = 0, 512, 1024
R_SBQ, R_SBK = 1536, 2560
R_RWR, R_RWK = 3584, 4096
R_WLO, R_ALO, R_GLO, R_VLO = 4608, 4736, 4864, 5120
NFM = 5248
C_HGI, C_SBV, C_RWV = 0, 512, 1536
NTM = 2048

SLABS = [
    (0, 512, "fm", R_HGQ), (512, 512, "fm", R_HGF), (1536, 512, "fm", R_HGG),
    (2048, 512, "fm", R_SBQ), (2560, 512, "fm", R_SBQ + 512),
    (3072, 512, "fm", R_SBK), (3584, 512, "fm", R_SBK + 512),
    (5120, 512, "fm", R_RWR), (5632, 512, "fm", R_RWK),
    (6656, 448, "lo", 0),
    (1024, 512, "tm", C_HGI), (4096, 512, "tm", C_SBV), (4608, 512, "tm", C_SBV + 512), (6144, 512, "tm", C_RWV),
]


def proj_phase(cx, hT, w_in, w_in_v, gvec, pfm, ptm, L):
    nc = cx.nc
    with ExitStack() as es:
        cx.uid[0] += 1
        tg = "_%d" % cx.uid[0]
        sb = lambda n, s, d: es.enter_context(nc.sbuf_tensor(n + tg, s, d))
        h = sb("p_h", [128, DC, 512], F32)
        u = sb("p_u", [128, DC, 512], BF16)
        sq = sb("p_sq", [128, DC, 512], BF16)
        rstd = sb("p_rstd", [128, 512], F32)
        g = sb("p_g", [128, DC], F32)
        ws = [sb("p_w%d" % i, [128, DC, 512], BF16) for i in range(2)]
        wv = sb("p_wv", [128, DC, 64], BF16)
        stg = [sb("p_stg%d" % i, [128, 512], F32) for i in range(4)]
        P = Phase(cx.st)
        P.dma("sp", g[:], gvec)
        hv = hT.rearrange("(c p) t -> p c t", p=128)
        if w_in_v is not None:
            P.dma("pool", wv[:], w_in_v)
        ps = cx.ps
        k = 0
        nslab = 0
        for (t0, TT) in tiles_of(L):
            P.dma("sp", h[:, :, :TT], hv[:, :, t0:t0 + TT])
            rmsnorm_fm(P, cx, h, g, u, sq, rstd, DC, TT, D, EPS, ps[7])

            def fm_chunk(wt, cs, M, drow):
                nonlocal k
                pb = ps[k % 4]
                sg = stg[k % 4]
                for c in range(DC):
                    P.mm(pb[:M, :TT], wt[:, c, cs:cs + M], u[:, c, :TT], start=(c == 0), stop=(c == DC - 1))
                if k % 2 == 0:
                    P.copy(sg[:M, :TT], pb[:M, :TT], eng="act")
                else:
                    P.copy(sg[:M, :TT], pb[:M, :TT], eng="dve")
                P.dma("sp", pfm[drow:drow + M, t0:t0 + TT], sg[:M, :TT])
                k += 1

            for (c0, ncol, kind, dst) in SLABS:
                wt = ws[nslab % 2]
                nslab += 1
                P.dma("pool", wt[:], w_in[c0 // 512])
                if kind == "fm":
                    for j in range(ncol // 128):
                        fm_chunk(wt, j * 128, 128, dst + j * 128)
                elif kind == "lo":
                    fm_chunk(wt, 0, 96, R_WLO)
                    fm_chunk(wt, 96, 96, R_ALO)
                    fm_chunk(wt, 192, 128, R_GLO)
                    fm_chunk(wt, 320, 128, R_GLO + 128)
                else:
                    for tb in range(TT // 128):
                        pb = ps[k % 4]
                        sg = stg[k % 4]
                        for c in range(DC):
                            P.mm(pb[:, :ncol], u[:, c, tb * 128:(tb + 1) * 128], wt[:, c, :ncol], start=(c == 0), stop=(c == DC - 1))
                        if k % 2 == 0:
                            P.copy(sg[:, :ncol], pb[:, :ncol], eng="act")
                        else:
                            P.copy(sg[:, :ncol], pb[:, :ncol], eng="dve")
                        P.dma("sp", ptm[t0 + tb * 128:t0 + (tb + 1) * 128, dst:dst + ncol], sg[:, :ncol])
                        k += 1
            if w_in_v is not None:
                fm_chunk(wv, 0, 64, R_VLO)
        P.flush()


SB_HEADS = 8
R_MIX_HG, R_MIX_SB, R_MIX_RW = 0, 512, 1536


def rmsnorm1(P, cx, x, gcol, out, sqs, rstd, TT, dn, eps, psb, extra=1.0):
    P.act(sqs[:, :TT], x, AF.Square)
    P.mm(psb[:, :TT], cx.ones_bf[:], sqs[:, :TT], start=True, stop=True)
    P.act(rstd[:, :TT], psb[:, :TT], AF.Ln, bias=float(eps), scale=1.0 / dn)
    P.act(rstd[:, :TT], rstd[:, :TT], AF.Exp, bias=float(math.log(extra)), scale=-0.5)
    P.stt(out, x, gcol, rstd[:, :TT], ALU.mult, ALU.mult)


def make_masks(cx, es):
    nc = cx.nc
    sb = lambda n, s, d: es.enter_context(nc.sbuf_tensor(n, s, d))
    cx.ones_w = sb("ones_w", [128, 896], BF16)
    cx.tri_incl = sb("tri_incl", [128, 128], BF16)
    cx.tri_ls = sb("tri_ls", [128, 128], BF16)
    cx.mw = sb("mw", [128, 896], BF16)
    cx.m_le = sb("m_le", [128, 128], BF16)
    cx.m_lt_f = sb("m_lt_f", [128, 128], F32)
    cx.m_le_f = sb("m_le_f", [128, 128], F32)
    cx.m_gt_f = sb("m_gt_f", [128, 128], F32)
    P = Phase(cx.st)
    P.memset(cx.ones_w[:], 1.0)

    def sel(out, in_, pat, cm, base, op):
        P.add("pool", lambda e: e.affine_select(out, in_, pattern=pat, compare_op=op, fill=0.0, base=base,
                                                 channel_multiplier=cm), [in_], [out])
    sel(cx.tri_incl[:], cx.ones_w[:, 0:128], [[-1, 128]], 1, 0, ALU.is_ge)
    sel(cx.tri_ls[:], cx.ones_w[:, 0:128], [[1, 128]], -1, 0, ALU.is_gt)
    sel(cx.mw[:], cx.ones_w[:], [[1, 896]], -1, -384, ALU.is_gt)
    sel(cx.m_le[:], cx.ones_w[:, 0:128], [[1, 128]], -1, 0, ALU.is_ge)
    sel(cx.m_lt_f[:], cx.ones_f[:], [[1, 128]], -1, 0, ALU.is_gt)
    sel(cx.m_le_f[:], cx.ones_f[:], [[1, 128]], -1, 0, ALU.is_ge)
    sel(cx.m_gt_f[:], cx.ones_f[:], [[-1, 128]], 1, 0, ALU.is_gt)
    P.flush()


def sb_phase(cx, pfm, ptm, gains, mixT, L):
    nc = cx.nc
    NT = L // 128
    with ExitStack() as es:
        cx.uid[0] += 1
        tg = "_%d" % cx.uid[0]
        sb = lambda n, s, d: es.enter_context(nc.sbuf_tensor(n + tg, s, d))
        gn = sb("s_gn", [128, 3], F32)
        qn = [sb("s_qn%d" % i, [128, L], BF16) for i in range(2)]
        kn = [sb("s_kn%d" % i, [128, L], BF16) for i in range(2)]
        vh = [sb("s_v%d" % i, [128, NT, 128], BF16) for i in range(2)]
        xin = [sb("s_x%d" % i, [128, 512], F32) for i in range(2)]
        sqs = sb("s_sq", [128, 512], BF16)
        rstd = sb("s_rstd", [128, 512], F32)
        E = [sb("s_E%d" % i, [128, 512], F32) for i in range(3)]
        Lb = [sb("s_L%d" % i, [128, 512], BF16) for i in range(3)]
        T1 = [sb("s_T1%d" % i, [128, 512], F32) for i in range(2)]
        T2 = [sb("s_T2%d" % i, [128, 512], F32) for i in range(3)]
        At = [sb("s_A%d" % i, [128, 512], BF16) for i in range(3)]
        Cs = sb("s_Cs", [128, 512], F32)
        oh = sb("s_oh", [128, 512], F32)
        ob = [sb("s_ob%d" % i, [128, 512], BF16) for i in range(2)]
        ps = cx.ps
        P = Phase(cx.st)
        P.dma("sp", gn[:], gains)
        ptv = ptm.rearrange("(n p) c -> p n c", p=128)
        kstep = 0
        for hd in range(SB_HEADS):
            b = hd % 2
            for n0 in range(0, NT, 8):
                n1 = min(NT, n0 + 8)
                P.dma("pool", vh[b][:, n0:n1, :], ptv[:, n0:n1, C_SBV + hd * 128:C_SBV + (hd + 1) * 128])
            i = 0
            for (t0, TT) in tiles_of(L):
                for (row, gi, dst, extra) in ((R_SBQ, 0, qn[b], 128.0 ** -0.5), (R_SBK, 1, kn[b], 1.0)):
                    x = xin[i % 2]
                    i += 1
                    P.dma("sp", x[:, :TT], pfm[row + hd * 128:row + (hd + 1) * 128, t0:t0 + TT])
                    rmsnorm1(P, cx, x[:, :TT], gn[:, gi:gi + 1], dst[:, t0:t0 + TT], sqs, rstd, TT, 128, EPS, ps[7], extra)
            for (t0, TQ) in tiles_of(L):
                sb_max = (t0 + TQ - 1) // 128
                P.memset(Cs[:, :TQ], 0.0, eng="pool")
                po = ps[6]
                steps = list(range(sb_max, -1, -1))

                def stageA1(sbk, w):
                    pa = ps[w]
                    off = sbk * 128 - t0
                    P.mm(pa[:, :TQ], kn[b][:, sbk * 128:(sbk + 1) * 128], qn[b][:, t0:t0 + TQ], start=True, stop=True)
                    P.act(E[w][:, :TQ], pa[:, :TQ], AF.Exp)
                    P.act(Lb[w][:, :TQ], E[w][:, :TQ], AF.Ln, bias=1.0)
                    if off >= 0:
                        P.tt(Lb[w][:, :TQ], Lb[w][:, :TQ], cx.mw[:, 384 - off:384 - off + TQ], ALU.mult, eng="pool")

                def stageA2(sbk, w):
                    pa, pc = ps[w], ps[3 + w]
                    P.mm(pa[:, :TQ], cx.tri_ls[:], Lb[w][:, :TQ], start=False, stop=True, sgc=True)
                    P.mm(pc[:, :TQ], cx.ones_bf[:], Lb[w][:, :TQ])

                def stageB(sbk, w):
                    pa, pc = ps[w], ps[3 + w]
                    off = sbk * 128 - t0
                    P.tt(Cs[:, :TQ], pc[:, :TQ], Cs[:, :TQ], ALU.add)
                    P.tt(T2[w][:, :TQ], pa[:, :TQ], Cs[:, :TQ], ALU.subtract)
                    P.act(At[w][:, :TQ], T2[w][:, :TQ], AF.Exp)
                    if off >= 0:
                        P.tt(At[w][:, :TQ], At[w][:, :TQ], cx.mw[:, 384 - off:384 - off + TQ], ALU.mult, eng="pool")
                    P.mm(po[:, :TQ], vh[b][:, sbk, :], At[w][:, :TQ], start=(sbk == sb_max), stop=(sbk == 0))

                n_s = len(steps)
                stageA1(steps[0], kstep % 3)
                if n_s > 1:
                    stageA1(steps[1], (kstep + 1) % 3)
                stageA2(steps[0], kstep % 3)
                for i_s, sbk in enumerate(steps):
                    w = kstep % 3
                    if i_s + 2 < n_s:
                        stageA1(steps[i_s + 2], (kstep + 2) % 3)
                    if i_s + 1 < n_s:
                        stageA2(steps[i_s + 1], (kstep + 1) % 3)
                    stageB(sbk, w)
                    kstep += 1
                P.copy(oh[:, :TQ], po[:, :TQ], eng="act")
                o2 = ob[(t0 // 512) % 2]
                rmsnorm1(P, cx, oh[:, :TQ], gn[:, 2:3], o2[:, :TQ], sqs, rstd, TQ, 128, EPS, ps[7])
                P.dma("sp", mixT[R_MIX_SB + hd * 128:R_MIX_SB + (hd + 1) * 128, t0:t0 + TQ], o2[:, :TQ])
        P.flush()


HG_HEADS = 4


def make_lb(cx, es, hg_lb_ap):
    nc = cx.nc
    sb = lambda n, s, d: es.enter_context(nc.sbuf_tensor(n, s, d))
    cx.lb = sb("lb", [128, 4, 4], F32)
    cx.oml = sb("oml", [128, 4, 4], F32)
    cx.noml = sb("noml", [128, 4, 4], F32)
    cx.rmask = sb("rmask", [128, 512], F32)
    with ExitStack() as es2:
        x = es2.enter_context(nc.sbuf_tensor("lb_x", [128, 4, 4], F32))
        e = es2.enter_context(nc.sbuf_tensor("lb_e", [128, 4, 4], F32))
        s = es2.enter_context(nc.sbuf_tensor("lb_s", [128, 4], F32))
        P = Phase(cx.st)
        P.dma("sp", x[:], hg_lb_ap)
        P.act(e[:], x[:], AF.Exp)
        P.tt(s[:], e[:, :, 0], e[:, :, 1], ALU.add)
        P.tt(s[:], s[:], e[:, :, 2], ALU.add)
        P.tt(s[:], s[:], e[:, :, 3], ALU.add)
        P.recip(s[:], s[:])
        P.memset(cx.lb[:, 0, :], 0.0)
        for l in range(1, 4):
            P.tt(e[:, :, l], e[:, :, l], s[:], ALU.mult)
            P.tt(cx.lb[:, l, :], cx.lb[:, l - 1, :], e[:, :, l], ALU.add)
        P.ts(cx.oml[:], cx.lb[:], -1.0, 1.0, op0=ALU.mult, op1=ALU.add)
        P.ts(cx.noml[:], cx.lb[:], 1.0, -1.0, op0=ALU.mult, op1=ALU.add)
        P.memset(cx.rmask[:], 1.0)
        for c in range(8):
            P.memset(cx.rmask[:, c * 64:c * 64 + 1], 0.0)
        P.flush()


def hg_phase(cx, pfm, ptm, layer, gnorm, mixT, L):
    nc = cx.nc
    NT = L // 128
    H = HG_HEADS
    with ExitStack() as es:
        cx.uid[0] += 1
        tg = "_%d" % cx.uid[0]
        sb = lambda n, s, d: es.enter_context(nc.sbuf_tensor(n + tg, s, d))
        gn = sb("g_gn", [128, 1], F32)
        V = [sb("g_v%d" % i, [64, 2 * NT, 128], BF16) for i in range(H)]
        X = [sb("g_x%d" % i, [128, 512], F32) for i in range(3)]
        SG = sb("g_sg", [128, 512], F32)
        FG = sb("g_fg", [128, 512], F32)
        QS = [sb("g_qs%d" % i, [128, 512], F32) for i in range(H)]
        KK = [sb("g_kk%d" % i, [128, 512], F32) for i in range(H)]
        G = [sb("g_G%d" % i, [128, 512], F32) for i in range(H)]
        NG = [sb("g_NG%d" % i, [128, 512], F32) for i in range(H)]
        EG = [sb("g_EG%d" % i, [128, 512], F32) for i in range(H)]
        QP = [sb("g_QP%d" % i, [128, 512], BF16) for i in range(H)]
        OH = [sb("g_OH%d" % i, [128, 512], F32) for i in range(H)]
        S = [sb("g_S%d" % i, [128, 128], F32) for i in range(H)]
        Sbf = [sb("g_Sb%d" % i, [128, 128], BF16) for i in range(H)]
        tmp = [sb("g_t%d" % i, [128, 64], F32) for i in range(6)]
        QT = [sb("g_QT%d" % i, [128, 64], BF16) for i in range(2)]
        KT = [sb("g_KT%d" % i, [128, 64], BF16) for i in range(2)]
        KH = [sb("g_KH%d" % i, [128, 64], F32) for i in range(2)]
        AM = [sb("g_AM%d" % i, [64, 64], BF16) for i in range(2)]
        AF32 = [sb("g_AF%d" % i, [64, 64], F32) for i in range(2)]
        KHt = [sb("g_KHt%d" % i, [64, 128], BF16) for i in range(2)]
        sqs = sb("g_sq", [128, 512], BF16)
        rstd = sb("g_rstd", [128, 512], F32)
        ON = sb("g_on", [128, 512], F32)
        MX = [sb("g_mx%d" % i, [128, 512], BF16) for i in range(2)]
        ps = cx.ps
        P = Phase(cx.st)
        P.dma("sp", gn[:], gnorm)
        ptv = ptm.rearrange("(n p) c -> p n c", p=64)
        for hd in range(H):
            for n0 in range(0, 2 * NT, 16):
                n1 = min(2 * NT, n0 + 16)
                P.dma("pool", V[hd][:, n0:n1, :], ptv[:, n0:n1, C_HGI + hd * 128:C_HGI + (hd + 1) * 128])
            P.memset(S[hd][:], 0.0)
            P.memset(Sbf[hd][:], 0.0)
        k = 0
        nm = 0
        for (t0, TT) in tiles_of(L):
            for hd in range(H):
                lb = cx.lb[:, layer, hd:hd + 1]
                oml = cx.oml[:, layer, hd:hd + 1]
                noml = cx.noml[:, layer, hd:hd + 1]
                P.dma("sp", X[0][:, :TT], pfm[R_HGF + hd * 128:R_HGF + (hd + 1) * 128, t0:t0 + TT])
                P.dma("sp", X[1][:, :TT], pfm[R_HGQ + hd * 128:R_HGQ + (hd + 1) * 128, t0:t0 + TT])
                P.act(SG[:, :TT], X[0][:, :TT], AF.Sigmoid)
                P.ts(FG[:, :TT], SG[:, :TT], oml, lb, op0=ALU.mult, op1=ALU.add)
                P.act(FG[:, :TT], FG[:, :TT], AF.Ln)
                P.ts(KK[hd][:, :TT], SG[:, :TT], noml, oml, op0=ALU.mult, op1=ALU.add)
                P.act(QS[hd][:, :TT], X[1][:, :TT], AF.Silu)
                P.scan(G[hd][:, :TT], cx.rmask[:, :TT], FG[:, :TT], 0.0, ALU.mult, ALU.add)
                P.ts(NG[hd][:, :TT], G[hd][:, :TT], -1.0, None, op0=ALU.mult)
                P.act(EG[hd][:, :TT], G[hd][:, :TT], AF.Exp)
                P.tt(QP[hd][:, :TT], QS[hd][:, :TT], EG[hd][:, :TT], ALU.mult)
            for c in range(TT // 64):
                blk = t0 // 64 + c
                cs = slice(c * 64, (c + 1) * 64)
                mid = c * 64 + 31
                end = c * 64 + 63
                for hd in range(H):
                    w = k % 2
                    k += 1
                    t1, t2, t3 = tmp[w * 3], tmp[w * 3 + 1], tmp[w * 3 + 2]
                    P.act(t1[:], G[hd][:, cs], AF.Exp, bias=NG[hd][:, mid:mid + 1])
                    P.stt(QT[w][:], t1[:], 1e30, QS[hd][:, cs], ALU.min, ALU.mult)
                    P.act(t2[:], G[hd][:, cs], AF.Exp, bias=G[hd][:, mid:mid + 1], scale=-1.0)
                    P.stt(KT[w][:], t2[:], 1e30, KK[hd][:, cs], ALU.min, ALU.mult)
                    P.act(t3[:], G[hd][:, cs], AF.Exp, bias=G[hd][:, end:end + 1], scale=-1.0)
                    P.tt(KH[w][:], KK[hd][:, cs], t3[:], ALU.mult, eng="pool")
                    pa, po, pt, pn = ps[w], ps[2 + w], ps[4 + w], ps[6]
                    P.mm(pa[:64, :64], KT[w][:], QT[w][:])
                    P.ts(AF32[w][:], pa[:64, :64], 1e30, -1e30, op0=ALU.min, op1=ALU.max)
                    P.tt(AM[w][:], AF32[w][:], cx.m_le[:64, :64], ALU.mult)
                    P.mm(po[:, :64], V[hd][:, blk, :], AM[w][:], start=True, stop=False)
                    P.mm(po[:, :64], Sbf[hd][:], QP[hd][:, cs], start=False, stop=True)
                    P.copy(OH[hd][:, cs], po[:, :64], eng="act")
                    P.transpose(pt[:64, :128], KH[w][:], cx.ident_f[:])
                    P.copy(KHt[w][:], pt[:64, :128], eng="dve")
                    P.mm(pn[:, :128], KHt[w][:], V[hd][:, blk, :])
                    P.stt(S[hd][:], S[hd][:], EG[hd][:, end:end + 1], pn[:, :128], ALU.mult, ALU.add)
                    P.copy(Sbf[hd][:], S[hd][:], eng="act")
            for hd in range(H):
                P.dma("sp", X[2][:, :TT], pfm[R_HGG + hd * 128:R_HGG + (hd + 1) * 128, t0:t0 + TT])
                P.act(X[2][:, :TT], X[2][:, :TT], AF.Silu)
                rmsnorm1(P, cx, OH[hd][:, :TT], gn[:, 0:1], ON[:, :TT], sqs, rstd, TT, 128, EPS, ps[7])
                mx = MX[nm % 2]
                nm += 1
                P.tt(mx[:, :TT], ON[:, :TT], X[2][:, :TT], ALU.mult)
                P.dma("sp", mixT[R_MIX_HG + hd * 128:R_MIX_HG + (hd + 1) * 128, t0:t0 + TT], mx[:, :TT])
        P.flush()


RW_H = 8
CW = -0.6065306597126334
RW_LN_EPS = 64e-5


def rw_phase(cx, pfm, ptm, layer, prm, vfirst, mixT, L):
    nc = cx.nc
    NT = L // 128
    H = RW_H
    with ExitStack() as es:
        cx.uid[0] += 1
        tg = "_%d" % cx.uid[0]
        sb = lambda n, s, d=F32: es.enter_context(nc.sbuf_tensor(n + tg, s, d))
        rwp = sb("r_rwp", [64, 7, 8])
        omka = sb("r_omka", [64, 8])
        lop = sb("r_lop", [128, 8])
        w2s = sb("r_w2", [96, 512])
        a2s = sb("r_a2", [96, 512])
        g2s = sb("r_g2", [128, 2, 512])
        v2s = sb("r_v2", [64, 512])
        tmb = sb("r_tmb", [128, 5, 512])
        m_gt4 = sb("r_mgt4", [128, 4, 128])
        m_lt4 = sb("r_mlt4", [128, 4, 128])
        m_le4 = sb("r_mle4", [128, 4, 128])
        id4 = sb("r_id4", [128, 4, 128])
        rmh = sb("r_rmh", [64, 8, 128])
        ST = sb("r_ST", [64, 8, 64])
        fm = {}
        for n in ("Rc", "Rp", "Kc", "Kp", "Rs", "Ks", "SW", "CS", "EP", "EN", "EX", "A", "KKn", "Bv",
                  "K2", "At", "Bt", "Kt", "Rt", "Bh", "Kh"):
            fm[n] = sb("r_f" + n, [64, 8, 128])
        fm["T0"], fm["T1"], fm["KK0"], fm["CX"] = fm["Rp"], fm["Kp"], fm["Rc"], fm["Kc"]
        lo = {}
        for n in ("WLc", "WLp", "ALc", "ALp"):
            lo[n] = sb("r_l" + n, [96, 128])
        for n in ("GLc", "GLp"):
            lo[n] = sb("r_l" + n, [128, 2, 128])
        for n in ("VLc", "VLp"):
            lo[n] = sb("r_l" + n, [64, 128])
        tm = {}
        for n in ("Vc", "Vp", "V", "VF", "SV", "Gt", "NXZ", "SA", "Y", "YN", "BHt", "KHt"):
            tm[n] = sb("r_t" + n, [128, 512])
        tm["CEN"], tm["SQ"], tm["BON"] = tm["Y"], tm["NXZ"], tm["SA"]
        MS = sb("r_MS", [128, 8])
        VS = sb("r_VS", [128, 8])
        RKS = sb("r_RKS", [128, 8])
        big = {}
        for n in ("M0", "M1", "N0", "N1", "PT", "LAK", "MRB", "MRK"):
            big[n] = sb("r_b" + n, [128, 8, 128])
        OB = [sb("r_OB%d" % i, [128, 4, 128], BF16) for i in range(2)]
        ps = cx.ps
        P = Phase(cx.st)
        bank = [0]

        def nb():
            b = ps[bank[0] % 8]
            bank[0] += 1
            return b

        def b3(p_, h=4):
            return p_[:].rearrange("p (h t) -> p h t", h=h)

        P.dma("sp", rwp[:], prm["rwp"])
        P.dma("sp", lop[:], prm["lop"])
        P.dma("sp", w2s[:], prm["w2"])
        P.dma("sp", a2s[:], prm["a2"])
        P.dma("sp", g2s[:], prm["g2"])
        P.dma("sp", tmb[:], prm["tmb"])
        if layer > 0:
            P.dma("sp", v2s[:], prm["v2"])
        P.ts(omka[:], rwp[:, 5, :], -1.0, 1.0, op0=ALU.mult, op1=ALU.add)
        for j in range(4):
            P.copy(m_gt4[:, j, :], cx.m_gt_f[:])
            P.copy(m_lt4[:, j, :], cx.m_lt_f[:])
            P.copy(m_le4[:, j, :], cx.m_le_f[:])
            P.copy(id4[:, j, :], cx.ident_f[:])
        P.memset(rmh[:], 1.0)
        P.memset(rmh[:, :, 0:1], 0.0)
        P.memset(ST[:], 0.0)

        def bc(ap2, n=128):
            return ap2.unsqueeze(2).to_broadcast([ap2.shape[0], ap2.shape[1], n])

        def fmv(row0):
            return pfm[row0:row0 + 512, :].rearrange("(h k) t -> k h t", k=64)

        rv, kv = fmv(R_RWR), fmv(R_RWK)
        mixv = mixT[R_MIX_RW:R_MIX_RW + 512, :].rearrange("(j p) t -> p j t", p=128)
        hs = lambda h: slice(h * 64, (h + 1) * 64)

        def shift_load(cur, prev, src3, t0, three):
            if three:
                P.dma("sp", cur[:], src3[:, :, t0:t0 + 128])
                if t0 == 0:
                    P.memset(prev[:, :, 0:1], 0.0)
                    P.dma("sp", prev[:, :, 1:128], src3[:, :, 0:127])
                else:
                    P.dma("sp", prev[:], src3[:, :, t0 - 1:t0 + 127])
            else:
                P.dma("sp", cur[:], src3[:, t0:t0 + 128])
                if t0 == 0:
                    P.memset(prev[:, 0:1], 0.0)
                    P.dma("sp", prev[:, 1:128], src3[:, 0:127])
                else:
                    P.dma("sp", prev[:], src3[:, t0 - 1:t0 + 127])

        for c in range(NT):
            t0 = c * 128
            f = fm
            shift_load(f["Rc"], f["Rp"], rv, t0, True)
            shift_load(f["Kc"], f["Kp"], kv, t0, True)
            for (cur, prev, out, mi, en) in ((f["Rc"], f["Rp"], f["Rs"], 0, "dve"), (f["Kc"], f["Kp"], f["Ks"], 1, "dve")):
                P.tt(prev[:], prev[:], cur[:], ALU.subtract, eng=en)
                P.tt(prev[:], prev[:], bc(rwp[:, mi, :]), ALU.mult, eng=en)
                P.tt(out[:], prev[:], cur[:], ALU.add, eng=en)
            shift_load(lo["WLc"], lo["WLp"], pfm[R_WLO:R_WLO + 96, :], t0, False)
            shift_load(lo["ALc"], lo["ALp"], pfm[R_ALO:R_ALO + 96, :], t0, False)
            shift_load(lo["GLc"], lo["GLp"], pfm[R_GLO:R_GLO + 256, :].rearrange("(j p) t -> p j t", p=128), t0, True)
            P.tt(lo["WLp"][:], lo["WLp"][:], lo["WLc"][:], ALU.subtract)
            P.stt(lo["WLc"][:], lo["WLp"][:], lop[:96, 0:1], lo["WLc"][:], ALU.mult, ALU.add)
            P.act(lo["WLc"][:], lo["WLc"][:], AF.Tanh)
            P.tt(lo["ALp"][:], lo["ALp"][:], lo["ALc"][:], ALU.subtract)
            P.stt(lo["ALc"][:], lo["ALp"][:], lop[:96, 1:2], lo["ALc"][:], ALU.mult, ALU.add)
            P.tt(lo["GLp"][:], lo["GLp"][:], lo["GLc"][:], ALU.subtract)
            for j in range(2):
                P.stt(lo["GLc"][:, j, :], lo["GLp"][:, j, :], lop[:, 2 + j:3 + j], lo["GLc"][:, j, :], ALU.mult, ALU.add)
            P.act(lo["GLc"][:], lo["GLc"][:], AF.Sigmoid)
            for (w_s, code, bias_i, out) in ((w2s, lo["WLc"], 2, f["SW"]), (a2s, lo["ALc"], 3, f["A"])):
                for half in range(2):
                    pb = nb()
                    for j in range(4):
                        h = half * 4 + j
                        P.mm(pb[:64, j * 128:(j + 1) * 128], w_s[:, hs(h)], code[:])
                    P.tt(out[:, half * 4:half * 4 + 4, :], b3(pb)[:64], bc(rwp[:, bias_i, half * 4:half * 4 + 4]), ALU.add)
                P.act(out[:], out[:], AF.Sigmoid)
            P.scan(f["CS"][:].rearrange("k h t -> k (h t)"), rmh[:].rearrange("k h t -> k (h t)"),
                   f["SW"][:].rearrange("k h t -> k (h t)"), 0.0, ALU.mult, ALU.add)
            P.tt(f["CX"][:], f["CS"][:], f["SW"][:], ALU.subtract)
            P.act(f["EP"][:], f["CS"][:], AF.Exp, scale=CW)
            P.act(f["EN"][:], f["CS"][:], AF.Exp, scale=-CW)
            P.act(f["EX"][:], f["CX"][:], AF.Exp, scale=CW)
            P.tt(f["KK0"][:], f["Ks"][:], bc(rwp[:, 4, :]), ALU.mult)
            P.tt(f["T0"][:], f["KK0"][:], f["KK0"][:], ALU.mult)
            for half in range(2):
                pb = nb()
                P.mm(pb[:64, :], cx.ones_f[:64, :64], f["T0"][:, half * 4:half * 4 + 4, :].rearrange("k h t -> k (h t)"))
                P.ts(f["T1"][:, half * 4:half * 4 + 4, :], b3(pb)[:64], 1e-16, None, op0=ALU.max)
            P.act(f["T1"][:], f["T1"][:], AF.Ln)
            P.act(f["T1"][:], f["T1"][:], AF.Exp, scale=-0.5)
            P.tt(f["KKn"][:], f["KK0"][:], f["T1"][:], ALU.mult)
            P.tt(f["Bv"][:], f["KKn"][:], f["A"][:], ALU.mult)
            P.tt(f["T0"][:], f["A"][:], bc(rwp[:, 5, :]), ALU.mult)
            P.tt(f["T0"][:], f["T0"][:], bc(omka[:]), ALU.add)
            P.tt(f["K2"][:], f["Ks"][:], f["T0"][:], ALU.mult)
            P.tt(f["At"][:], f["KKn"][:], f["EX"][:], ALU.mult)
            P.tt(f["Bt"][:], f["Bv"][:], f["EN"][:], ALU.mult)
            P.tt(f["Kt"][:], f["K2"][:], f["EN"][:], ALU.mult)
            P.tt(f["Rt"][:], f["Rs"][:], f["EP"][:], ALU.mult)
            eg = f["EP"][:, :, 127:128].to_broadcast([64, 8, 128])
            P.tt(f["Bh"][:], f["Bt"][:], eg, ALU.mult)
            P.tt(f["Kh"][:], f["Kt"][:], eg, ALU.mult)
            P.tt(f["T0"][:], f["Rs"][:], f["K2"][:], ALU.mult)
            P.tt(f["T0"][:], f["T0"][:], bc(rwp[:, 6, :]), ALU.mult)
            pb = nb()
            for h in range(H):
                P.mm(pb[:, h:h + 1], f["T0"][:, h, :], cx.ones_f[:64, 0:1])
            P.copy(RKS[:], pb[:, 0:8], eng="act")
            t = tm
            P.dma("sp", t["Vc"][:], ptm[t0:t0 + 128, C_RWV:C_RWV + 512])
            if t0 == 0:
                P.memset(t["Vp"][0:1, :], 0.0)
                P.dma("sp", t["Vp"][1:128, :], ptm[0:127, C_RWV:C_RWV + 512])
            else:
                P.dma("sp", t["Vp"][:], ptm[t0 - 1:t0 + 127, C_RWV:C_RWV + 512])
            P.tt(t["Vp"][:], t["Vp"][:], t["Vc"][:], ALU.subtract)
            P.tt(t["Vp"][:], t["Vp"][:], tmb[:, 0, :], ALU.mult)
            if layer == 0:
                P.tt(t["V"][:], t["Vp"][:], t["Vc"][:], ALU.add)
                P.dma("sp", vfirst[t0:t0 + 128, :], t["V"][:])
            else:
                P.tt(t["Vc"][:], t["Vp"][:], t["Vc"][:], ALU.add)
                shift_load(lo["VLc"], lo["VLp"], pfm[R_VLO:R_VLO + 64, :], t0, False)
                P.tt(lo["VLp"][:], lo["VLp"][:], lo["VLc"][:], ALU.subtract)
                P.stt(lo["VLc"][:], lo["VLp"][:], lop[:64, 4:5], lo["VLc"][:], ALU.mult, ALU.add)
                pb = nb()
                P.mm(pb[:, :], lo["VLc"][:], v2s[:])
                P.tt(t["SV"][:], pb[:, :], tmb[:, 1, :], ALU.add)
                P.act(t["SV"][:], t["SV"][:], AF.Sigmoid)
                P.dma("sp", t["VF"][:], vfirst[t0:t0 + 128, :])
                P.tt(t["VF"][:], t["VF"][:], t["Vc"][:], ALU.subtract)
                P.tt(t["VF"][:], t["VF"][:], t["SV"][:], ALU.mult)
                P.tt(t["V"][:], t["VF"][:], t["Vc"][:], ALU.add)
            pb = nb()
            for j in range(2):
                P.mm(pb[:, :], lo["GLc"][:, j, :], g2s[:, j, :], start=(j == 0), stop=(j == 1))
            P.copy(t["Gt"][:], pb[:, :], eng="act")
            M, N, PT = big["M0"], big["N0"], big["PT"]
            M2, N2 = big["M1"], big["N1"]
            for half in range(2):
                pa, pb = nb(), nb()
                for j in range(4):
                    h = half * 4 + j
                    P.mm(pa[:, j * 128:(j + 1) * 128], f["At"][:, h, :], f["Bt"][:, h, :])
                    P.mm(pb[:, j * 128:(j + 1) * 128], f["Bt"][:, h, :], f["At"][:, h, :])
                hh = slice(half * 4, half * 4 + 4)
                P.stt(M[:, hh, :], b3(pa), -1.0, m_gt4[:], ALU.mult, ALU.mult)
                P.stt(N[:, hh, :], b3(pb), -1.0, m_lt4[:], ALU.mult, ALU.mult)
                P.tt(PT[:, hh, :], N[:, hh, :], id4[:], ALU.add, eng="pool")
            for step in range(6):
                last = step == 5
                for half in range(2):
                    hh = slice(half * 4, half * 4 + 4)
                    pa = nb()
                    for j in range(4):
                        h = half * 4 + j
                        P.mm(pa[:, j * 128:(j + 1) * 128], N[:, h, :], M[:, h, :])
                    P.copy(M2[:, hh, :], b3(pa), eng="act")
                    if not last:
                        pb = nb()
                        for j in range(4):
                            h = half * 4 + j
                            P.mm(pb[:, j * 128:(j + 1) * 128], M[:, h, :], N[:, h, :])
                        P.copy(N2[:, hh, :], b3(pb), eng="dve")
                    pc = nb()
                    for j in range(4):
                        h = half * 4 + j
                        P.mm(pc[:, j * 128:(j + 1) * 128], M2[:, h, :], PT[:, h, :])
                    P.tt(PT[:, hh, :], b3(pc), PT[:, hh, :], ALU.add)
                M, M2 = M2, M
                N, N2 = N2, N
            for (dst, lt, rt, msk) in ((big["LAK"], f["Kt"], f["At"], m_lt4), (big["MRB"], f["Bt"], f["Rt"], m_le4),
                                       (big["MRK"], f["Kt"], f["Rt"], m_le4)):
                for half in range(2):
                    pa = nb()
                    for j in range(4):
                        h = half * 4 + j
                        P.mm(pa[:, j * 128:(j + 1) * 128], lt[:, h, :], rt[:, h, :])
                    P.tt(dst[:, half * 4:half * 4 + 4, :], b3(pa), msk[:], ALU.mult)
            for (src, dst) in ((f["Bh"], t["BHt"]), (f["Kh"], t["KHt"])):
                pa = nb()
                for h in range(H):
                    P.transpose(pa[:, hs(h)], src[:, h, :], cx.ident_f[:64, :64])
                P.copy(dst[:], pa[:, :], eng="act")
            pa = nb()
            for h in range(H):
                P.mm(pa[:, hs(h)], big["LAK"][:, h, :], t["V"][:, hs(h)], start=True, stop=False)
                P.mm(pa[:, hs(h)], f["At"][:, h, :], ST[:, h, :], start=False, stop=True)
            P.ts(t["NXZ"][:], pa[:, :], -1.0, None, op0=ALU.mult)
            pa = nb()
            for h in range(H):
                P.mm(pa[:, hs(h)], PT[:, h, :], t["NXZ"][:, hs(h)])
            P.copy(t["SA"][:], pa[:, :], eng="act")
            pa = nb()
            for h in range(H):
                P.mm(pa[:, hs(h)], f["Rt"][:, h, :], ST[:, h, :], start=True, stop=False)
                P.mm(pa[:, hs(h)], big["MRB"][:, h, :], t["SA"][:, hs(h)], start=False, stop=False)
                P.mm(pa[:, hs(h)], big["MRK"][:, h, :], t["V"][:, hs(h)], start=False, stop=True)
            P.copy(t["Y"][:], pa[:, :], eng="act")
            pa = nb()
            for h in range(H):
                P.mm(pa[:64, hs(h)], t["BHt"][:, hs(h)], t["SA"][:, hs(h)], start=True, stop=False)
                P.mm(pa[:64, hs(h)], t["KHt"][:, hs(h)], t["V"][:, hs(h)], start=False, stop=True)
            P.tt(ST[:], ST[:], f["EP"][:, :, 127:128].to_broadcast([64, 8, 64]), ALU.mult)
            P.tt(ST[:], ST[:], pa[:64, :].rearrange("k (h v) -> k h v", h=8), ALU.add)
            Y3 = t["Y"][:].rearrange("p (h v) -> p h v", h=8)
            C3 = t["CEN"][:].rearrange("p (h v) -> p h v", h=8)
            S3 = t["SQ"][:].rearrange("p (h v) -> p h v", h=8)
            N3 = t["YN"][:].rearrange("p (h v) -> p h v", h=8)
            V3 = t["V"][:].rearrange("p (h v) -> p h v", h=8)
            B3 = t["BON"][:].rearrange("p (h v) -> p h v", h=8)
            P.add("dve", lambda e: e.tensor_reduce(MS[:], Y3, AX.X, ALU.add), [t["Y"][:]], [MS[:]])
            P.ts(MS[:], MS[:], 1.0 / 64, None, op0=ALU.mult)
            P.tt(C3, Y3, MS[:].unsqueeze(2).to_broadcast([128, 8, 64]), ALU.subtract)
            P.tt(S3, C3, C3, ALU.mult, eng="pool")
            P.add("dve", lambda e: e.tensor_reduce(VS[:], S3, AX.X, ALU.add), [t["SQ"][:]], [VS[:]])
            P.act(VS[:], VS[:], AF.Ln, bias=RW_LN_EPS, scale=1.0 / 64)
            P.act(VS[:], VS[:], AF.Exp, scale=-0.5)
            P.tt(N3, C3, VS[:].unsqueeze(2).to_broadcast([128, 8, 64]), ALU.mult)
            P.tt(t["YN"][:], t["YN"][:], tmb[:, 2, :], ALU.mult)
            P.tt(t["YN"][:], t["YN"][:], tmb[:, 3, :], ALU.add)
            P.tt(B3, V3, RKS[:].unsqueeze(2).to_broadcast([128, 8, 64]), ALU.mult, eng="pool")
            P.tt(t["YN"][:], t["YN"][:], t["BON"][:], ALU.add)
            P.tt(t["YN"][:], t["YN"][:], t["Gt"][:], ALU.mult)
            pa = nb()
            for j in range(4):
                P.transpose(pa[:, j * 128:(j + 1) * 128], t["YN"][:, j * 128:(j + 1) * 128], cx.ident_f[:])
            ob = OB[c % 2]
            P.copy(ob[:], b3(pa), eng="act")
            P.dma("sp", mixv[:, :, t0:t0 + 128], ob[:])
        P.flush()


from concourse.bass_utils import run_bass_kernel_spmd

DEPTH = 4
N_META = 16
RW_SHIFT_N = 1984


def build_program(L, depth=DEPTH, Lr=None):
    nc = bass.Bass("TRN2", target_bir_lowering=False)
    dt = lambda n, s, k="ExternalInput", d=F32: nc.dram_tensor(n, s, d, kind=k).ap()
    h0 = dt("h0", [D, L])
    gains = dt("gains", [depth, 3, 128, DC])
    f1wi = dt("ffn1_wi", [depth, FC, 128, DC, 256])
    f1wo = dt("ffn1_wo", [depth, DC, 128, FC, 128])
    f2wi = dt("ffn2_wi", [depth, FC, 128, DC, 256])
    f2wo = dt("ffn2_wo", [depth, DC, 128, FC, 128])
    w_in = dt("w_in", [depth, 14, 128, DC, 512])
    w_in_v = dt("w_in_v", [depth - 1, 128, DC, 64]) if depth > 1 else None
    w_out = dt("w_out", [depth, 4, 128, DC, 512])
    hglb = dt("hglb", [128, 4, 4])
    hgn = dt("hgn", [depth, 128, 1])
    sbg = dt("sbg", [depth, 128, 3])
    rwp = dt("rwp", [depth, 64, 7, 8])
    lop = dt("lop", [depth, 128, 8])
    w2 = dt("rw_w2", [depth, 96, 512])
    a2 = dt("rw_a2", [depth, 96, 512])
    g2 = dt("rw_g2", [depth, 128, 2, 512])
    v2 = dt("rw_v2", [depth, 64, 512])
    tmb = dt("tmb", [depth, 128, 5, 512])
    hT = dt("hT", [D, L], "ExternalOutput")
    pfm = dt("pfm", [NFM, L], "Internal")
    ptm = dt("ptm", [L, NTM], "Internal")
    vfirst = dt("vfirst", [L, 512], "Internal")
    mixT = dt("mixT", [D, L], "Internal", BF16)
    with ExitStack() as es:
        cx = make_ctx(nc, es)
        cx.eps_ap = lambda e: float(e)
        init_consts(cx)
        make_masks(cx, es)
        make_lb(cx, es, hglb)
        P = Phase(cx.st)
        P.dma("sp", hT, h0)
        P.flush()
        for l in range(depth):
            ffn_phase(cx, hT, f1wi[l], f1wo[l], gains[l, 0], L, Lr=Lr)
            proj_phase(cx, hT, w_in[l], (w_in_v[l - 1] if l > 0 else None), gains[l, 1], pfm, ptm, L)
            hg_phase(cx, pfm, ptm, l, hgn[l], mixT, L)
            sb_phase(cx, pfm, ptm, sbg[l], mixT, L)
            prm = {"rwp": rwp[l], "lop": lop[l], "w2": w2[l], "a2": a2[l], "g2": g2[l], "v2": v2[l], "tmb": tmb[l]}
            rw_phase(cx, pfm, ptm, l, prm, vfirst, mixT, L)
            ffn_phase(cx, hT, f2wi[l], f2wo[l], gains[l, 2], L, pre=(mixT, w_out[l]), Lr=Lr)
    return nc, cx.st.nops


def _c(a):
    return np.ascontiguousarray(a, dtype=np.float32)


def layout_params(inp, depth=DEPTH):
    fm16 = lambda v: v.reshape(DC, 128).T
    fmh = lambda v: v.reshape(8, 64).T
    out = {}
    out["gains"] = _c(np.stack([np.stack([fm16(inp[k][l]) for k in ("norm_ffn1", "norm_mix", "norm_ffn2")]) for l in range(depth)]))
    for k in ("ffn1_wi", "ffn2_wi"):
        out[k] = _c(inp[k][:depth].reshape(depth, DC, 128, FC, 256).transpose(0, 3, 2, 1, 4))
    for k in ("ffn1_wo", "ffn2_wo"):
        out[k] = _c(inp[k][:depth].reshape(depth, FC, 128, DC, 128).transpose(0, 3, 2, 1, 4))
    wpad = np.zeros((depth, D, 7168), np.float32)
    wpad[:, :, :7104] = inp["w_in"][:depth]
    out["w_in"] = _c(wpad.reshape(depth, DC, 128, 14, 512).transpose(0, 3, 2, 1, 4))
    out["w_out"] = _c(inp["w_out"][:depth].reshape(depth, DC, 128, 4, 512).transpose(0, 3, 2, 1, 4))
    if depth > 1:
        out["w_in_v"] = _c(inp["w_in_v"][:depth - 1].reshape(depth - 1, DC, 128, 64).transpose(0, 2, 1, 3))
    out["hglb"] = _c(inp["hg_lb"].reshape(4, 4, 128).transpose(2, 1, 0))
    out["hgn"] = _c(inp["hg_norm"][:depth].reshape(depth, 128, 1))
    out["sbg"] = _c(np.stack([np.stack([inp["sb_qn"][l], inp["sb_kn"][l], inp["sb_on"][l]], axis=1) for l in range(depth)]))
    rwp, lop, v2, tmb = [], [], [], []
    for l in range(depth):
        mu = inp["rw_mu"][l]
        rwp.append(np.stack([fmh(mu[0:512]), fmh(mu[512:1024]), fmh(inp["rw_w0"][l]), fmh(inp["rw_a0"][l]), fmh(inp["rw_kk"][l]),
                             fmh(inp["rw_ka"][l]), fmh(inp["rw_rk"][l].reshape(-1))], axis=1))
        lp = np.zeros((128, 8), np.float32)
        lp[:96, 0] = mu[1536:1632]
        lp[:96, 1] = mu[1632:1728]
        lp[:, 2] = mu[1728:1856]
        lp[:, 3] = mu[1856:1984]
        tb = np.zeros((128, 5, 512), np.float32)
        tb[:, 0] = mu[1024:1536][None]
        tb[:, 2] = inp["rw_ln_w"][l][None]
        tb[:, 3] = inp["rw_ln_b"][l][None]
        vv = np.zeros((64, 512), np.float32)
        if l > 0:
            lp[:64, 4] = inp["rw_mu_v"][l - 1]
            tb[:, 1] = inp["rw_v0"][l - 1][None]
            vv = inp["rw_v2"][l - 1]
        lop.append(lp)
        tmb.append(tb)
        v2.append(vv)
    out["rwp"] = _c(np.stack(rwp))
    out["lop"] = _c(np.stack(lop))
    out["tmb"] = _c(np.stack(tmb))
    out["rw_v2"] = _c(np.stack(v2))
    out["rw_w2"] = _c(inp["rw_w2"][:depth])
    out["rw_a2"] = _c(inp["rw_a2"][:depth])
    out["rw_g2"] = _c(np.stack([inp["rw_g2"][l].reshape(2, 128, 512).transpose(1, 0, 2) for l in range(depth)]))
    return out


def kernel(**inputs):
    inp = {k: np.asarray(v) for k, v in inputs.items()}
    x = inp["x"]
    B, S, _ = x.shape
    depth = inp["norm_ffn1"].shape[0]
    L_real = N_META + S
    L = ((L_real + 127) // 128) * 128
    nc, nops = build_program(L, depth, L_real)
    shared = layout_params(inp, depth)
    n_cores = 8 if B == 4 else B
    in_maps = []
    for c in range(n_cores):
        b = c % B
        h0 = np.zeros((L, D), np.float32)
        h0[:N_META] = inp["meta"]
        h0[N_META:L_real] = x[b]
        m = dict(shared)
        m["h0"] = _c(h0.T)
        in_maps.append(m)
    res = run_bass_kernel_spmd(nc, in_maps, core_ids=list(range(n_cores)))
    out = np.stack([np.ascontiguousarray(res.results[b]["hT"].T[N_META:L_real]) for b in range(B)])
    return out.astype(np.float32)
```

```python
from contextlib import ExitStack
import math
import numpy as np
import concourse.bass as bass
import concourse.mybir as mybir

F32 = mybir.dt.float32
BF16 = mybir.dt.bfloat16
AF = mybir.ActivationFunctionType
ALU = mybir.AluOpType
AX = mybir.AxisListType

ENGS = ("pe", "act", "dve", "pool", "sp")
NDMA = 12


def region(ap):
    t = ap.tensor
    shp = list(t.shape)
    rowlen = 1
    for s in shp[1:]:
        rowlen *= s
    off = int(ap.offset)
    r0 = off // rowlen
    c0 = off % rowlen
    rext = 0
    cext = 0
    for step, cnt in ap.ap:
        step = abs(int(step))
        cnt = int(cnt)
        if cnt <= 1 or step == 0:
            continue
        if step >= rowlen and step % rowlen == 0:
            rext += (cnt - 1) * (step // rowlen)
        else:
            cext += (cnt - 1) * step
    c1 = c0 + cext + 1
    if c1 > rowlen:
        extra = (c1 - 1) // rowlen
        rext += extra
        c0, c1 = 0, rowlen
    return (ap.name, r0, r0 + rext + 1, c0, c1)


class State:
    def __init__(self, nc, es):
        self.nc = nc
        self.sem = {}
        self.cnt = {}
        for e in ENGS:
            self.sem[e] = es.enter_context(nc.semaphore("s_" + e))
            self.cnt[e] = 0
        self.dsem = {}
        self.dcnt = {}
        self.dnext = {}
        for q in ("sp", "pool", "act"):
            self.dsem[q] = [es.enter_context(nc.semaphore("d_%s%d" % (q, i))) for i in range(NDMA)]
            self.dcnt[q] = [0] * NDMA
            self.dnext[q] = 0
        self.waited = {e: {} for e in ENGS}
        self.nops = 0


class Phase:
    def __init__(self, st):
        self.st = st
        self.nc = st.nc
        self.ops = {e: [] for e in ENGS}
        self.recs = {}
        self.order = 0

    def _deps(self, reads, writes):
        deps = {}

        def add(done):
            k = done[0]
            if k not in deps or deps[k][2] < done[2]:
                deps[k] = done

        for ap in reads:
            nm, r0, r1, c0, c1 = region(ap)
            for rec in self.recs.get(nm, ()):
                if rec[5] and rec[0] < r1 and r0 < rec[1] and rec[2] < c1 and c0 < rec[3]:
                    add(rec[4])
        for ap in writes:
            nm, r0, r1, c0, c1 = region(ap)
            for rec in self.recs.get(nm, ()):
                if rec[0] < r1 and r0 < rec[1] and rec[2] < c1 and c0 < rec[3]:
                    add(rec[4])
        return deps

    def _record(self, reads, writes, done):
        for ap in writes:
            nm, r0, r1, c0, c1 = region(ap)
            lst = self.recs.setdefault(nm, [])
            lst[:] = [rc for rc in lst if not (r0 <= rc[0] and rc[1] <= r1 and c0 <= rc[2] and rc[3] <= c1)]
            lst.append([r0, r1, c0, c1, done, True])
        for ap in reads:
            nm, r0, r1, c0, c1 = region(ap)
            lst = self.recs.setdefault(nm, [])
            lst[:] = [rc for rc in lst if not ((not rc[5]) and rc[4][0] == done[0] and r0 <= rc[0] and rc[1] <= r1 and c0 <= rc[2] and rc[3] <= c1)]
            lst.append([r0, r1, c0, c1, done, False])

    def add(self, eng, fn, reads, writes, pe_skip=True):
        st = self.st
        deps = self._deps(reads, writes)
        st.cnt[eng] += 1
        done = ("e_" + eng, st.sem[eng], st.cnt[eng])
        waits = []
        for k, d in deps.items():
            if eng == "pe" and k == "e_pe":
                continue
            if st.waited[eng].get(k, 0) >= d[2]:
                continue
            st.waited[eng][k] = d[2]
            waits.append((d[1], d[2]))
        self.ops[eng].append((fn, waits, (st.sem[eng], 1)))
        self._record(reads, writes, done)
        st.nops += 1

    def dma(self, q, out, in_, **kw):
        st = self.st
        reads, writes = [in_], [out]
        deps = self._deps(reads, writes)
        i = st.dnext[q]
        st.dnext[q] = (i + 1) % NDMA
        sem = st.dsem[q][i]
        key = "d_%s%d" % (q, i)
        waits = []
        if st.dcnt[q][i] > 0 and st.waited[q].get(key, 0) < st.dcnt[q][i]:
            waits.append((sem, st.dcnt[q][i]))
            st.waited[q][key] = st.dcnt[q][i]
        st.dcnt[q][i] += 16
        done = (key, sem, st.dcnt[q][i])
        for k, d in deps.items():
            if st.waited[q].get(k, 0) >= d[2]:
                continue
            st.waited[q][k] = d[2]
            waits.append((d[1], d[2]))
        self.ops[q].append((lambda e: e.dma_start(out=out, in_=in_, **kw), waits, (sem, 16)))
        self._record(reads, writes, done)
        st.nops += 1

    def mm(self, out, lhsT, rhs, start=True, stop=True, sgc=False):
        if sgc:
            self.add("pe", lambda e: e.matmul(out, lhsT, rhs, start=start, stop=stop, skip_group_check=True), [lhsT, rhs], [out])
        else:
            self.add("pe", lambda e: e.matmul(out, lhsT, rhs, start=start, stop=stop), [lhsT, rhs], [out])

    def transpose(self, out, in_, ident):
        self.add("pe", lambda e: e.transpose(out, in_, ident), [in_, ident], [out])

    def act(self, out, in_, func, bias=None, scale=1.0, eng="act"):
        rd = [in_]
        kw = {}
        if bias is not None:
            kw["bias"] = bias
            if not isinstance(bias, (int, float)):
                rd.append(bias)
        if not isinstance(scale, (int, float)):
            rd.append(scale)
        self.add("act", lambda e: e.activation(out, in_, func, scale=scale, **kw), rd, [out])

    def tt(self, out, in0, in1, op, eng="dve"):
        self.add(eng, lambda e: e.tensor_tensor(out, in0, in1, op), [in0, in1], [out])

    def ts(self, out, in0, s1, s2=None, op0=ALU.mult, op1=ALU.bypass, eng="dve"):
        rd = [in0]
        for s in (s1, s2):
            if s is not None and not isinstance(s, (int, float)):
                rd.append(s)
        if s2 is None:
            self.add(eng, lambda e: e.tensor_scalar(out, in0, s1, None, op0), rd, [out])
        else:
            self.add(eng, lambda e: e.tensor_scalar(out, in0, s1, s2, op0, op1), rd, [out])

    def stt(self, out, in0, scalar, in1, op0, op1):
        rd = [in0, in1]
        if not isinstance(scalar, (int, float)):
            rd.append(scalar)
        self.add("dve", lambda e: e.scalar_tensor_tensor(out, in0, scalar, in1, op0, op1), rd, [out])

    def copy(self, out, in_, eng="dve"):
        if eng == "act":
            self.add("act", lambda e: e.copy(out, in_), [in_], [out])
        else:
            self.add(eng, lambda e: e.tensor_copy(out, in_), [in_], [out])

    def memset(self, out, val, eng="dve"):
        self.add(eng, lambda e: e.memset(out, val), [], [out])

    def recip(self, out, in_):
        self.add("dve", lambda e: e.reciprocal(out, in_), [in_], [out])

    def scan(self, out, d0, d1, init, op0, op1):
        rd = [d0, d1]
        if not isinstance(init, (int, float)):
            rd.append(init)
        self.add("dve", lambda e: e.tensor_tensor_scan(out, d0, d1, init, op0, op1), rd, [out])

    def flush(self, final=False):
        st = self.st
        nc = self.nc
        finals = []
        for e in ENGS:
            if st.cnt[e] > 0:
                finals.append(("e_" + e, st.sem[e], st.cnt[e]))
        for q in st.dsem:
            for i in range(NDMA):
                if st.dcnt[q][i] > 0:
                    finals.append(("d_%s%d" % (q, i), st.dsem[q][i], st.dcnt[q][i]))
        ops = self.ops
        emap = {"pe": "tensor", "act": "scalar", "dve": "vector", "pool": "gpsimd", "sp": "sync"}

        def mk(ename):
            def body(eng):
                for fn, waits, inc in ops[ename]:
                    for s, v in waits:
                        eng.wait_ge(s, v)
                    ins = fn(eng)
                    ins.then_inc(inc[0], inc[1])
                for k, s, v in finals:
                    if st.waited[ename].get(k, 0) >= v:
                        continue
                    st.waited[ename][k] = v
                    eng.wait_ge(s, v)
            return body

        with nc.Block() as block:
            for ename in ENGS:
                getattr(block, emap[ename])(mk(ename))
        self.ops = {e: [] for e in ENGS}
        self.recs = {}


def _coll(self, in_ap, out_ap, groups):
    st = self.st
    q = "pool"
    reads, writes = [in_ap], [out_ap]
    deps = self._deps(reads, writes)
    i = st.dnext[q]
    st.dnext[q] = (i + 1) % NDMA
    sem = st.dsem[q][i]
    key = "d_%s%d" % (q, i)
    waits = []
    if st.dcnt[q][i] > 0 and st.waited[q].get(key, 0) < st.dcnt[q][i]:
        waits.append((sem, st.dcnt[q][i]))
        st.waited[q][key] = st.dcnt[q][i]
    st.dcnt[q][i] += 16
    done = (key, sem, st.dcnt[q][i])
    for k, d in deps.items():
        if st.waited[q].get(k, 0) >= d[2]:
            continue
        st.waited[q][k] = d[2]
        waits.append((d[1], d[2]))
    self.ops[q].append((lambda e: e.collective_compute("AllGather", ALU.bypass, replica_groups=groups, ins=[in_ap], outs=[out_ap]), waits, (sem, 16)))
    self._record(reads, writes, done)
    st.nops += 1


Phase.allgather = _coll


D = 2048
DC = 16
DFF = 5632
FC = 44
EPS = 1e-6


def tiles_of(L, TT=512):
    out = []
    t = 0
    while t < L:
        n = min(TT, L - t)
        out.append((t, n))
        t += n
    return out


class Ctx:
    pass


def make_ctx(nc, es):
    cx = Ctx()
    cx.nc = nc
    cx.uid = [0]
    cx.st = State(nc, es)
    cx.ps = [es.enter_context(nc.psum_tensor("ps%d" % i, [128, 512], F32)) for i in range(8)]
    cx.ones_bf = es.enter_context(nc.sbuf_tensor("ones_bf", [128, 128], BF16))
    cx.ones_f = es.enter_context(nc.sbuf_tensor("ones_f", [128, 128], F32))
    cx.ident_f = es.enter_context(nc.sbuf_tensor("ident_f", [128, 128], F32))
    return cx


def init_consts(cx):
    P = Phase(cx.st)
    P.memset(cx.ones_bf[:], 1.0)
    P.memset(cx.ones_f[:], 1.0)
    nc = cx.nc
    P.add("pool", lambda e: e.affine_select(cx.ident_f[:], cx.ones_f[:], pattern=[[1, 128]], compare_op=ALU.is_equal,
                                             fill=0.0, base=0, channel_multiplier=-1), [cx.ones_f[:]], [cx.ident_f[:]])
    P.flush()


def rmsnorm_fm(P, cx, h, g, u, sq, rstd, ncn, TT, dn, eps, psb, extra_scale=1.0):
    for c in range(ncn):
        P.act(sq[:, c, :TT], h[:, c, :TT], AF.Square)
    for c in range(ncn):
        P.mm(psb[:, :TT], cx.ones_bf[:], sq[:, c, :TT], start=(c == 0), stop=(c == ncn - 1))
    P.act(rstd[:, :TT], psb[:, :TT], AF.Sqrt, bias=cx.eps_ap(eps), scale=1.0 / dn)
    P.recip(rstd[:, :TT], rstd[:, :TT])
    if extra_scale != 1.0:
        P.ts(rstd[:, :TT], rstd[:, :TT], float(extra_scale), None, op0=ALU.mult)
    for c in range(ncn):
        P.stt(u[:, c, :TT], h[:, c, :TT], g[:, c:c + 1], rstd[:, :TT], ALU.mult, ALU.mult)


def ffn_phase(cx, hT, wi, wo, gvec, L, pre=None, Lr=None):
    nc = cx.nc
    with ExitStack() as es:
        cx.uid[0] += 1
        tg = "_%d" % cx.uid[0]
        sb = lambda n, s, d: es.enter_context(nc.sbuf_tensor(n + tg, s, d))
        h = sb("f_h", [128, DC, 512], F32)
        u = sb("f_u", [128, DC, 512], BF16)
        hid = sb("f_hid", [128, FC, 512], BF16)
        rstd = sb("f_rstd", [128, 512], F32)
        g = sb("f_g", [128, DC], F32)
        wg = [sb("f_wg%d" % i, [128, DC, 256], BF16) for i in range(2)]
        wu = [sb("f_wu%d" % i, [128, DC, 256], BF16) for i in range(2)]
        wos = [sb("f_wo%d" % i, [128, FC, 128], BF16) for i in range(2)]
        tmp = [sb("f_tmp%d" % i, [128, 512], F32) for i in range(2)]
        P = Phase(cx.st)
        P.dma("sp", g[:], gvec)
        hv = hT.rearrange("(c p) t -> p c t", p=128)
        ps = cx.ps
        for (t0, TT) in tiles_of(L):
            P.dma("sp", h[:, :, :TT], hv[:, :, t0:t0 + TT])
            if pre is not None:
                mixT, w_out = pre
                mv = mixT.rearrange("(c p) t -> p c t", p=128)
                P.dma("sp", u[:, :, :TT], mv[:, :, t0:t0 + TT])
                for ds in range(4):
                    slab = hid[:, (ds % 2) * 16:(ds % 2) * 16 + 16, :]
                    P.dma("pool", slab, w_out[ds])
                    for dj in range(4):
                        dc = ds * 4 + dj
                        po = ps[4 + dc % 2]
                        for mc in range(DC):
                            P.mm(po[:, :TT], slab[:, mc, dj * 128:(dj + 1) * 128], u[:, mc, :TT], start=(mc == 0), stop=(mc == DC - 1))
                        P.tt(h[:, dc, :TT], po[:, :TT], h[:, dc, :TT], ALU.add)
            rmsnorm_fm(P, cx, h, g, u, hid, rstd, DC, TT, D, EPS, ps[7])
            k = 0
            for j2 in range(FC // 2):
                b = j2 % 2
                P.dma("pool", wg[b][:], wi[j2])
                P.dma("pool", wu[b][:], wi[FC // 2 + j2])
                for jj in range(2):
                    j = 2 * j2 + jj
                    pa = ps[(k % 2) * 2]
                    pb = ps[(k % 2) * 2 + 1]
                    tm = tmp[k % 2]
                    k += 1
                    for c in range(DC):
                        P.mm(pa[:, :TT], wg[b][:, c, jj * 128:(jj + 1) * 128], u[:, c, :TT], start=(c == 0), stop=(c == DC - 1))
                    for c in range(DC):
                        P.mm(pb[:, :TT], wu[b][:, c, jj * 128:(jj + 1) * 128], u[:, c, :TT], start=(c == 0), stop=(c == DC - 1))
                    P.act(tm[:, :TT], pa[:, :TT], AF.Silu)
                    P.tt(hid[:, j, :TT], tm[:, :TT], pb[:, :TT], ALU.mult)
            for dc in range(DC):
                b = dc % 2
                P.dma("pool", wos[b][:], wo[dc])
                po = ps[4 + b]
                for j in range(FC):
                    P.mm(po[:, :TT], wos[b][:, j, :], hid[:, j, :TT], start=(j == 0), stop=(j == FC - 1))
                P.stt(h[:, dc, :TT], po[:, :TT], 0.5, h[:, dc, :TT], ALU.mult, ALU.add)
            if Lr is not None and t0 + TT > Lr:
                P.memset(h[:, :, max(0, Lr - t0):TT], 0.0)
            P.dma("sp", hv[:, :, t0:t0 + TT], h[:, :, :TT])
        P.flush()


R_HGQ, R_HGF, R_HGG = 0, 512, 1024
R_SBQ, R_SBK = 1536, 2560
R_RWR, R_RWK = 3584, 4096
R_WLO, R_ALO, R_GLO, R_VLO = 4608, 4736, 4864, 5120
NFM = 5248
C_HGI, C_SBV, C_RWV = 0, 512, 1536
NTM = 2048

SLABS = [
    (0, 512, "fm", R_HGQ), (512, 512, "fm", R_HGF), (1536, 512, "fm", R_HGG),
    (2048, 512, "fm", R_SBQ), (2560, 512, "fm", R_SBQ + 512),
    (3072, 512, "fm", R_SBK), (3584, 512, "fm", R_SBK + 512),
    (5120, 512, "fm", R_RWR), (5632, 512, "fm", R_RWK),
    (6656, 448, "lo", 0),
    (1024, 512, "tm", C_HGI), (4096, 512, "tm", C_SBV), (4608, 512, "tm", C_SBV + 512), (6144, 512, "tm", C_RWV),
]


def proj_phase(cx, hT, w_in, w_in_v, gvec, pfm, ptm, L):
    nc = cx.nc
    with ExitStack() as es:
        cx.uid[0] += 1
        tg = "_%d" % cx.uid[0]
        sb = lambda n, s, d: es.enter_context(nc.sbuf_tensor(n + tg, s, d))
        h = sb("p_h", [128, DC, 512], F32)
        u = sb("p_u", [128, DC, 512], BF16)
        sq = sb("p_sq", [128, DC, 512], BF16)
        rstd = sb("p_rstd", [128, 512], F32)
        g = sb("p_g", [128, DC], F32)
        ws = [sb("p_w%d" % i, [128, DC, 512], BF16) for i in range(2)]
        wv = sb("p_wv", [128, DC, 64], BF16)
        stg = [sb("p_stg%d" % i, [128, 512], F32) for i in range(4)]
        P = Phase(cx.st)
        P.dma("sp", g[:], gvec)
        hv = hT.rearrange("(c p) t -> p c t", p=128)
        if w_in_v is not None:
            P.dma("pool", wv[:], w_in_v)
        ps = cx.ps
        k = 0
        nslab = 0
        for (t0, TT) in tiles_of(L):
            P.dma("sp", h[:, :, :TT], hv[:, :, t0:t0 + TT])
            rmsnorm_fm(P, cx, h, g, u, sq, rstd, DC, TT, D, EPS, ps[7])

            def fm_chunk(wt, cs, M, drow):
                nonlocal k
                pb = ps[k % 4]
                sg = stg[k % 4]
                for c in range(DC):
                    P.mm(pb[:M, :TT], wt[:, c, cs:cs + M], u[:, c, :TT], start=(c == 0), stop=(c == DC - 1))
                if k % 2 == 0:
                    P.copy(sg[:M, :TT], pb[:M, :TT], eng="act")
                else:
                    P.copy(sg[:M, :TT], pb[:M, :TT], eng="dve")
                P.dma("sp", pfm[drow:drow + M, t0:t0 + TT], sg[:M, :TT])
                k += 1

            for (c0, ncol, kind, dst) in SLABS:
                wt = ws[nslab % 2]
                nslab += 1
                P.dma("pool", wt[:], w_in[c0 // 512])
                if kind == "fm":
                    for j in range(ncol // 128):
                        fm_chunk(wt, j * 128, 128, dst + j * 128)
                elif kind == "lo":
                    fm_chunk(wt, 0, 96, R_WLO)
                    fm_chunk(wt, 96, 96, R_ALO)
                    fm_chunk(wt, 192, 128, R_GLO)
                    fm_chunk(wt, 320, 128, R_GLO + 128)
                else:
                    for tb in range(TT // 128):
                        pb = ps[k % 4]
                        sg = stg[k % 4]
                        for c in range(DC):
                            P.mm(pb[:, :ncol], u[:, c, tb * 128:(tb + 1) * 128], wt[:, c, :ncol], start=(c == 0), stop=(c == DC - 1))
                        if k % 2 == 0:
                            P.copy(sg[:, :ncol], pb[:, :ncol], eng="act")
                        else:
                            P.copy(sg[:, :ncol], pb[:, :ncol], eng="dve")
                        P.dma("sp", ptm[t0 + tb * 128:t0 + (tb + 1) * 128, dst:dst + ncol], sg[:, :ncol])
                        k += 1
            if w_in_v is not None:
                fm_chunk(wv, 0, 64, R_VLO)
        P.flush()


SB_HEADS = 8
R_MIX_HG, R_MIX_SB, R_MIX_RW = 0, 512, 1536


def rmsnorm1(P, cx, x, gcol, out, sqs, rstd, TT, dn, eps, psb, extra=1.0):
    P.act(sqs[:, :TT], x, AF.Square)
    P.mm(psb[:, :TT], cx.ones_bf[:], sqs[:, :TT], start=True, stop=True)
    P.act(rstd[:, :TT], psb[:, :TT], AF.Ln, bias=float(eps), scale=1.0 / dn)
    P.act(rstd[:, :TT], rstd[:, :TT], AF.Exp, bias=float(math.log(extra)), scale=-0.5)
    P.stt(out, x, gcol, rstd[:, :TT], ALU.mult, ALU.mult)


def make_masks(cx, es):
    nc = cx.nc
    sb = lambda n, s, d: es.enter_context(nc.sbuf_tensor(n, s, d))
    cx.ones_w = sb("ones_w", [128, 896], BF16)
    cx.tri_incl = sb("tri_incl", [128, 128], BF16)
    cx.tri_ls = sb("tri_ls", [128, 128], BF16)
    cx.mw = sb("mw", [128, 896], BF16)
    cx.m_le = sb("m_le", [128, 128], BF16)
    cx.m_lt_f = sb("m_lt_f", [128, 128], F32)
    cx.m_le_f = sb("m_le_f", [128, 128], F32)
    cx.m_gt_f = sb("m_gt_f", [128, 128], F32)
    P = Phase(cx.st)
    P.memset(cx.ones_w[:], 1.0)

    def sel(out, in_, pat, cm, base, op):
        P.add("pool", lambda e: e.affine_select(out, in_, pattern=pat, compare_op=op, fill=0.0, base=base,
                                                 channel_multiplier=cm), [in_], [out])
    sel(cx.tri_incl[:], cx.ones_w[:, 0:128], [[-1, 128]], 1, 0, ALU.is_ge)
    sel(cx.tri_ls[:], cx.ones_w[:, 0:128], [[1, 128]], -1, 0, ALU.is_gt)
    sel(cx.mw[:], cx.ones_w[:], [[1, 896]], -1, -384, ALU.is_gt)
    sel(cx.m_le[:], cx.ones_w[:, 0:128], [[1, 128]], -1, 0, ALU.is_ge)
    sel(cx.m_lt_f[:], cx.ones_f[:], [[1, 128]], -1, 0, ALU.is_gt)
    sel(cx.m_le_f[:], cx.ones_f[:], [[1, 128]], -1, 0, ALU.is_ge)
    sel(cx.m_gt_f[:], cx.ones_f[:], [[-1, 128]], 1, 0, ALU.is_gt)
    P.flush()


def sb_phase(cx, pfm, ptm, gains, mixT, L):
    nc = cx.nc
    NT = L // 128
    with ExitStack() as es:
        cx.uid[0] += 1
        tg = "_%d" % cx.uid[0]
        sb = lambda n, s, d: es.enter_context(nc.sbuf_tensor(n + tg, s, d))
        gn = sb("s_gn", [128, 3], F32)
        qn = [sb("s_qn%d" % i, [128, L], BF16) for i in range(2)]
        kn = [sb("s_kn%d" % i, [128, L], BF16) for i in range(2)]
        vh = [sb("s_v%d" % i, [128, NT, 128], BF16) for i in range(2)]
        xin = [sb("s_x%d" % i, [128, 512], F32) for i in range(2)]
        sqs = sb("s_sq", [128, 512], BF16)
        rstd = sb("s_rstd", [128, 512], F32)
        E = [sb("s_E%d" % i, [128, 512], F32) for i in range(4)]
        Lb = [sb("s_L%d" % i, [128, 512], BF16) for i in range(4)]
        T1 = [sb("s_T1%d" % i, [128, 512], F32) for i in range(2)]
        T2 = [sb("s_T2%d" % i, [128, 512], F32) for i in range(4)]
        At = [sb("s_A%d" % i, [128, 512], BF16) for i in range(4)]
        Cs = sb("s_Cs", [128, 512], F32)
        oh = sb("s_oh", [128, 512], F32)
        ob = [sb("s_ob%d" % i, [128, 512], BF16) for i in range(2)]
        ps = cx.ps
        P = Phase(cx.st)
        P.dma("sp", gn[:], gains)
        ptv = ptm.rearrange("(n p) c -> p n c", p=128)
        kstep = 0
        for hd in range(SB_HEADS):
            b = hd % 2
            for n0 in range(0, NT, 8):
                n1 = min(NT, n0 + 8)
                P.dma("pool", vh[b][:, n0:n1, :], ptv[:, n0:n1, C_SBV + hd * 128:C_SBV + (hd + 1) * 128])
            i = 0
            for (t0, TT) in tiles_of(L):
                for (row, gi, dst, extra) in ((R_SBQ, 0, qn[b], 128.0 ** -0.5), (R_SBK, 1, kn[b], 1.0)):
                    x = xin[i % 2]
                    i += 1
                    P.dma("sp", x[:, :TT], pfm[row + hd * 128:row + (hd + 1) * 128, t0:t0 + TT])
                    rmsnorm1(P, cx, x[:, :TT], gn[:, gi:gi + 1], dst[:, t0:t0 + TT], sqs, rstd, TT, 128, EPS, ps[4], extra)
            for (t0, TQ) in tiles_of(L):
                sb_max = (t0 + TQ - 1) // 128
                P.memset(Cs[:, :TQ], 0.0, eng="pool")
                po = ps[7]
                steps = list(range(sb_max, -1, -1))

                def stageA0(sbk, k_):
                    pa = ps[k_ % 4]
                    P.mm(pa[:, :TQ], kn[b][:, sbk * 128:(sbk + 1) * 128], qn[b][:, t0:t0 + TQ], start=True, stop=True)

                def stageA1(sbk, k_):
                    w = k_ % 4
                    pa = ps[w]
                    off = sbk * 128 - t0
                    P.act(E[w][:, :TQ], pa[:, :TQ], AF.Exp)
                    P.act(Lb[w][:, :TQ], E[w][:, :TQ], AF.Ln, bias=1.0)
                    if off >= 0:
                        P.tt(Lb[w][:, :TQ], Lb[w][:, :TQ], cx.mw[:, 384 - off:384 - off + TQ], ALU.mult, eng="pool")

                def stageA2(sbk, k_):
                    w = k_ % 4
                    pa, pc = ps[w], ps[4 + k_ % 3]
                    P.mm(pa[:, :TQ], cx.tri_ls[:], Lb[w][:, :TQ], start=False, stop=True, sgc=True)
                    P.mm(pc[:, :TQ], cx.ones_bf[:], Lb[w][:, :TQ])

                def stageB(sbk, k_):
                    w = k_ % 4
                    pa, pc = ps[w], ps[4 + k_ % 3]
                    off = sbk * 128 - t0
                    P.tt(Cs[:, :TQ], pc[:, :TQ], Cs[:, :TQ], ALU.add)
                    P.tt(T2[w][:, :TQ], pa[:, :TQ], Cs[:, :TQ], ALU.subtract)
                    P.act(At[w][:, :TQ], T2[w][:, :TQ], AF.Exp)
                    if off >= 0:
                        P.tt(At[w][:, :TQ], At[w][:, :TQ], cx.mw[:, 384 - off:384 - off + TQ], ALU.mult, eng="pool")
                    P.mm(po[:, :TQ], vh[b][:, sbk, :], At[w][:, :TQ], start=(sbk == sb_max), stop=(sbk == 0))

                n_s = len(steps)
                for j in range(min(3, n_s)):
                    stageA0(steps[j], kstep + j)
                for j in range(min(2, n_s)):
                    stageA1(steps[j], kstep + j)
                stageA2(steps[0], kstep)
                for i_s, sbk in enumerate(steps):
                    if i_s + 3 < n_s:
                        stageA0(steps[i_s + 3], kstep + 3)
                    if i_s + 2 < n_s:
                        stageA1(steps[i_s + 2], kstep + 2)
                    if i_s + 1 < n_s:
                        stageA2(steps[i_s + 1], kstep + 1)
                    stageB(sbk, kstep)
                    kstep += 1
                P.copy(oh[:, :TQ], po[:, :TQ], eng="act")
                o2 = ob[(t0 // 512) % 2]
                rmsnorm1(P, cx, oh[:, :TQ], gn[:, 2:3], o2[:, :TQ], sqs, rstd, TQ, 128, EPS, ps[4])
                P.dma("sp", mixT[R_MIX_SB + hd * 128:R_MIX_SB + (hd + 1) * 128, t0:t0 + TQ], o2[:, :TQ])
        P.flush()


HG_HEADS = 4


def make_lb(cx, es, hg_lb_ap):
    nc = cx.nc
    sb = lambda n, s, d: es.enter_context(nc.sbuf_tensor(n, s, d))
    cx.lb = sb("lb", [128, 4, 4], F32)
    cx.oml = sb("oml", [128, 4, 4], F32)
    cx.noml = sb("noml", [128, 4, 4], F32)
    cx.rmask = sb("rmask", [128, 512], F32)
    with ExitStack() as es2:
        x = es2.enter_context(nc.sbuf_tensor("lb_x", [128, 4, 4], F32))
        e = es2.enter_context(nc.sbuf_tensor("lb_e", [128, 4, 4], F32))
        s = es2.enter_context(nc.sbuf_tensor("lb_s", [128, 4], F32))
        P = Phase(cx.st)
        P.dma("sp", x[:], hg_lb_ap)
        P.act(e[:], x[:], AF.Exp)
        P.tt(s[:], e[:, :, 0], e[:, :, 1], ALU.add)
        P.tt(s[:], s[:], e[:, :, 2], ALU.add)
        P.tt(s[:], s[:], e[:, :, 3], ALU.add)
        P.recip(s[:], s[:])
        P.memset(cx.lb[:, 0, :], 0.0)
        for l in range(1, 4):
            P.tt(e[:, :, l], e[:, :, l], s[:], ALU.mult)
            P.tt(cx.lb[:, l, :], cx.lb[:, l - 1, :], e[:, :, l], ALU.add)
        P.ts(cx.oml[:], cx.lb[:], -1.0, 1.0, op0=ALU.mult, op1=ALU.add)
        P.ts(cx.noml[:], cx.lb[:], 1.0, -1.0, op0=ALU.mult, op1=ALU.add)
        P.memset(cx.rmask[:], 1.0)
        for c in range(8):
            P.memset(cx.rmask[:, c * 64:c * 64 + 1], 0.0)
        P.flush()


def hg_phase(cx, pfm, ptm, layer, gnorm, mixT, L):
    nc = cx.nc
    NT = L // 128
    H = HG_HEADS
    with ExitStack() as es:
        cx.uid[0] += 1
        tg = "_%d" % cx.uid[0]
        sb = lambda n, s, d: es.enter_context(nc.sbuf_tensor(n + tg, s, d))
        gn = sb("g_gn", [128, 1], F32)
        V = [sb("g_v%d" % i, [64, 2 * NT, 128], BF16) for i in range(H)]
        X = [sb("g_x%d" % i, [128, 512], F32) for i in range(3)]
        SG = sb("g_sg", [128, 512], F32)
        FG = sb("g_fg", [128, 512], F32)
        QS = [sb("g_qs%d" % i, [128, 512], F32) for i in range(H)]
        KK = [sb("g_kk%d" % i, [128, 512], F32) for i in range(H)]
        G = [sb("g_G%d" % i, [128, 512], F32) for i in range(H)]
        NG = [sb("g_NG%d" % i, [128, 512], F32) for i in range(H)]
        EG = [sb("g_EG%d" % i, [128, 512], F32) for i in range(H)]
        QP = [sb("g_QP%d" % i, [128, 512], BF16) for i in range(H)]
        OH = [sb("g_OH%d" % i, [128, 512], F32) for i in range(H)]
        S = [sb("g_S%d" % i, [128, 128], F32) for i in range(H)]
        Sbf = [sb("g_Sb%d" % i, [128, 128], BF16) for i in range(H)]
        tmp = [sb("g_t%d" % i, [128, 64], F32) for i in range(6)]
        QT = [sb("g_QT%d" % i, [128, 64], BF16) for i in range(2)]
        KT = [sb("g_KT%d" % i, [128, 64], BF16) for i in range(2)]
        KH = [sb("g_KH%d" % i, [128, 64], F32) for i in range(2)]
        AM = [sb("g_AM%d" % i, [64, 64], BF16) for i in range(2)]
        AF32 = [sb("g_AF%d" % i, [64, 64], F32) for i in range(2)]
        KHt = [sb("g_KHt%d" % i, [64, 128], BF16) for i in range(2)]
        sqs = sb("g_sq", [128, 512], BF16)
        rstd = sb("g_rstd", [128, 512], F32)
        ON = sb("g_on", [128, 512], F32)
        MX = [sb("g_mx%d" % i, [128, 512], BF16) for i in range(2)]
        ps = cx.ps
        P = Phase(cx.st)
        P.dma("sp", gn[:], gnorm)
        ptv = ptm.rearrange("(n p) c -> p n c", p=64)
        for hd in range(H):
            for n0 in range(0, 2 * NT, 16):
                n1 = min(2 * NT, n0 + 16)
                P.dma("pool", V[hd][:, n0:n1, :], ptv[:, n0:n1, C_HGI + hd * 128:C_HGI + (hd + 1) * 128])
            P.memset(S[hd][:], 0.0)
            P.memset(Sbf[hd][:], 0.0)
        k = 0
        nm = 0
        for (t0, TT) in tiles_of(L):
            for hd in range(H):
                lb = cx.lb[:, layer, hd:hd + 1]
                oml = cx.oml[:, layer, hd:hd + 1]
                noml = cx.noml[:, layer, hd:hd + 1]
                P.dma("sp", X[0][:, :TT], pfm[R_HGF + hd * 128:R_HGF + (hd + 1) * 128, t0:t0 + TT])
                P.dma("sp", X[1][:, :TT], pfm[R_HGQ + hd * 128:R_HGQ + (hd + 1) * 128, t0:t0 + TT])
                P.act(SG[:, :TT], X[0][:, :TT], AF.Sigmoid)
                P.ts(FG[:, :TT], SG[:, :TT], oml, lb, op0=ALU.mult, op1=ALU.add)
                P.act(FG[:, :TT], FG[:, :TT], AF.Ln)
                P.ts(KK[hd][:, :TT], SG[:, :TT], noml, oml, op0=ALU.mult, op1=ALU.add)
                P.act(QS[hd][:, :TT], X[1][:, :TT], AF.Silu)
                P.scan(G[hd][:, :TT], cx.rmask[:, :TT], FG[:, :TT], 0.0, ALU.mult, ALU.add)
                P.ts(NG[hd][:, :TT], G[hd][:, :TT], -1.0, None, op0=ALU.mult)
                P.act(EG[hd][:, :TT], G[hd][:, :TT], AF.Exp)
                P.tt(QP[hd][:, :TT], QS[hd][:, :TT], EG[hd][:, :TT], ALU.mult)
            for c in range(TT // 64):
                blk = t0 // 64 + c
                cs = slice(c * 64, (c + 1) * 64)
                mid = c * 64 + 31
                end = c * 64 + 63
                for hd in range(H):
                    w = k % 2
                    k += 1
                    t1, t2, t3 = tmp[w * 3], tmp[w * 3 + 1], tmp[w * 3 + 2]
                    P.act(t1[:], G[hd][:, cs], AF.Exp, bias=NG[hd][:, mid:mid + 1])
                    P.stt(QT[w][:], t1[:], 1e30, QS[hd][:, cs], ALU.min, ALU.mult)
                    P.act(t2[:], G[hd][:, cs], AF.Exp, bias=G[hd][:, mid:mid + 1], scale=-1.0)
                    P.stt(KT[w][:], t2[:], 1e30, KK[hd][:, cs], ALU.min, ALU.mult)
                    P.act(t3[:], G[hd][:, cs], AF.Exp, bias=G[hd][:, end:end + 1], scale=-1.0)
                    P.tt(KH[w][:], KK[hd][:, cs], t3[:], ALU.mult, eng="pool")
                    pa, po, pt, pn = ps[w], ps[2 + w], ps[4 + w], ps[6]
                    P.mm(pa[:64, :64], KT[w][:], QT[w][:])
                    P.ts(AF32[w][:], pa[:64, :64], 1e30, -1e30, op0=ALU.min, op1=ALU.max)
                    P.tt(AM[w][:], AF32[w][:], cx.m_le[:64, :64], ALU.mult)
                    P.mm(po[:, :64], V[hd][:, blk, :], AM[w][:], start=True, stop=False)
                    P.mm(po[:, :64], Sbf[hd][:], QP[hd][:, cs], start=False, stop=True)
                    P.copy(OH[hd][:, cs], po[:, :64], eng="act")
                    P.transpose(pt[:64, :128], KH[w][:], cx.ident_f[:])
                    P.copy(KHt[w][:], pt[:64, :128], eng="dve")
                    P.mm(pn[:, :128], KHt[w][:], V[hd][:, blk, :])
                    P.stt(S[hd][:], S[hd][:], EG[hd][:, end:end + 1], pn[:, :128], ALU.mult, ALU.add)
                    P.copy(Sbf[hd][:], S[hd][:], eng="act")
            for hd in range(H):
                P.dma("sp", X[2][:, :TT], pfm[R_HGG + hd * 128:R_HGG + (hd + 1) * 128, t0:t0 + TT])
                P.act(X[2][:, :TT], X[2][:, :TT], AF.Silu)
                rmsnorm1(P, cx, OH[hd][:, :TT], gn[:, 0:1], ON[:, :TT], sqs, rstd, TT, 128, EPS, ps[7])
                mx = MX[nm % 2]
                nm += 1
                P.tt(mx[:, :TT], ON[:, :TT], X[2][:, :TT], ALU.mult)
                P.dma("sp", mixT[R_MIX_HG + hd * 128:R_MIX_HG + (hd + 1) * 128, t0:t0 + TT], mx[:, :TT])
        P.flush()


RW_H = 8
CW = -0.6065306597126334
RW_LN_EPS = 64e-5


def rw_phase(cx, pfm, ptm, layer, prm, vfirst, mixT, L):
    nc = cx.nc
    NT = L // 128
    H = RW_H
    with ExitStack() as es:
        cx.uid[0] += 1
        tg = "_%d" % cx.uid[0]
        sb = lambda n, s, d=F32: es.enter_context(nc.sbuf_tensor(n + tg, s, d))
        rwp = sb("r_rwp", [64, 7, 8])
        omka = sb("r_omka", [64, 8])
        lop = sb("r_lop", [128, 8])
        w2s = sb("r_w2", [96, 512])
        a2s = sb("r_a2", [96, 512])
        g2s = sb("r_g2", [128, 2, 512])
        v2s = sb("r_v2", [64, 512])
        tmb = sb("r_tmb", [128, 5, 512])
        m_gt4 = sb("r_mgt4", [128, 4, 128])
        m_lt4 = sb("r_mlt4", [128, 4, 128])
        m_le4 = sb("r_mle4", [128, 4, 128])
        id4 = sb("r_id4", [128, 4, 128])
        rmh = sb("r_rmh", [64, 8, 128])
        ST = sb("r_ST", [64, 8, 64])
        fm = {}
        for n in ("Rc", "Rp", "Kc", "Kp", "Rs", "Ks", "SW", "CS", "EP", "EN", "EX", "A", "KKn", "Bv",
                  "K2", "At", "Bt", "Kt", "Rt", "Bh", "Kh"):
            fm[n] = sb("r_f" + n, [64, 8, 128])
        fm["T0"], fm["T1"], fm["KK0"], fm["CX"] = fm["Rp"], fm["Kp"], fm["Rc"], fm["Kc"]
        lo = {}
        for n in ("WLc", "WLp", "ALc", "ALp"):
            lo[n] = sb("r_l" + n, [96, 128])
        for n in ("GLc", "GLp"):
            lo[n] = sb("r_l" + n, [128, 2, 128])
        for n in ("VLc", "VLp"):
            lo[n] = sb("r_l" + n, [64, 128])
        tm = {}
        for n in ("Vc", "Vp", "V", "VF", "SV", "Gt", "NXZ", "SA", "Y", "YN", "BHt", "KHt"):
            tm[n] = sb("r_t" + n, [128, 512])
        tm["CEN"], tm["SQ"], tm["BON"] = tm["Y"], tm["NXZ"], tm["SA"]
        MS = sb("r_MS", [128, 8])
        VS = sb("r_VS", [128, 8])
        RKS = sb("r_RKS", [128, 8])
        big = {}
        for n in ("M0", "M1", "N0", "N1", "PT", "LAK", "MRB", "MRK"):
            big[n] = sb("r_b" + n, [128, 8, 128])
        OB = [sb("r_OB%d" % i, [128, 4, 128], BF16) for i in range(2)]
        ps = cx.ps
        P = Phase(cx.st)
        bank = [0]

        def nb():
            b = ps[bank[0] % 8]
            bank[0] += 1
            return b

        def b3(p_, h=4):
            return p_[:].rearrange("p (h t) -> p h t", h=h)

        P.dma("sp", rwp[:], prm["rwp"])
        P.dma("sp", lop[:], prm["lop"])
        P.dma("sp", w2s[:], prm["w2"])
        P.dma("sp", a2s[:], prm["a2"])
        P.dma("sp", g2s[:], prm["g2"])
        P.dma("sp", tmb[:], prm["tmb"])
        if layer > 0:
            P.dma("sp", v2s[:], prm["v2"])
        P.ts(omka[:], rwp[:, 5, :], -1.0, 1.0, op0=ALU.mult, op1=ALU.add)
        for j in range(4):
            P.copy(m_gt4[:, j, :], cx.m_gt_f[:])
            P.copy(m_lt4[:, j, :], cx.m_lt_f[:])
            P.copy(m_le4[:, j, :], cx.m_le_f[:])
            P.copy(id4[:, j, :], cx.ident_f[:])
        P.memset(rmh[:], 1.0)
        P.memset(rmh[:, :, 0:1], 0.0)
        P.memset(ST[:], 0.0)

        def bc(ap2, n=128):
            return ap2.unsqueeze(2).to_broadcast([ap2.shape[0], ap2.shape[1], n])

        def fmv(row0):
            return pfm[row0:row0 + 512, :].rearrange("(h k) t -> k h t", k=64)

        rv, kv = fmv(R_RWR), fmv(R_RWK)
        mixv = mixT[R_MIX_RW:R_MIX_RW + 512, :].rearrange("(j p) t -> p j t", p=128)
        hs = lambda h: slice(h * 64, (h + 1) * 64)

        def shift_load(cur, prev, src3, t0, three):
            if three:
                P.dma("sp", cur[:], src3[:, :, t0:t0 + 128])
                if t0 == 0:
                    P.memset(prev[:, :, 0:1], 0.0)
                    P.dma("sp", prev[:, :, 1:128], src3[:, :, 0:127])
                else:
                    P.dma("sp", prev[:], src3[:, :, t0 - 1:t0 + 127])
            else:
                P.dma("sp", cur[:], src3[:, t0:t0 + 128])
                if t0 == 0:
                    P.memset(prev[:, 0:1], 0.0)
                    P.dma("sp", prev[:, 1:128], src3[:, 0:127])
                else:
                    P.dma("sp", prev[:], src3[:, t0 - 1:t0 + 127])

        for c in range(NT):
            t0 = c * 128
            f = fm
            shift_load(f["Rc"], f["Rp"], rv, t0, True)
            shift_load(f["Kc"], f["Kp"], kv, t0, True)
            for (cur, prev, out, mi, en) in ((f["Rc"], f["Rp"], f["Rs"], 0, "dve"), (f["Kc"], f["Kp"], f["Ks"], 1, "dve")):
                P.tt(prev[:], prev[:], cur[:], ALU.subtract, eng=en)
                P.tt(prev[:], prev[:], bc(rwp[:, mi, :]), ALU.mult, eng=en)
                P.tt(out[:], prev[:], cur[:], ALU.add, eng=en)
            shift_load(lo["WLc"], lo["WLp"], pfm[R_WLO:R_WLO + 96, :], t0, False)
            shift_load(lo["ALc"], lo["ALp"], pfm[R_ALO:R_ALO + 96, :], t0, False)
            shift_load(lo["GLc"], lo["GLp"], pfm[R_GLO:R_GLO + 256, :].rearrange("(j p) t -> p j t", p=128), t0, True)
            P.tt(lo["WLp"][:], lo["WLp"][:], lo["WLc"][:], ALU.subtract)
            P.stt(lo["WLc"][:], lo["WLp"][:], lop[:96, 0:1], lo["WLc"][:], ALU.mult, ALU.add)
            P.act(lo["WLc"][:], lo["WLc"][:], AF.Tanh)
            P.tt(lo["ALp"][:], lo["ALp"][:], lo["ALc"][:], ALU.subtract)
            P.stt(lo["ALc"][:], lo["ALp"][:], lop[:96, 1:2], lo["ALc"][:], ALU.mult, ALU.add)
            P.tt(lo["GLp"][:], lo["GLp"][:], lo["GLc"][:], ALU.subtract)
            for j in range(2):
                P.stt(lo["GLc"][:, j, :], lo["GLp"][:, j, :], lop[:, 2 + j:3 + j], lo["GLc"][:, j, :], ALU.mult, ALU.add)
            P.act(lo["GLc"][:], lo["GLc"][:], AF.Sigmoid)
            for (w_s, code, bias_i, out) in ((w2s, lo["WLc"], 2, f["SW"]), (a2s, lo["ALc"], 3, f["A"])):
                for half in range(2):
                    pb = nb()
                    for j in range(4):
                        h = half * 4 + j
                        P.mm(pb[:64, j * 128:(j + 1) * 128], w_s[:, hs(h)], code[:])
                    P.tt(out[:, half * 4:half * 4 + 4, :], b3(pb)[:64], bc(rwp[:, bias_i, half * 4:half * 4 + 4]), ALU.add)
                P.act(out[:], out[:], AF.Sigmoid)
            P.scan(f["CS"][:].rearrange("k h t -> k (h t)"), rmh[:].rearrange("k h t -> k (h t)"),
                   f["SW"][:].rearrange("k h t -> k (h t)"), 0.0, ALU.mult, ALU.add)
            P.tt(f["CX"][:], f["CS"][:], f["SW"][:], ALU.subtract)
            P.act(f["EP"][:], f["CS"][:], AF.Exp, scale=CW)
            P.act(f["EN"][:], f["CS"][:], AF.Exp, scale=-CW)
            P.act(f["EX"][:], f["CX"][:], AF.Exp, scale=CW)
            P.tt(f["KK0"][:], f["Ks"][:], bc(rwp[:, 4, :]), ALU.mult)
            P.tt(f["T0"][:], f["KK0"][:], f["KK0"][:], ALU.mult)
            for half in range(2):
                pb = nb()
                P.mm(pb[:64, :], cx.ones_f[:64, :64], f["T0"][:, half * 4:half * 4 + 4, :].rearrange("k h t -> k (h t)"))
                P.ts(f["T1"][:, half * 4:half * 4 + 4, :], b3(pb)[:64], 1e-16, None, op0=ALU.max)
            P.act(f["T1"][:], f["T1"][:], AF.Ln)
            P.act(f["T1"][:], f["T1"][:], AF.Exp, scale=-0.5)
            P.tt(f["KKn"][:], f["KK0"][:], f["T1"][:], ALU.mult)
            P.tt(f["Bv"][:], f["KKn"][:], f["A"][:], ALU.mult)
            P.tt(f["T0"][:], f["A"][:], bc(rwp[:, 5, :]), ALU.mult)
            P.tt(f["T0"][:], f["T0"][:], bc(omka[:]), ALU.add)
            P.tt(f["K2"][:], f["Ks"][:], f["T0"][:], ALU.mult)
            P.tt(f["At"][:], f["KKn"][:], f["EX"][:], ALU.mult)
            P.tt(f["Bt"][:], f["Bv"][:], f["EN"][:], ALU.mult)
            P.tt(f["Kt"][:], f["K2"][:], f["EN"][:], ALU.mult)
            P.tt(f["Rt"][:], f["Rs"][:], f["EP"][:], ALU.mult)
            eg = f["EP"][:, :, 127:128].to_broadcast([64, 8, 128])
            P.tt(f["Bh"][:], f["Bt"][:], eg, ALU.mult)
            P.tt(f["Kh"][:], f["Kt"][:], eg, ALU.mult)
            P.tt(f["T0"][:], f["Rs"][:], f["K2"][:], ALU.mult)
            P.tt(f["T0"][:], f["T0"][:], bc(rwp[:, 6, :]), ALU.mult)
            pb = nb()
            for h in range(H):
                P.mm(pb[:, h:h + 1], f["T0"][:, h, :], cx.ones_f[:64, 0:1])
            P.copy(RKS[:], pb[:, 0:8], eng="act")
            t = tm
            P.dma("sp", t["Vc"][:], ptm[t0:t0 + 128, C_RWV:C_RWV + 512])
            if t0 == 0:
                P.memset(t["Vp"][0:1, :], 0.0)
                P.dma("sp", t["Vp"][1:128, :], ptm[0:127, C_RWV:C_RWV + 512])
            else:
                P.dma("sp", t["Vp"][:], ptm[t0 - 1:t0 + 127, C_RWV:C_RWV + 512])
            P.tt(t["Vp"][:], t["Vp"][:], t["Vc"][:], ALU.subtract)
            P.tt(t["Vp"][:], t["Vp"][:], tmb[:, 0, :], ALU.mult)
            if layer == 0:
                P.tt(t["V"][:], t["Vp"][:], t["Vc"][:], ALU.add)
                P.dma("sp", vfirst[t0:t0 + 128, :], t["V"][:])
            else:
                P.tt(t["Vc"][:], t["Vp"][:], t["Vc"][:], ALU.add)
                shift_load(lo["VLc"], lo["VLp"], pfm[R_VLO:R_VLO + 64, :], t0, False)
                P.tt(lo["VLp"][:], lo["VLp"][:], lo["VLc"][:], ALU.subtract)
                P.stt(lo["VLc"][:], lo["VLp"][:], lop[:64, 4:5], lo["VLc"][:], ALU.mult, ALU.add)
                pb = nb()
                P.mm(pb[:, :], lo["VLc"][:], v2s[:])
                P.tt(t["SV"][:], pb[:, :], tmb[:, 1, :], ALU.add)
                P.act(t["SV"][:], t["SV"][:], AF.Sigmoid)
                P.dma("sp", t["VF"][:], vfirst[t0:t0 + 128, :])
                P.tt(t["VF"][:], t["VF"][:], t["Vc"][:], ALU.subtract)
                P.tt(t["VF"][:], t["VF"][:], t["SV"][:], ALU.mult)
                P.tt(t["V"][:], t["VF"][:], t["Vc"][:], ALU.add)
            pb = nb()
            for j in range(2):
                P.mm(pb[:, :], lo["GLc"][:, j, :], g2s[:, j, :], start=(j == 0), stop=(j == 1))
            P.copy(t["Gt"][:], pb[:, :], eng="act")
            M, N, PT = big["M0"], big["N0"], big["PT"]
            M2, N2 = big["M1"], big["N1"]
            for half in range(2):
                pa, pb = nb(), nb()
                for j in range(4):
                    h = half * 4 + j
                    P.mm(pa[:, j * 128:(j + 1) * 128], f["At"][:, h, :], f["Bt"][:, h, :])
                    P.mm(pb[:, j * 128:(j + 1) * 128], f["Bt"][:, h, :], f["At"][:, h, :])
                hh = slice(half * 4, half * 4 + 4)
                P.stt(M[:, hh, :], b3(pa), -1.0, m_gt4[:], ALU.mult, ALU.mult)
                P.stt(N[:, hh, :], b3(pb), -1.0, m_lt4[:], ALU.mult, ALU.mult)
                P.tt(PT[:, hh, :], N[:, hh, :], id4[:], ALU.add, eng="pool")
            for step in range(6):
                last = step == 5
                for half in range(2):
                    hh = slice(half * 4, half * 4 + 4)
                    pa = nb()
                    for j in range(4):
                        h = half * 4 + j
                        P.mm(pa[:, j * 128:(j + 1) * 128], N[:, h, :], M[:, h, :])
                    P.copy(M2[:, hh, :], b3(pa), eng="act")
                    if not last:
                        pb = nb()
                        for j in range(4):
                            h = half * 4 + j
                            P.mm(pb[:, j * 128:(j + 1) * 128], M[:, h, :], N[:, h, :])
                        P.copy(N2[:, hh, :], b3(pb), eng="dve")
                    pc = nb()
                    for j in range(4):
                        h = half * 4 + j
                        P.mm(pc[:, j * 128:(j + 1) * 128], M2[:, h, :], PT[:, h, :])
                    P.tt(PT[:, hh, :], b3(pc), PT[:, hh, :], ALU.add)
                M, M2 = M2, M
                N, N2 = N2, N
            for (dst, lt, rt, msk) in ((big["LAK"], f["Kt"], f["At"], m_lt4), (big["MRB"], f["Bt"], f["Rt"], m_le4),
                                       (big["MRK"], f["Kt"], f["Rt"], m_le4)):
                for half in range(2):
                    pa = nb()
                    for j in range(4):
                        h = half * 4 + j
                        P.mm(pa[:, j * 128:(j + 1) * 128], lt[:, h, :], rt[:, h, :])
                    P.tt(dst[:, half * 4:half * 4 + 4, :], b3(pa), msk[:], ALU.mult)
            for (src, dst) in ((f["Bh"], t["BHt"]), (f["Kh"], t["KHt"])):
                pa = nb()
                for h in range(H):
                    P.transpose(pa[:, hs(h)], src[:, h, :], cx.ident_f[:64, :64])
                P.copy(dst[:], pa[:, :], eng="act")
            pa = nb()
            for h in range(H):
                P.mm(pa[:, hs(h)], big["LAK"][:, h, :], t["V"][:, hs(h)], start=True, stop=False)
                P.mm(pa[:, hs(h)], f["At"][:, h, :], ST[:, h, :], start=False, stop=True)
            P.ts(t["NXZ"][:], pa[:, :], -1.0, None, op0=ALU.mult)
            pa = nb()
            for h in range(H):
                P.mm(pa[:, hs(h)], PT[:, h, :], t["NXZ"][:, hs(h)])
            P.copy(t["SA"][:], pa[:, :], eng="act")
            pa = nb()
            for h in range(H):
                P.mm(pa[:, hs(h)], f["Rt"][:, h, :], ST[:, h, :], start=True, stop=False)
                P.mm(pa[:, hs(h)], big["MRB"][:, h, :], t["SA"][:, hs(h)], start=False, stop=False)
                P.mm(pa[:, hs(h)], big["MRK"][:, h, :], t["V"][:, hs(h)], start=False, stop=True)
            P.copy(t["Y"][:], pa[:, :], eng="act")
            pa = nb()
            for h in range(H):
                P.mm(pa[:64, hs(h)], t["BHt"][:, hs(h)], t["SA"][:, hs(h)], start=True, stop=False)
                P.mm(pa[:64, hs(h)], t["KHt"][:, hs(h)], t["V"][:, hs(h)], start=False, stop=True)
            P.tt(ST[:], ST[:], f["EP"][:, :, 127:128].to_broadcast([64, 8, 64]), ALU.mult)
            P.tt(ST[:], ST[:], pa[:64, :].rearrange("k (h v) -> k h v", h=8), ALU.add)
            Y3 = t["Y"][:].rearrange("p (h v) -> p h v", h=8)
            C3 = t["CEN"][:].rearrange("p (h v) -> p h v", h=8)
            S3 = t["SQ"][:].rearrange("p (h v) -> p h v", h=8)
            N3 = t["YN"][:].rearrange("p (h v) -> p h v", h=8)
            V3 = t["V"][:].rearrange("p (h v) -> p h v", h=8)
            B3 = t["BON"][:].rearrange("p (h v) -> p h v", h=8)
            P.add("dve", lambda e: e.tensor_reduce(MS[:], Y3, AX.X, ALU.add), [t["Y"][:]], [MS[:]])
            P.ts(MS[:], MS[:], 1.0 / 64, None, op0=ALU.mult)
            P.tt(C3, Y3, MS[:].unsqueeze(2).to_broadcast([128, 8, 64]), ALU.subtract)
            P.tt(S3, C3, C3, ALU.mult, eng="pool")
            P.add("dve", lambda e: e.tensor_reduce(VS[:], S3, AX.X, ALU.add), [t["SQ"][:]], [VS[:]])
            P.act(VS[:], VS[:], AF.Ln, bias=RW_LN_EPS, scale=1.0 / 64)
            P.act(VS[:], VS[:], AF.Exp, scale=-0.5)
            P.tt(N3, C3, VS[:].unsqueeze(2).to_broadcast([128, 8, 64]), ALU.mult)
            P.tt(t["YN"][:], t["YN"][:], tmb[:, 2, :], ALU.mult)
            P.tt(t["YN"][:], t["YN"][:], tmb[:, 3, :], ALU.add)
            P.tt(B3, V3, RKS[:].unsqueeze(2).to_broadcast([128, 8, 64]), ALU.mult, eng="pool")
            P.tt(t["YN"][:], t["YN"][:], t["BON"][:], ALU.add)
            P.tt(t["YN"][:], t["YN"][:], t["Gt"][:], ALU.mult)
            pa = nb()
            for j in range(4):
                P.transpose(pa[:, j * 128:(j + 1) * 128], t["YN"][:, j * 128:(j + 1) * 128], cx.ident_f[:])
            ob = OB[c % 2]
            P.copy(ob[:], b3(pa), eng="act")
            P.dma("sp", mixv[:, :, t0:t0 + 128], ob[:])
        P.flush()


from concourse.bass_utils import run_bass_kernel_spmd

DEPTH = 4
N_META = 16
RW_SHIFT_N = 1984


def build_program(L, depth=DEPTH, Lr=None):
    nc = bass.Bass("TRN2", target_bir_lowering=False)
    dt = lambda n, s, k="ExternalInput", d=F32: nc.dram_tensor(n, s, d, kind=k).ap()
    h0 = dt("h0", [D, L])
    gains = dt("gains", [depth, 3, 128, DC])
    f1wi = dt("ffn1_wi", [depth, FC, 128, DC, 256])
    f1wo = dt("ffn1_wo", [depth, DC, 128, FC, 128])
    f2wi = dt("ffn2_wi", [depth, FC, 128, DC, 256])
    f2wo = dt("ffn2_wo", [depth, DC, 128, FC, 128])
    w_in = dt("w_in", [depth, 14, 128, DC, 512])
    w_in_v = dt("w_in_v", [depth - 1, 128, DC, 64]) if depth > 1 else None
    w_out = dt("w_out", [depth, 4, 128, DC, 512])
    hglb = dt("hglb", [128, 4, 4])
    hgn = dt("hgn", [depth, 128, 1])
    sbg = dt("sbg", [depth, 128, 3])
    rwp = dt("rwp", [depth, 64, 7, 8])
    lop = dt("lop", [depth, 128, 8])
    w2 = dt("rw_w2", [depth, 96, 512])
    a2 = dt("rw_a2", [depth, 96, 512])
    g2 = dt("rw_g2", [depth, 128, 2, 512])
    v2 = dt("rw_v2", [depth, 64, 512])
    tmb = dt("tmb", [depth, 128, 5, 512])
    hT = dt("hT", [D, L], "ExternalOutput")
    pfm = dt("pfm", [NFM, L], "Internal")
    ptm = dt("ptm", [L, NTM], "Internal")
    vfirst = dt("vfirst", [L, 512], "Internal")
    mixT = dt("mixT", [D, L], "Internal", BF16)
    with ExitStack() as es:
        cx = make_ctx(nc, es)
        cx.eps_ap = lambda e: float(e)
        init_consts(cx)
        make_masks(cx, es)
        make_lb(cx, es, hglb)
        P = Phase(cx.st)
        P.dma("sp", hT, h0)
        P.flush()
        for l in range(depth):
            ffn_phase(cx, hT, f1wi[l], f1wo[l], gains[l, 0], L, Lr=Lr)
            proj_phase(cx, hT, w_in[l], (w_in_v[l - 1] if l > 0 else None), gains[l, 1], pfm, ptm, L)
            hg_phase(cx, pfm, ptm, l, hgn[l], mixT, L)
            sb_phase(cx, pfm, ptm, sbg[l], mixT, L)
            prm = {"rwp": rwp[l], "lop": lop[l], "w2": w2[l], "a2": a2[l], "g2": g2[l], "v2": v2[l], "tmb": tmb[l]}
            rw_phase(cx, pfm, ptm, l, prm, vfirst, mixT, L)
            ffn_phase(cx, hT, f2wi[l], f2wo[l], gains[l, 2], L, pre=(mixT, w_out[l]), Lr=Lr)
    return nc, cx.st.nops


def _c(a):
    return np.ascontiguousarray(a, dtype=np.float32)


def layout_params(inp, depth=DEPTH):
    fm16 = lambda v: v.reshape(DC, 128).T
    fmh = lambda v: v.reshape(8, 64).T
    out = {}
    out["gains"] = _c(np.stack([np.stack([fm16(inp[k][l]) for k in ("norm_ffn1", "norm_mix", "norm_ffn2")]) for l in range(depth)]))
    for k in ("ffn1_wi", "ffn2_wi"):
        out[k] = _c(inp[k][:depth].reshape(depth, DC, 128, FC, 256).transpose(0, 3, 2, 1, 4))
    for k in ("ffn1_wo", "ffn2_wo"):
        out[k] = _c(inp[k][:depth].reshape(depth, FC, 128, DC, 128).transpose(0, 3, 2, 1, 4))
    wpad = np.zeros((depth, D, 7168), np.float32)
    wpad[:, :, :7104] = inp["w_in"][:depth]
    out["w_in"] = _c(wpad.reshape(depth, DC, 128, 14, 512).transpose(0, 3, 2, 1, 4))
    out["w_out"] = _c(inp["w_out"][:depth].reshape(depth, DC, 128, 4, 512).transpose(0, 3, 2, 1, 4))
    if depth > 1:
        out["w_in_v"] = _c(inp["w_in_v"][:depth - 1].reshape(depth - 1, DC, 128, 64).transpose(0, 2, 1, 3))
    out["hglb"] = _c(inp["hg_lb"].reshape(4, 4, 128).transpose(2, 1, 0))
    out["hgn"] = _c(inp["hg_norm"][:depth].reshape(depth, 128, 1))
    out["sbg"] = _c(np.stack([np.stack([inp["sb_qn"][l], inp["sb_kn"][l], inp["sb_on"][l]], axis=1) for l in range(depth)]))
    rwp, lop, v2, tmb = [], [], [], []
    for l in range(depth):
        mu = inp["rw_mu"][l]
        rwp.append(np.stack([fmh(mu[0:512]), fmh(mu[512:1024]), fmh(inp["rw_w0"][l]), fmh(inp["rw_a0"][l]), fmh(inp["rw_kk"][l]),
                             fmh(inp["rw_ka"][l]), fmh(inp["rw_rk"][l].reshape(-1))], axis=1))
        lp = np.zeros((128, 8), np.float32)
        lp[:96, 0] = mu[1536:1632]
        lp[:96, 1] = mu[1632:1728]
        lp[:, 2] = mu[1728:1856]
        lp[:, 3] = mu[1856:1984]
        tb = np.zeros((128, 5, 512), np.float32)
        tb[:, 0] = mu[1024:1536][None]
        tb[:, 2] = inp["rw_ln_w"][l][None]
        tb[:, 3] = inp["rw_ln_b"][l][None]
        vv = np.zeros((64, 512), np.float32)
        if l > 0:
            lp[:64, 4] = inp["rw_mu_v"][l - 1]
            tb[:, 1] = inp["rw_v0"][l - 1][None]
            vv = inp["rw_v2"][l - 1]
        lop.append(lp)
        tmb.append(tb)
        v2.append(vv)
    out["rwp"] = _c(np.stack(rwp))
    out["lop"] = _c(np.stack(lop))
    out["tmb"] = _c(np.stack(tmb))
    out["rw_v2"] = _c(np.stack(v2))
    out["rw_w2"] = _c(inp["rw_w2"][:depth])
    out["rw_a2"] = _c(inp["rw_a2"][:depth])
    out["rw_g2"] = _c(np.stack([inp["rw_g2"][l].reshape(2, 128, 512).transpose(1, 0, 2) for l in range(depth)]))
    return out


def kernel(**inputs):
    inp = {k: np.asarray(v) for k, v in inputs.items()}
    x = inp["x"]
    B, S, _ = x.shape
    depth = inp["norm_ffn1"].shape[0]
    L_real = N_META + S
    L = ((L_real + 127) // 128) * 128
    nc, nops = build_program(L, depth, L_real)
    shared = layout_params(inp, depth)
    n_cores = 8 if B == 4 else B
    in_maps = []
    for c in range(n_cores):
        b = c % B
        h0 = np.zeros((L, D), np.float32)
        h0[:N_META] = inp["meta"]
        h0[N_META:L_real] = x[b]
        m = dict(shared)
        m["h0"] = _c(h0.T)
        in_maps.append(m)
    res = run_bass_kernel_spmd(nc, in_maps, core_ids=list(range(n_cores)))
    out = np.stack([np.ascontiguousarray(res.results[b]["hT"].T[N_META:L_real]) for b in range(B)])
    return out.astype(np.float32)
```

```python
from contextlib import ExitStack
import math
import numpy as np
import concourse.bass as bass
import concourse.mybir as mybir

F32 = mybir.dt.float32
BF16 = mybir.dt.bfloat16
AF = mybir.ActivationFunctionType
ALU = mybir.AluOpType
AX = mybir.AxisListType

ENGS = ("pe", "act", "dve", "pool", "sp")
NDMA = 12


def region(ap):
    t = ap.tensor
    shp = list(t.shape)
    rowlen = 1
    for s in shp[1:]:
        rowlen *= s
    off = int(ap.offset)
    r0 = off // rowlen
    c0 = off % rowlen
    rext = 0
    cext = 0
    for step, cnt in ap.ap:
        step = abs(int(step))
        cnt = int(cnt)
        if cnt <= 1 or step == 0:
            continue
        if step >= rowlen and step % rowlen == 0:
            rext += (cnt - 1) * (step // rowlen)
        else:
            cext += (cnt - 1) * step
    c1 = c0 + cext + 1
    if c1 > rowlen:
        extra = (c1 - 1) // rowlen
        rext += extra
        c0, c1 = 0, rowlen
    return (ap.name, r0, r0 + rext + 1, c0, c1)


class State:
    def __init__(self, nc, es):
        self.nc = nc
        self.sem = {}
        self.cnt = {}
        for e in ENGS:
            self.sem[e] = es.enter_context(nc.semaphore("s_" + e))
            self.cnt[e] = 0
        self.dsem = {}
        self.dcnt = {}
        self.dnext = {}
        for q in ("sp", "pool", "act"):
            self.dsem[q] = [es.enter_context(nc.semaphore("d_%s%d" % (q, i))) for i in range(NDMA)]
            self.dcnt[q] = [0] * NDMA
            self.dnext[q] = 0
        self.waited = {e: {} for e in ENGS}
        self.nops = 0


class Phase:
    def __init__(self, st):
        self.st = st
        self.nc = st.nc
        self.ops = {e: [] for e in ENGS}
        self.recs = {}
        self.order = 0

    def _deps(self, reads, writes):
        deps = {}

        def add(done):
            k = done[0]
            if k not in deps or deps[k][2] < done[2]:
                deps[k] = done

        for ap in reads:
            nm, r0, r1, c0, c1 = region(ap)
            for rec in self.recs.get(nm, ()):
                if rec[5] and rec[0] < r1 and r0 < rec[1] and rec[2] < c1 and c0 < rec[3]:
                    add(rec[4])
        for ap in writes:
            nm, r0, r1, c0, c1 = region(ap)
            for rec in self.recs.get(nm, ()):
                if rec[0] < r1 and r0 < rec[1] and rec[2] < c1 and c0 < rec[3]:
                    add(rec[4])
        return deps

    def _record(self, reads, writes, done):
        for ap in writes:
            nm, r0, r1, c0, c1 = region(ap)
            lst = self.recs.setdefault(nm, [])
            lst[:] = [rc for rc in lst if not (r0 <= rc[0] and rc[1] <= r1 and c0 <= rc[2] and rc[3] <= c1)]
            lst.append([r0, r1, c0, c1, done, True])
        for ap in reads:
            nm, r0, r1, c0, c1 = region(ap)
            lst = self.recs.setdefault(nm, [])
            lst[:] = [rc for rc in lst if not ((not rc[5]) and rc[4][0] == done[0] and r0 <= rc[0] and rc[1] <= r1 and c0 <= rc[2] and rc[3] <= c1)]
            lst.append([r0, r1, c0, c1, done, False])

    def add(self, eng, fn, reads, writes, pe_skip=True):
        st = self.st
        deps = self._deps(reads, writes)
        st.cnt[eng] += 1
        done = ("e_" + eng, st.sem[eng], st.cnt[eng])
        waits = []
        for k, d in deps.items():
            if eng == "pe" and k == "e_pe":
                continue
            if st.waited[eng].get(k, 0) >= d[2]:
                continue
            st.waited[eng][k] = d[2]
            waits.append((d[1], d[2]))
        self.ops[eng].append((fn, waits, (st.sem[eng], 1)))
        self._record(reads, writes, done)
        st.nops += 1

    def dma(self, q, out, in_, **kw):
        st = self.st
        reads, writes = [in_], [out]
        deps = self._deps(reads, writes)
        i = st.dnext[q]
        st.dnext[q] = (i + 1) % NDMA
        sem = st.dsem[q][i]
        key = "d_%s%d" % (q, i)
        waits = []
        if st.dcnt[q][i] > 0 and st.waited[q].get(key, 0) < st.dcnt[q][i]:
            waits.append((sem, st.dcnt[q][i]))
            st.waited[q][key] = st.dcnt[q][i]
        st.dcnt[q][i] += 16
        done = (key, sem, st.dcnt[q][i])
        for k, d in deps.items():
            if st.waited[q].get(k, 0) >= d[2]:
                continue
            st.waited[q][k] = d[2]
            waits.append((d[1], d[2]))
        self.ops[q].append((lambda e: e.dma_start(out=out, in_=in_, **kw), waits, (sem, 16)))
        self._record(reads, writes, done)
        st.nops += 1

    def mm(self, out, lhsT, rhs, start=True, stop=True, sgc=False):
        if sgc:
            self.add("pe", lambda e: e.matmul(out, lhsT, rhs, start=start, stop=stop, skip_group_check=True), [lhsT, rhs], [out])
        else:
            self.add("pe", lambda e: e.matmul(out, lhsT, rhs, start=start, stop=stop), [lhsT, rhs], [out])

    def transpose(self, out, in_, ident):
        self.add("pe", lambda e: e.transpose(out, in_, ident), [in_, ident], [out])

    def act(self, out, in_, func, bias=None, scale=1.0, eng="act"):
        rd = [in_]
        kw = {}
        if bias is not None:
            kw["bias"] = bias
            if not isinstance(bias, (int, float)):
                rd.append(bias)
        if not isinstance(scale, (int, float)):
            rd.append(scale)
        self.add("act", lambda e: e.activation(out, in_, func, scale=scale, **kw), rd, [out])

    def tt(self, out, in0, in1, op, eng="dve"):
        self.add(eng, lambda e: e.tensor_tensor(out, in0, in1, op), [in0, in1], [out])

    def ts(self, out, in0, s1, s2=None, op0=ALU.mult, op1=ALU.bypass, eng="dve"):
        rd = [in0]
        for s in (s1, s2):
            if s is not None and not isinstance(s, (int, float)):
                rd.append(s)
        if s2 is None:
            self.add(eng, lambda e: e.tensor_scalar(out, in0, s1, None, op0), rd, [out])
        else:
            self.add(eng, lambda e: e.tensor_scalar(out, in0, s1, s2, op0, op1), rd, [out])

    def stt(self, out, in0, scalar, in1, op0, op1):
        rd = [in0, in1]
        if not isinstance(scalar, (int, float)):
            rd.append(scalar)
        self.add("dve", lambda e: e.scalar_tensor_tensor(out, in0, scalar, in1, op0, op1), rd, [out])

    def copy(self, out, in_, eng="dve"):
        if eng == "act":
            self.add("act", lambda e: e.copy(out, in_), [in_], [out])
        else:
            self.add(eng, lambda e: e.tensor_copy(out, in_), [in_], [out])

    def memset(self, out, val, eng="dve"):
        self.add(eng, lambda e: e.memset(out, val), [], [out])

    def recip(self, out, in_):
        self.add("dve", lambda e: e.reciprocal(out, in_), [in_], [out])

    def scan(self, out, d0, d1, init, op0, op1):
        rd = [d0, d1]
        if not isinstance(init, (int, float)):
            rd.append(init)
        self.add("dve", lambda e: e.tensor_tensor_scan(out, d0, d1, init, op0, op1), rd, [out])

    def flush(self, final=False):
        st = self.st
        nc = self.nc
        finals = []
        for e in ENGS:
            if st.cnt[e] > 0:
                finals.append(("e_" + e, st.sem[e], st.cnt[e]))
        for q in st.dsem:
            for i in range(NDMA):
                if st.dcnt[q][i] > 0:
                    finals.append(("d_%s%d" % (q, i), st.dsem[q][i], st.dcnt[q][i]))
        ops = self.ops
        emap = {"pe": "tensor", "act": "scalar", "dve": "vector", "pool": "gpsimd", "sp": "sync"}

        def mk(ename):
            def body(eng):
                for fn, waits, inc in ops[ename]:
                    for s, v in waits:
                        eng.wait_ge(s, v)
                    ins = fn(eng)
                    ins.then_inc(inc[0], inc[1])
                for k, s, v in finals:
                    if st.waited[ename].get(k, 0) >= v:
                        continue
                    st.waited[ename][k] = v
                    eng.wait_ge(s, v)
            return body

        with nc.Block() as block:
            for ename in ENGS:
                getattr(block, emap[ename])(mk(ename))
        self.ops = {e: [] for e in ENGS}
        self.recs = {}


def _coll(self, in_ap, out_ap, groups):
    st = self.st
    q = "pool"
    reads, writes = [in_ap], [out_ap]
    deps = self._deps(reads, writes)
    i = st.dnext[q]
    st.dnext[q] = (i + 1) % NDMA
    sem = st.dsem[q][i]
    key = "d_%s%d" % (q, i)
    waits = []
    if st.dcnt[q][i] > 0 and st.waited[q].get(key, 0) < st.dcnt[q][i]:
        waits.append((sem, st.dcnt[q][i]))
        st.waited[q][key] = st.dcnt[q][i]
    st.dcnt[q][i] += 16
    done = (key, sem, st.dcnt[q][i])
    for k, d in deps.items():
        if st.waited[q].get(k, 0) >= d[2]:
            continue
        st.waited[q][k] = d[2]
        waits.append((d[1], d[2]))
    self.ops[q].append((lambda e: e.collective_compute("AllGather", ALU.bypass, replica_groups=groups, ins=[in_ap], outs=[out_ap]), waits, (sem, 16)))
    self._record(reads, writes, done)
    st.nops += 1


Phase.allgather = _coll


D = 2048
DC = 16
DFF = 5632
FC = 44
EPS = 1e-6


def tiles_of(L, TT=512):
    out = []
    t = 0
    while t < L:
        n = min(TT, L - t)
        out.append((t, n))
        t += n
    return out


class Ctx:
    pass


def make_ctx(nc, es):
    cx = Ctx()
    cx.nc = nc
    cx.uid = [0]
    cx.st = State(nc, es)
    cx.ps = [es.enter_context(nc.psum_tensor("ps%d" % i, [128, 512], F32)) for i in range(8)]
    cx.ones_bf = es.enter_context(nc.sbuf_tensor("ones_bf", [128, 128], BF16))
    cx.ones_f = es.enter_context(nc.sbuf_tensor("ones_f", [128, 128], F32))
    cx.ident_f = es.enter_context(nc.sbuf_tensor("ident_f", [128, 128], F32))
    return cx


def init_consts(cx):
    P = Phase(cx.st)
    P.memset(cx.ones_bf[:], 1.0)
    P.memset(cx.ones_f[:], 1.0)
    nc = cx.nc
    P.add("pool", lambda e: e.affine_select(cx.ident_f[:], cx.ones_f[:], pattern=[[1, 128]], compare_op=ALU.is_equal,
                                             fill=0.0, base=0, channel_multiplier=-1), [cx.ones_f[:]], [cx.ident_f[:]])
    P.flush()


def rmsnorm_fm(P, cx, h, g, u, sq, rstd, ncn, TT, dn, eps, psb, extra_scale=1.0):
    for c in range(ncn):
        P.act(sq[:, c, :TT], h[:, c, :TT], AF.Square)
    for c in range(ncn):
        P.mm(psb[:, :TT], cx.ones_bf[:], sq[:, c, :TT], start=(c == 0), stop=(c == ncn - 1))
    P.act(rstd[:, :TT], psb[:, :TT], AF.Sqrt, bias=cx.eps_ap(eps), scale=1.0 / dn)
    P.recip(rstd[:, :TT], rstd[:, :TT])
    if extra_scale != 1.0:
        P.ts(rstd[:, :TT], rstd[:, :TT], float(extra_scale), None, op0=ALU.mult)
    for c in range(ncn):
        P.stt(u[:, c, :TT], h[:, c, :TT], g[:, c:c + 1], rstd[:, :TT], ALU.mult, ALU.mult)


def ffn_phase(cx, hT, wi, wo, gvec, L, pre=None, Lr=None):
    nc = cx.nc
    with ExitStack() as es:
        cx.uid[0] += 1
        tg = "_%d" % cx.uid[0]
        sb = lambda n, s, d: es.enter_context(nc.sbuf_tensor(n + tg, s, d))
        h = sb("f_h", [128, DC, 512], F32)
        u = sb("f_u", [128, DC, 512], BF16)
        hid = sb("f_hid", [128, FC, 512], BF16)
        rstd = sb("f_rstd", [128, 512], F32)
        g = sb("f_g", [128, DC], F32)
        wg = [sb("f_wg%d" % i, [128, DC, 256], BF16) for i in range(2)]
        wu = [sb("f_wu%d" % i, [128, DC, 256], BF16) for i in range(2)]
        wos = [sb("f_wo%d" % i, [128, FC, 128], BF16) for i in range(2)]
        tmp = [sb("f_tmp%d" % i, [128, 512], F32) for i in range(2)]
        P = Phase(cx.st)
        P.dma("sp", g[:], gvec)
        hv = hT.rearrange("(c p) t -> p c t", p=128)
        ps = cx.ps
        for (t0, TT) in tiles_of(L):
            P.dma("sp", h[:, :, :TT], hv[:, :, t0:t0 + TT])
            if pre is not None:
                mixT, w_out = pre
                mv = mixT.rearrange("(c p) t -> p c t", p=128)
                P.dma("sp", u[:, :, :TT], mv[:, :, t0:t0 + TT])
                for ds in range(4):
                    slab = hid[:, (ds % 2) * 16:(ds % 2) * 16 + 16, :]
                    P.dma("pool", slab, w_out[ds])
                    for dj in range(4):
                        dc = ds * 4 + dj
                        po = ps[4 + dc % 2]
                        for mc in range(DC):
                            P.mm(po[:, :TT], slab[:, mc, dj * 128:(dj + 1) * 128], u[:, mc, :TT], start=(mc == 0), stop=(mc == DC - 1))
                        P.tt(h[:, dc, :TT], po[:, :TT], h[:, dc, :TT], ALU.add)
            rmsnorm_fm(P, cx, h, g, u, hid, rstd, DC, TT, D, EPS, ps[7])
            k = 0
            for j2 in range(FC // 2):
                b = j2 % 2
                P.dma("pool", wg[b][:], wi[j2])
                P.dma("pool", wu[b][:], wi[FC // 2 + j2])
                for jj in range(2):
                    j = 2 * j2 + jj
                    pa = ps[(k % 2) * 2]
                    pb = ps[(k % 2) * 2 + 1]
                    tm = tmp[k % 2]
                    k += 1
                    for c in range(DC):
                        P.mm(pa[:, :TT], wg[b][:, c, jj * 128:(jj + 1) * 128], u[:, c, :TT], start=(c == 0), stop=(c == DC - 1))
                    for c in range(DC):
                        P.mm(pb[:, :TT], wu[b][:, c, jj * 128:(jj + 1) * 128], u[:, c, :TT], start=(c == 0), stop=(c == DC - 1))
                    P.act(tm[:, :TT], pa[:, :TT], AF.Silu)
                    P.tt(hid[:, j, :TT], tm[:, :TT], pb[:, :TT], ALU.mult)
            for dc in range(DC):
                b = dc % 2
                P.dma("pool", wos[b][:], wo[dc])
                po = ps[4 + b]
                for j in range(FC):
                    P.mm(po[:, :TT], wos[b][:, j, :], hid[:, j, :TT], start=(j == 0), stop=(j == FC - 1))
                P.stt(h[:, dc, :TT], po[:, :TT], 0.5, h[:, dc, :TT], ALU.mult, ALU.add)
            if Lr is not None and t0 + TT > Lr:
                P.memset(h[:, :, max(0, Lr - t0):TT], 0.0)
            P.dma("sp", hv[:, :, t0:t0 + TT], h[:, :, :TT])
        P.flush()


R_HGQ, R_HGF, R_HGG = 0, 512, 1024
R_SBQ, R_SBK = 1536, 2560
R_RWR, R_RWK = 3584, 4096
R_WLO, R_ALO, R_GLO, R_VLO = 4608, 4736, 4864, 5120
NFM = 5248
C_HGI, C_SBV, C_RWV = 0, 512, 1536
NTM = 2048

SLABS = [
    (0, 512, "fm", R_HGQ), (512, 512, "fm", R_HGF), (1536, 512, "fm", R_HGG),
    (2048, 512, "fm", R_SBQ), (2560, 512, "fm", R_SBQ + 512),
    (3072, 512, "fm", R_SBK), (3584, 512, "fm", R_SBK + 512),
    (5120, 512, "fm", R_RWR), (5632, 512, "fm", R_RWK),
    (6656, 448, "lo", 0),
    (1024, 512, "tm", C_HGI), (4096, 512, "tm", C_SBV), (4608, 512, "tm", C_SBV + 512), (6144, 512, "tm", C_RWV),
]


def proj_phase(cx, hT, w_in, w_in_v, gvec, pfm, ptm, L):
    nc = cx.nc
    with ExitStack() as es:
        cx.uid[0] += 1
        tg = "_%d" % cx.uid[0]
        sb = lambda n, s, d: es.enter_context(nc.sbuf_tensor(n + tg, s, d))
        h = sb("p_h", [128, DC, 512], F32)
        u = sb("p_u", [128, DC, 512], BF16)
        sq = sb("p_sq", [128, DC, 512], BF16)
        rstd = sb("p_rstd", [128, 512], F32)
        g = sb("p_g", [128, DC], F32)
        ws = [sb("p_w%d" % i, [128, DC, 512], BF16) for i in range(2)]
        wv = sb("p_wv", [128, DC, 64], BF16)
        stg = [sb("p_stg%d" % i, [128, 512], F32) for i in range(4)]
        P = Phase(cx.st)
        P.dma("sp", g[:], gvec)
        hv = hT.rearrange("(c p) t -> p c t", p=128)
        if w_in_v is not None:
            P.dma("pool", wv[:], w_in_v)
        ps = cx.ps
        k = 0
        nslab = 0
        for (t0, TT) in tiles_of(L):
            P.dma("sp", h[:, :, :TT], hv[:, :, t0:t0 + TT])
            rmsnorm_fm(P, cx, h, g, u, sq, rstd, DC, TT, D, EPS, ps[7])

            def fm_chunk(wt, cs, M, drow):
                nonlocal k
                pb = ps[k % 4]
                sg = stg[k % 4]
                for c in range(DC):
                    P.mm(pb[:M, :TT], wt[:, c, cs:cs + M], u[:, c, :TT], start=(c == 0), stop=(c == DC - 1))
                if k % 2 == 0:
                    P.copy(sg[:M, :TT], pb[:M, :TT], eng="act")
                else:
                    P.copy(sg[:M, :TT], pb[:M, :TT], eng="dve")
                P.dma("sp", pfm[drow:drow + M, t0:t0 + TT], sg[:M, :TT])
                k += 1

            for (c0, ncol, kind, dst) in SLABS:
                wt = ws[nslab % 2]
                nslab += 1
                P.dma("pool", wt[:], w_in[c0 // 512])
                if kind == "fm":
                    for j in range(ncol // 128):
                        fm_chunk(wt, j * 128, 128, dst + j * 128)
                elif kind == "lo":
                    fm_chunk(wt, 0, 96, R_WLO)
                    fm_chunk(wt, 96, 96, R_ALO)
                    fm_chunk(wt, 192, 128, R_GLO)
                    fm_chunk(wt, 320, 128, R_GLO + 128)
                else:
                    for tb in range(TT // 128):
                        pb = ps[k % 4]
                        sg = stg[k % 4]
                        for c in range(DC):
                            P.mm(pb[:, :ncol], u[:, c, tb * 128:(tb + 1) * 128], wt[:, c, :ncol], start=(c == 0), stop=(c == DC - 1))
                        if k % 2 == 0:
                            P.copy(sg[:, :ncol], pb[:, :ncol], eng="act")
                        else:
                            P.copy(sg[:, :ncol], pb[:, :ncol], eng="dve")
                        P.dma("sp", ptm[t0 + tb * 128:t0 + (tb + 1) * 128, dst:dst + ncol], sg[:, :ncol])
                        k += 1
            if w_in_v is not None:
                fm_chunk(wv, 0, 64, R_VLO)
        P.flush()


SB_HEADS = 8
R_MIX_HG, R_MIX_SB, R_MIX_RW = 0, 512, 1536


def rmsnorm1(P, cx, x, gcol, out, sqs, rstd, TT, dn, eps, psb, extra=1.0):
    P.act(sqs[:, :TT], x, AF.Square)
    P.mm(psb[:, :TT], cx.ones_bf[:], sqs[:, :TT], start=True, stop=True)
    P.act(rstd[:, :TT], psb[:, :TT], AF.Ln, bias=float(eps), scale=1.0 / dn)
    P.act(rstd[:, :TT], rstd[:, :TT], AF.Exp, bias=float(math.log(extra)), scale=-0.5)
    P.stt(out, x, gcol, rstd[:, :TT], ALU.mult, ALU.mult)


def make_masks(cx, es):
    nc = cx.nc
    sb = lambda n, s, d: es.enter_context(nc.sbuf_tensor(n, s, d))
    cx.ones_w = sb("ones_w", [128, 896], BF16)
    cx.tri_incl = sb("tri_incl", [128, 128], BF16)
    cx.tri_ls = sb("tri_ls", [128, 128], BF16)
    cx.mw = sb("mw", [128, 896], BF16)
    cx.m_le = sb("m_le", [128, 128], BF16)
    cx.m_lt_f = sb("m_lt_f", [128, 128], F32)
    cx.m_le_f = sb("m_le_f", [128, 128], F32)
    cx.m_gt_f = sb("m_gt_f", [128, 128], F32)
    P = Phase(cx.st)
    P.memset(cx.ones_w[:], 1.0)

    def sel(out, in_, pat, cm, base, op):
        P.add("pool", lambda e: e.affine_select(out, in_, pattern=pat, compare_op=op, fill=0.0, base=base,
                                                 channel_multiplier=cm), [in_], [out])
    sel(cx.tri_incl[:], cx.ones_w[:, 0:128], [[-1, 128]], 1, 0, ALU.is_ge)
    sel(cx.tri_ls[:], cx.ones_w[:, 0:128], [[1, 128]], -1, 0, ALU.is_gt)
    sel(cx.mw[:], cx.ones_w[:], [[1, 896]], -1, -384, ALU.is_gt)
    sel(cx.m_le[:], cx.ones_w[:, 0:128], [[1, 128]], -1, 0, ALU.is_ge)
    sel(cx.m_lt_f[:], cx.ones_f[:], [[1, 128]], -1, 0, ALU.is_gt)
    sel(cx.m_le_f[:], cx.ones_f[:], [[1, 128]], -1, 0, ALU.is_ge)
    sel(cx.m_gt_f[:], cx.ones_f[:], [[-1, 128]], 1, 0, ALU.is_gt)
    P.flush()


def sb_phase(cx, pfm, ptm, gains, mixT, L):
    nc = cx.nc
    NT = L // 128
    with ExitStack() as es:
        cx.uid[0] += 1
        tg = "_%d" % cx.uid[0]
        sb = lambda n, s, d: es.enter_context(nc.sbuf_tensor(n + tg, s, d))
        gn = sb("s_gn", [128, 3], F32)
        qn = [sb("s_qn%d" % i, [128, L], BF16) for i in range(2)]
        kn = [sb("s_kn%d" % i, [128, L], BF16) for i in range(2)]
        vh = [sb("s_v%d" % i, [128, NT, 128], BF16) for i in range(2)]
        xin = [sb("s_x%d" % i, [128, 512], F32) for i in range(2)]
        sqs = sb("s_sq", [128, 512], BF16)
        rstd = sb("s_rstd", [128, 512], F32)
        E = [sb("s_E%d" % i, [128, 512], F32) for i in range(4)]
        Lb = [sb("s_L%d" % i, [128, 512], BF16) for i in range(4)]
        T1 = [sb("s_T1%d" % i, [128, 512], F32) for i in range(2)]
        T2 = [sb("s_T2%d" % i, [128, 512], F32) for i in range(4)]
        At = [sb("s_A%d" % i, [128, 512], BF16) for i in range(4)]
        Cs = sb("s_Cs", [128, 512], F32)
        oh = sb("s_oh", [128, 512], F32)
        ob = [sb("s_ob%d" % i, [128, 512], BF16) for i in range(2)]
        ps = cx.ps
        P = Phase(cx.st)
        P.dma("sp", gn[:], gains)
        ptv = ptm.rearrange("(n p) c -> p n c", p=128)
        kstep = 0
        for hd in range(SB_HEADS):
            b = hd % 2
            for n0 in range(0, NT, 8):
                n1 = min(NT, n0 + 8)
                P.dma("pool", vh[b][:, n0:n1, :], ptv[:, n0:n1, C_SBV + hd * 128:C_SBV + (hd + 1) * 128])
            i = 0
            for (t0, TT) in tiles_of(L):
                for (row, gi, dst, extra) in ((R_SBQ, 0, qn[b], 128.0 ** -0.5), (R_SBK, 1, kn[b], 1.0)):
                    x = xin[i % 2]
                    i += 1
                    P.dma("sp", x[:, :TT], pfm[row + hd * 128:row + (hd + 1) * 128, t0:t0 + TT])
                    rmsnorm1(P, cx, x[:, :TT], gn[:, gi:gi + 1], dst[:, t0:t0 + TT], sqs, rstd, TT, 128, EPS, ps[6], extra)
            po = ps[7]
            allsteps = []
            for (t0, TQ) in tiles_of(L):
                sb_max = (t0 + TQ - 1) // 128
                for sbk in range(sb_max, -1, -1):
                    allsteps.append((t0, TQ, sb_max, sbk))

            def stageA0(stp, k_):
                t0, TQ, sb_max, sbk = stp
                pa = ps[k_ % 4]
                P.mm(pa[:, :TQ], kn[b][:, sbk * 128:(sbk + 1) * 128], qn[b][:, t0:t0 + TQ], start=True, stop=True)

            def stageA1(stp, k_):
                t0, TQ, sb_max, sbk = stp
                w = k_ % 4
                pa = ps[w]
                off = sbk * 128 - t0
                P.act(E[w][:, :TQ], pa[:, :TQ], AF.Exp)
                P.act(Lb[w][:, :TQ], E[w][:, :TQ], AF.Ln, bias=1.0)
                if off >= 0:
                    P.tt(Lb[w][:, :TQ], Lb[w][:, :TQ], cx.mw[:, 384 - off:384 - off + TQ], ALU.mult, eng="pool")

            def stageA2(stp, k_):
                t0, TQ, sb_max, sbk = stp
                w = k_ % 4
                pa, pc = ps[w], ps[4 + k_ % 2]
                P.mm(pa[:, :TQ], cx.tri_ls[:], Lb[w][:, :TQ], start=False, stop=True, sgc=True)
                P.mm(pc[:, :TQ], cx.ones_bf[:], Lb[w][:, :TQ])

            def stageB(stp, k_):
                t0, TQ, sb_max, sbk = stp
                w = k_ % 4
                pa, pc = ps[w], ps[4 + k_ % 2]
                off = sbk * 128 - t0
                if sbk == sb_max:
                    P.memset(Cs[:, :TQ], 0.0, eng="pool")
                P.tt(Cs[:, :TQ], pc[:, :TQ], Cs[:, :TQ], ALU.add)
                P.tt(T2[w][:, :TQ], pa[:, :TQ], Cs[:, :TQ], ALU.subtract)
                P.act(At[w][:, :TQ], T2[w][:, :TQ], AF.Exp)
                if off >= 0:
                    P.tt(At[w][:, :TQ], At[w][:, :TQ], cx.mw[:, 384 - off:384 - off + TQ], ALU.mult, eng="pool")
                P.mm(po[:, :TQ], vh[b][:, sbk, :], At[w][:, :TQ], start=(sbk == sb_max), stop=(sbk == 0))
                if sbk == 0:
                    P.copy(oh[:, :TQ], po[:, :TQ], eng="act")
                    o2 = ob[(t0 // 512) % 2]
                    rmsnorm1(P, cx, oh[:, :TQ], gn[:, 2:3], o2[:, :TQ], sqs, rstd, TQ, 128, EPS, ps[6])
                    P.dma("sp", mixT[R_MIX_SB + hd * 128:R_MIX_SB + (hd + 1) * 128, t0:t0 + TQ], o2[:, :TQ])

            n_s = len(allsteps)
            for j in range(min(3, n_s)):
                stageA0(allsteps[j], kstep + j)
            for j in range(min(2, n_s)):
                stageA1(allsteps[j], kstep + j)
            stageA2(allsteps[0], kstep)
            for i_s, stp in enumerate(allsteps):
                if i_s + 3 < n_s:
                    stageA0(allsteps[i_s + 3], kstep + 3)
                if i_s + 2 < n_s:
                    stageA1(allsteps[i_s + 2], kstep + 2)
                if i_s + 1 < n_s:
                    stageA2(allsteps[i_s + 1], kstep + 1)
                stageB(stp, kstep)
                kstep += 1
        P.flush()


HG_HEADS = 4


def make_lb(cx, es, hg_lb_ap):
    nc = cx.nc
    sb = lambda n, s, d: es.enter_context(nc.sbuf_tensor(n, s, d))
    cx.lb = sb("lb", [128, 4, 4], F32)
    cx.oml = sb("oml", [128, 4, 4], F32)
    cx.noml = sb("noml", [128, 4, 4], F32)
    cx.rmask = sb("rmask", [128, 512], F32)
    with ExitStack() as es2:
        x = es2.enter_context(nc.sbuf_tensor("lb_x", [128, 4, 4], F32))
        e = es2.enter_context(nc.sbuf_tensor("lb_e", [128, 4, 4], F32))
        s = es2.enter_context(nc.sbuf_tensor("lb_s", [128, 4], F32))
        P = Phase(cx.st)
        P.dma("sp", x[:], hg_lb_ap)
        P.act(e[:], x[:], AF.Exp)
        P.tt(s[:], e[:, :, 0], e[:, :, 1], ALU.add)
        P.tt(s[:], s[:], e[:, :, 2], ALU.add)
        P.tt(s[:], s[:], e[:, :, 3], ALU.add)
        P.recip(s[:], s[:])
        P.memset(cx.lb[:, 0, :], 0.0)
        for l in range(1, 4):
            P.tt(e[:, :, l], e[:, :, l], s[:], ALU.mult)
            P.tt(cx.lb[:, l, :], cx.lb[:, l - 1, :], e[:, :, l], ALU.add)
        P.ts(cx.oml[:], cx.lb[:], -1.0, 1.0, op0=ALU.mult, op1=ALU.add)
        P.ts(cx.noml[:], cx.lb[:], 1.0, -1.0, op0=ALU.mult, op1=ALU.add)
        P.memset(cx.rmask[:], 1.0)
        for c in range(8):
            P.memset(cx.rmask[:, c * 64:c * 64 + 1], 0.0)
        P.flush()


def hg_phase(cx, pfm, ptm, layer, gnorm, mixT, L):
    nc = cx.nc
    NT = L // 128
    H = HG_HEADS
    with ExitStack() as es:
        cx.uid[0] += 1
        tg = "_%d" % cx.uid[0]
        sb = lambda n, s, d: es.enter_context(nc.sbuf_tensor(n + tg, s, d))
        gn = sb("g_gn", [128, 1], F32)
        V = [sb("g_v%d" % i, [64, 2 * NT, 128], BF16) for i in range(H)]
        X = [sb("g_x%d" % i, [128, 512], F32) for i in range(3)]
        SG = sb("g_sg", [128, 512], F32)
        FG = sb("g_fg", [128, 512], F32)
        QS = [sb("g_qs%d" % i, [128, 512], F32) for i in range(H)]
        KK = [sb("g_kk%d" % i, [128, 512], F32) for i in range(H)]
        G = [sb("g_G%d" % i, [128, 512], F32) for i in range(H)]
        NG = [sb("g_NG%d" % i, [128, 512], F32) for i in range(H)]
        EG = [sb("g_EG%d" % i, [128, 512], F32) for i in range(H)]
        QP = [sb("g_QP%d" % i, [128, 512], BF16) for i in range(H)]
        OH = [sb("g_OH%d" % i, [128, 512], F32) for i in range(H)]
        S = [sb("g_S%d" % i, [128, 128], F32) for i in range(H)]
        Sbf = [sb("g_Sb%d" % i, [128, 128], BF16) for i in range(H)]
        tmp = [sb("g_t%d" % i, [128, 64], F32) for i in range(6)]
        QT = [sb("g_QT%d" % i, [128, 64], BF16) for i in range(2)]
        KT = [sb("g_KT%d" % i, [128, 64], BF16) for i in range(2)]
        KH = [sb("g_KH%d" % i, [128, 64], F32) for i in range(2)]
        AM = [sb("g_AM%d" % i, [64, 64], BF16) for i in range(2)]
        AF32 = [sb("g_AF%d" % i, [64, 64], F32) for i in range(2)]
        KHt = [sb("g_KHt%d" % i, [64, 128], BF16) for i in range(2)]
        sqs = sb("g_sq", [128, 512], BF16)
        rstd = sb("g_rstd", [128, 512], F32)
        ON = sb("g_on", [128, 512], F32)
        MX = [sb("g_mx%d" % i, [128, 512], BF16) for i in range(2)]
        ps = cx.ps
        P = Phase(cx.st)
        P.dma("sp", gn[:], gnorm)
        ptv = ptm.rearrange("(n p) c -> p n c", p=64)
        for hd in range(H):
            for n0 in range(0, 2 * NT, 16):
                n1 = min(2 * NT, n0 + 16)
                P.dma("pool", V[hd][:, n0:n1, :], ptv[:, n0:n1, C_HGI + hd * 128:C_HGI + (hd + 1) * 128])
            P.memset(S[hd][:], 0.0)
            P.memset(Sbf[hd][:], 0.0)
        k = 0
        nm = 0
        for (t0, TT) in tiles_of(L):
            for hd in range(H):
                lb = cx.lb[:, layer, hd:hd + 1]
                oml = cx.oml[:, layer, hd:hd + 1]
                noml = cx.noml[:, layer, hd:hd + 1]
                P.dma("sp", X[0][:, :TT], pfm[R_HGF + hd * 128:R_HGF + (hd + 1) * 128, t0:t0 + TT])
                P.dma("sp", X[1][:, :TT], pfm[R_HGQ + hd * 128:R_HGQ + (hd + 1) * 128, t0:t0 + TT])
                P.act(SG[:, :TT], X[0][:, :TT], AF.Sigmoid)
                P.ts(FG[:, :TT], SG[:, :TT], oml, lb, op0=ALU.mult, op1=ALU.add)
                P.act(FG[:, :TT], FG[:, :TT], AF.Ln)
                P.ts(KK[hd][:, :TT], SG[:, :TT], noml, oml, op0=ALU.mult, op1=ALU.add)
                P.act(QS[hd][:, :TT], X[1][:, :TT], AF.Silu)
                P.scan(G[hd][:, :TT], cx.rmask[:, :TT], FG[:, :TT], 0.0, ALU.mult, ALU.add)
                P.ts(NG[hd][:, :TT], G[hd][:, :TT], -1.0, None, op0=ALU.mult)
                P.act(EG[hd][:, :TT], G[hd][:, :TT], AF.Exp)
                P.tt(QP[hd][:, :TT], QS[hd][:, :TT], EG[hd][:, :TT], ALU.mult)
            for c in range(TT // 64):
                blk = t0 // 64 + c
                cs = slice(c * 64, (c + 1) * 64)
                mid = c * 64 + 31
                end = c * 64 + 63
                for hd in range(H):
                    w = k % 2
                    k += 1
                    t1, t2, t3 = tmp[w * 3], tmp[w * 3 + 1], tmp[w * 3 + 2]
                    P.act(t1[:], G[hd][:, cs], AF.Exp, bias=NG[hd][:, mid:mid + 1])
                    P.stt(QT[w][:], t1[:], 1e30, QS[hd][:, cs], ALU.min, ALU.mult)
                    P.act(t2[:], G[hd][:, cs], AF.Exp, bias=G[hd][:, mid:mid + 1], scale=-1.0)
                    P.stt(KT[w][:], t2[:], 1e30, KK[hd][:, cs], ALU.min, ALU.mult)
                    P.act(t3[:], G[hd][:, cs], AF.Exp, bias=G[hd][:, end:end + 1], scale=-1.0)
                    P.tt(KH[w][:], KK[hd][:, cs], t3[:], ALU.mult, eng="pool")
                    pa, po, pt, pn = ps[w], ps[2 + w], ps[4 + w], ps[6]
                    P.mm(pa[:64, :64], KT[w][:], QT[w][:])
                    P.ts(AF32[w][:], pa[:64, :64], 1e30, -1e30, op0=ALU.min, op1=ALU.max)
                    P.tt(AM[w][:], AF32[w][:], cx.m_le[:64, :64], ALU.mult)
                    P.mm(po[:, :64], V[hd][:, blk, :], AM[w][:], start=True, stop=False)
                    P.mm(po[:, :64], Sbf[hd][:], QP[hd][:, cs], start=False, stop=True)
                    P.copy(OH[hd][:, cs], po[:, :64], eng="act")
                    P.transpose(pt[:64, :128], KH[w][:], cx.ident_f[:])
                    P.copy(KHt[w][:], pt[:64, :128], eng="dve")
                    P.mm(pn[:, :128], KHt[w][:], V[hd][:, blk, :])
                    P.stt(S[hd][:], S[hd][:], EG[hd][:, end:end + 1], pn[:, :128], ALU.mult, ALU.add)
                    P.copy(Sbf[hd][:], S[hd][:], eng="act")
            for hd in range(H):
                P.dma("sp", X[2][:, :TT], pfm[R_HGG + hd * 128:R_HGG + (hd + 1) * 128, t0:t0 + TT])
                P.act(X[2][:, :TT], X[2][:, :TT], AF.Silu)
                rmsnorm1(P, cx, OH[hd][:, :TT], gn[:, 0:1], ON[:, :TT], sqs, rstd, TT, 128, EPS, ps[7])
                mx = MX[nm % 2]
                nm += 1
                P.tt(mx[:, :TT], ON[:, :TT], X[2][:, :TT], ALU.mult)
                P.dma("sp", mixT[R_MIX_HG + hd * 128:R_MIX_HG + (hd + 1) * 128, t0:t0 + TT], mx[:, :TT])
        P.flush()


RW_H = 8
CW = -0.6065306597126334
RW_LN_EPS = 64e-5


def rw_phase(cx, pfm, ptm, layer, prm, vfirst, mixT, L):
    nc = cx.nc
    NT = L // 128
    H = RW_H
    with ExitStack() as es:
        cx.uid[0] += 1
        tg = "_%d" % cx.uid[0]
        sb = lambda n, s, d=F32: es.enter_context(nc.sbuf_tensor(n + tg, s, d))
        rwp = sb("r_rwp", [64, 7, 8])
        omka = sb("r_omka", [64, 8])
        lop = sb("r_lop", [128, 8])
        w2s = sb("r_w2", [96, 512])
        a2s = sb("r_a2", [96, 512])
        g2s = sb("r_g2", [128, 2, 512])
        v2s = sb("r_v2", [64, 512])
        tmb = sb("r_tmb", [128, 5, 512])
        m_gt4 = sb("r_mgt4", [128, 4, 128])
        m_lt4 = sb("r_mlt4", [128, 4, 128])
        m_le4 = sb("r_mle4", [128, 4, 128])
        id4 = sb("r_id4", [128, 4, 128])
        rmh = sb("r_rmh", [64, 8, 128])
        ST = sb("r_ST", [64, 8, 64])
        fm = {}
        for n in ("Rc", "Rp", "Kc", "Kp", "Rs", "Ks", "SW", "CS", "EP", "EN", "EX", "A", "KKn", "Bv",
                  "K2", "At", "Bt", "Kt", "Rt", "Bh", "Kh"):
            fm[n] = sb("r_f" + n, [64, 8, 128])
        fm["T0"], fm["T1"], fm["KK0"], fm["CX"] = fm["Rp"], fm["Kp"], fm["Rc"], fm["Kc"]
        lo = {}
        for n in ("WLc", "WLp", "ALc", "ALp"):
            lo[n] = sb("r_l" + n, [96, 128])
        for n in ("GLc", "GLp"):
            lo[n] = sb("r_l" + n, [128, 2, 128])
        for n in ("VLc", "VLp"):
            lo[n] = sb("r_l" + n, [64, 128])
        tm = {}
        for n in ("Vc", "Vp", "V", "VF", "SV", "Gt", "NXZ", "SA", "Y", "YN", "BHt", "KHt"):
            tm[n] = sb("r_t" + n, [128, 512])
        tm["CEN"], tm["SQ"], tm["BON"] = tm["Y"], tm["NXZ"], tm["SA"]
        MS = sb("r_MS", [128, 8])
        VS = sb("r_VS", [128, 8])
        RKS = sb("r_RKS", [128, 8])
        big = {}
        for n in ("M0", "M1", "N0", "N1", "PT", "LAK", "MRB", "MRK"):
            big[n] = sb("r_b" + n, [128, 8, 128])
        OB = [sb("r_OB%d" % i, [128, 4, 128], BF16) for i in range(2)]
        ps = cx.ps
        P = Phase(cx.st)
        bank = [0]

        def nb():
            b = ps[bank[0] % 8]
            bank[0] += 1
            return b

        def b3(p_, h=4):
            return p_[:].rearrange("p (h t) -> p h t", h=h)

        P.dma("sp", rwp[:], prm["rwp"])
        P.dma("sp", lop[:], prm["lop"])
        P.dma("sp", w2s[:], prm["w2"])
        P.dma("sp", a2s[:], prm["a2"])
        P.dma("sp", g2s[:], prm["g2"])
        P.dma("sp", tmb[:], prm["tmb"])
        if layer > 0:
            P.dma("sp", v2s[:], prm["v2"])
        P.ts(omka[:], rwp[:, 5, :], -1.0, 1.0, op0=ALU.mult, op1=ALU.add)
        for j in range(4):
            P.copy(m_gt4[:, j, :], cx.m_gt_f[:])
            P.copy(m_lt4[:, j, :], cx.m_lt_f[:])
            P.copy(m_le4[:, j, :], cx.m_le_f[:])
            P.copy(id4[:, j, :], cx.ident_f[:])
        P.memset(rmh[:], 1.0)
        P.memset(rmh[:, :, 0:1], 0.0)
        P.memset(ST[:], 0.0)

        def bc(ap2, n=128):
            return ap2.unsqueeze(2).to_broadcast([ap2.shape[0], ap2.shape[1], n])

        def fmv(row0):
            return pfm[row0:row0 + 512, :].rearrange("(h k) t -> k h t", k=64)

        rv, kv = fmv(R_RWR), fmv(R_RWK)
        mixv = mixT[R_MIX_RW:R_MIX_RW + 512, :].rearrange("(j p) t -> p j t", p=128)
        hs = lambda h: slice(h * 64, (h + 1) * 64)

        def shift_load(cur, prev, src3, t0, three):
            if three:
                P.dma("sp", cur[:], src3[:, :, t0:t0 + 128])
                if t0 == 0:
                    P.memset(prev[:, :, 0:1], 0.0)
                    P.dma("sp", prev[:, :, 1:128], src3[:, :, 0:127])
                else:
                    P.dma("sp", prev[:], src3[:, :, t0 - 1:t0 + 127])
            else:
                P.dma("sp", cur[:], src3[:, t0:t0 + 128])
                if t0 == 0:
                    P.memset(prev[:, 0:1], 0.0)
                    P.dma("sp", prev[:, 1:128], src3[:, 0:127])
                else:
                    P.dma("sp", prev[:], src3[:, t0 - 1:t0 + 127])

        for c in range(NT):
            t0 = c * 128
            f = fm
            shift_load(f["Rc"], f["Rp"], rv, t0, True)
            shift_load(f["Kc"], f["Kp"], kv, t0, True)
            for (cur, prev, out, mi, en) in ((f["Rc"], f["Rp"], f["Rs"], 0, "dve"), (f["Kc"], f["Kp"], f["Ks"], 1, "dve")):
                P.tt(prev[:], prev[:], cur[:], ALU.subtract, eng=en)
                P.tt(prev[:], prev[:], bc(rwp[:, mi, :]), ALU.mult, eng=en)
                P.tt(out[:], prev[:], cur[:], ALU.add, eng=en)
            shift_load(lo["WLc"], lo["WLp"], pfm[R_WLO:R_WLO + 96, :], t0, False)
            shift_load(lo["ALc"], lo["ALp"], pfm[R_ALO:R_ALO + 96, :], t0, False)
            shift_load(lo["GLc"], lo["GLp"], pfm[R_GLO:R_GLO + 256, :].rearrange("(j p) t -> p j t", p=128), t0, True)
            P.tt(lo["WLp"][:], lo["WLp"][:], lo["WLc"][:], ALU.subtract)
            P.stt(lo["WLc"][:], lo["WLp"][:], lop[:96, 0:1], lo["WLc"][:], ALU.mult, ALU.add)
            P.act(lo["WLc"][:], lo["WLc"][:], AF.Tanh)
            P.tt(lo["ALp"][:], lo["ALp"][:], lo["ALc"][:], ALU.subtract)
            P.stt(lo["ALc"][:], lo["ALp"][:], lop[:96, 1:2], lo["ALc"][:], ALU.mult, ALU.add)
            P.tt(lo["GLp"][:], lo["GLp"][:], lo["GLc"][:], ALU.subtract)
            for j in range(2):
                P.stt(lo["GLc"][:, j, :], lo["GLp"][:, j, :], lop[:, 2 + j:3 + j], lo["GLc"][:, j, :], ALU.mult, ALU.add)
            P.act(lo["GLc"][:], lo["GLc"][:], AF.Sigmoid)
            for (w_s, code, bias_i, out) in ((w2s, lo["WLc"], 2, f["SW"]), (a2s, lo["ALc"], 3, f["A"])):
                for half in range(2):
                    pb = nb()
                    for j in range(4):
                        h = half * 4 + j
                        P.mm(pb[:64, j * 128:(j + 1) * 128], w_s[:, hs(h)], code[:])
                    P.tt(out[:, half * 4:half * 4 + 4, :], b3(pb)[:64], bc(rwp[:, bias_i, half * 4:half * 4 + 4]), ALU.add)
                P.act(out[:], out[:], AF.Sigmoid)
            P.scan(f["CS"][:].rearrange("k h t -> k (h t)"), rmh[:].rearrange("k h t -> k (h t)"),
                   f["SW"][:].rearrange("k h t -> k (h t)"), 0.0, ALU.mult, ALU.add)
            P.tt(f["CX"][:], f["CS"][:], f["SW"][:], ALU.subtract)
            P.act(f["EP"][:], f["CS"][:], AF.Exp, scale=CW)
            P.act(f["EN"][:], f["CS"][:], AF.Exp, scale=-CW)
            P.act(f["EX"][:], f["CX"][:], AF.Exp, scale=CW)
            P.tt(f["KK0"][:], f["Ks"][:], bc(rwp[:, 4, :]), ALU.mult)
            P.tt(f["T0"][:], f["KK0"][:], f["KK0"][:], ALU.mult)
            for half in range(2):
                pb = nb()
                P.mm(pb[:64, :], cx.ones_f[:64, :64], f["T0"][:, half * 4:half * 4 + 4, :].rearrange("k h t -> k (h t)"))
                P.ts(f["T1"][:, half * 4:half * 4 + 4, :], b3(pb)[:64], 1e-16, None, op0=ALU.max)
            P.act(f["T1"][:], f["T1"][:], AF.Ln)
            P.act(f["T1"][:], f["T1"][:], AF.Exp, scale=-0.5)
            P.tt(f["KKn"][:], f["KK0"][:], f["T1"][:], ALU.mult)
            P.tt(f["Bv"][:], f["KKn"][:], f["A"][:], ALU.mult)
            P.tt(f["T0"][:], f["A"][:], bc(rwp[:, 5, :]), ALU.mult)
            P.tt(f["T0"][:], f["T0"][:], bc(omka[:]), ALU.add)
            P.tt(f["K2"][:], f["Ks"][:], f["T0"][:], ALU.mult)
            P.tt(f["At"][:], f["KKn"][:], f["EX"][:], ALU.mult)
            P.tt(f["Bt"][:], f["Bv"][:], f["EN"][:], ALU.mult)
            P.tt(f["Kt"][:], f["K2"][:], f["EN"][:], ALU.mult)
            P.tt(f["Rt"][:], f["Rs"][:], f["EP"][:], ALU.mult)
            eg = f["EP"][:, :, 127:128].to_broadcast([64, 8, 128])
            P.tt(f["Bh"][:], f["Bt"][:], eg, ALU.mult)
            P.tt(f["Kh"][:], f["Kt"][:], eg, ALU.mult)
            P.tt(f["T0"][:], f["Rs"][:], f["K2"][:], ALU.mult)
            P.tt(f["T0"][:], f["T0"][:], bc(rwp[:, 6, :]), ALU.mult)
            pb = nb()
            for h in range(H):
                P.mm(pb[:, h:h + 1], f["T0"][:, h, :], cx.ones_f[:64, 0:1])
            P.copy(RKS[:], pb[:, 0:8], eng="act")
            t = tm
            P.dma("sp", t["Vc"][:], ptm[t0:t0 + 128, C_RWV:C_RWV + 512])
            if t0 == 0:
                P.memset(t["Vp"][0:1, :], 0.0)
                P.dma("sp", t["Vp"][1:128, :], ptm[0:127, C_RWV:C_RWV + 512])
            else:
                P.dma("sp", t["Vp"][:], ptm[t0 - 1:t0 + 127, C_RWV:C_RWV + 512])
            P.tt(t["Vp"][:], t["Vp"][:], t["Vc"][:], ALU.subtract)
            P.tt(t["Vp"][:], t["Vp"][:], tmb[:, 0, :], ALU.mult)
            if layer == 0:
                P.tt(t["V"][:], t["Vp"][:], t["Vc"][:], ALU.add)
                P.dma("sp", vfirst[t0:t0 + 128, :], t["V"][:])
            else:
                P.tt(t["Vc"][:], t["Vp"][:], t["Vc"][:], ALU.add)
                shift_load(lo["VLc"], lo["VLp"], pfm[R_VLO:R_VLO + 64, :], t0, False)
                P.tt(lo["VLp"][:], lo["VLp"][:], lo["VLc"][:], ALU.subtract)
                P.stt(lo["VLc"][:], lo["VLp"][:], lop[:64, 4:5], lo["VLc"][:], ALU.mult, ALU.add)
                pb = nb()
                P.mm(pb[:, :], lo["VLc"][:], v2s[:])
                P.tt(t["SV"][:], pb[:, :], tmb[:, 1, :], ALU.add)
                P.act(t["SV"][:], t["SV"][:], AF.Sigmoid)
                P.dma("sp", t["VF"][:], vfirst[t0:t0 + 128, :])
                P.tt(t["VF"][:], t["VF"][:], t["Vc"][:], ALU.subtract)
                P.tt(t["VF"][:], t["VF"][:], t["SV"][:], ALU.mult)
                P.tt(t["V"][:], t["VF"][:], t["Vc"][:], ALU.add)
            pb = nb()
            for j in range(2):
                P.mm(pb[:, :], lo["GLc"][:, j, :], g2s[:, j, :], start=(j == 0), stop=(j == 1))
            P.copy(t["Gt"][:], pb[:, :], eng="act")
            M, N, PT = big["M0"], big["N0"], big["PT"]
            M2, N2 = big["M1"], big["N1"]
            for half in range(2):
                pa, pb = nb(), nb()
                for j in range(4):
                    h = half * 4 + j
                    P.mm(pa[:, j * 128:(j + 1) * 128], f["At"][:, h, :], f["Bt"][:, h, :])
                    P.mm(pb[:, j * 128:(j + 1) * 128], f["Bt"][:, h, :], f["At"][:, h, :])
                hh = slice(half * 4, half * 4 + 4)
                P.stt(M[:, hh, :], b3(pa), -1.0, m_gt4[:], ALU.mult, ALU.mult)
                P.stt(N[:, hh, :], b3(pb), -1.0, m_lt4[:], ALU.mult, ALU.mult)
                P.tt(PT[:, hh, :], N[:, hh, :], id4[:], ALU.add, eng="pool")
            for step in range(6):
                last = step == 5
                for half in range(2):
                    hh = slice(half * 4, half * 4 + 4)
                    pa = nb()
                    for j in range(4):
                        h = half * 4 + j
                        P.mm(pa[:, j * 128:(j + 1) * 128], N[:, h, :], M[:, h, :])
                    P.copy(M2[:, hh, :], b3(pa), eng="act")
                    if not last:
                        pb = nb()
                        for j in range(4):
                            h = half * 4 + j
                            P.mm(pb[:, j * 128:(j + 1) * 128], M[:, h, :], N[:, h, :])
                        P.copy(N2[:, hh, :], b3(pb), eng="dve")
                    pc = nb()
                    for j in range(4):
                        h = half * 4 + j
                        P.mm(pc[:, j * 128:(j + 1) * 128], M2[:, h, :], PT[:, h, :])
                    P.tt(PT[:, hh, :], b3(pc), PT[:, hh, :], ALU.add)
                M, M2 = M2, M
                N, N2 = N2, N
            for (dst, lt, rt, msk) in ((big["LAK"], f["Kt"], f["At"], m_lt4), (big["MRB"], f["Bt"], f["Rt"], m_le4),
                                       (big["MRK"], f["Kt"], f["Rt"], m_le4)):
                for half in range(2):
                    pa = nb()
                    for j in range(4):
                        h = half * 4 + j
                        P.mm(pa[:, j * 128:(j + 1) * 128], lt[:, h, :], rt[:, h, :])
                    P.tt(dst[:, half * 4:half * 4 + 4, :], b3(pa), msk[:], ALU.mult)
            for (src, dst) in ((f["Bh"], t["BHt"]), (f["Kh"], t["KHt"])):
                pa = nb()
                for h in range(H):
                    P.transpose(pa[:, hs(h)], src[:, h, :], cx.ident_f[:64, :64])
                P.copy(dst[:], pa[:, :], eng="act")
            pa = nb()
            for h in range(H):
                P.mm(pa[:, hs(h)], big["LAK"][:, h, :], t["V"][:, hs(h)], start=True, stop=False)
                P.mm(pa[:, hs(h)], f["At"][:, h, :], ST[:, h, :], start=False, stop=True)
            P.ts(t["NXZ"][:], pa[:, :], -1.0, None, op0=ALU.mult)
            pa = nb()
            for h in range(H):
                P.mm(pa[:, hs(h)], PT[:, h, :], t["NXZ"][:, hs(h)])
            P.copy(t["SA"][:], pa[:, :], eng="act")
            pa = nb()
            for h in range(H):
                P.mm(pa[:, hs(h)], f["Rt"][:, h, :], ST[:, h, :], start=True, stop=False)
                P.mm(pa[:, hs(h)], big["MRB"][:, h, :], t["SA"][:, hs(h)], start=False, stop=False)
                P.mm(pa[:, hs(h)], big["MRK"][:, h, :], t["V"][:, hs(h)], start=False, stop=True)
            P.copy(t["Y"][:], pa[:, :], eng="act")
            pa = nb()
            for h in range(H):
                P.mm(pa[:64, hs(h)], t["BHt"][:, hs(h)], t["SA"][:, hs(h)], start=True, stop=False)
                P.mm(pa[:64, hs(h)], t["KHt"][:, hs(h)], t["V"][:, hs(h)], start=False, stop=True)
            P.tt(ST[:], ST[:], f["EP"][:, :, 127:128].to_broadcast([64, 8, 64]), ALU.mult)
            P.tt(ST[:], ST[:], pa[:64, :].rearrange("k (h v) -> k h v", h=8), ALU.add)
            Y3 = t["Y"][:].rearrange("p (h v) -> p h v", h=8)
            C3 = t["CEN"][:].rearrange("p (h v) -> p h v", h=8)
            S3 = t["SQ"][:].rearrange("p (h v) -> p h v", h=8)
            N3 = t["YN"][:].rearrange("p (h v) -> p h v", h=8)
            V3 = t["V"][:].rearrange("p (h v) -> p h v", h=8)
            B3 = t["BON"][:].rearrange("p (h v) -> p h v", h=8)
            P.add("dve", lambda e: e.tensor_reduce(MS[:], Y3, AX.X, ALU.add), [t["Y"][:]], [MS[:]])
            P.ts(MS[:], MS[:], 1.0 / 64, None, op0=ALU.mult)
            P.tt(C3, Y3, MS[:].unsqueeze(2).to_broadcast([128, 8, 64]), ALU.subtract)
            P.tt(S3, C3, C3, ALU.mult, eng="pool")
            P.add("dve", lambda e: e.tensor_reduce(VS[:], S3, AX.X, ALU.add), [t["SQ"][:]], [VS[:]])
            P.act(VS[:], VS[:], AF.Ln, bias=RW_LN_EPS, scale=1.0 / 64)
            P.act(VS[:], VS[:], AF.Exp, scale=-0.5)
            P.tt(N3, C3, VS[:].unsqueeze(2).to_broadcast([128, 8, 64]), ALU.mult)
            P.tt(t["YN"][:], t["YN"][:], tmb[:, 2, :], ALU.mult)
            P.tt(t["YN"][:], t["YN"][:], tmb[:, 3, :], ALU.add)
            P.tt(B3, V3, RKS[:].unsqueeze(2).to_broadcast([128, 8, 64]), ALU.mult, eng="pool")
            P.tt(t["YN"][:], t["YN"][:], t["BON"][:], ALU.add)
            P.tt(t["YN"][:], t["YN"][:], t["Gt"][:], ALU.mult)
            pa = nb()
            for j in range(4):
                P.transpose(pa[:, j * 128:(j + 1) * 128], t["YN"][:, j * 128:(j + 1) * 128], cx.ident_f[:])
            ob = OB[c % 2]
            P.copy(ob[:], b3(pa), eng="act")
            P.dma("sp", mixv[:, :, t0:t0 + 128], ob[:])
        P.flush()


from concourse.bass_utils import run_bass_kernel_spmd

DEPTH = 4
N_META = 16
RW_SHIFT_N = 1984


def build_program(L, depth=DEPTH, Lr=None):
    nc = bass.Bass("TRN2", target_bir_lowering=False)
    dt = lambda n, s, k="ExternalInput", d=F32: nc.dram_tensor(n, s, d, kind=k).ap()
    h0 = dt("h0", [D, L])
    gains = dt("gains", [depth, 3, 128, DC])
    f1wi = dt("ffn1_wi", [depth, FC, 128, DC, 256])
    f1wo = dt("ffn1_wo", [depth, DC, 128, FC, 128])
    f2wi = dt("ffn2_wi", [depth, FC, 128, DC, 256])
    f2wo = dt("ffn2_wo", [depth, DC, 128, FC, 128])
    w_in = dt("w_in", [depth, 14, 128, DC, 512])
    w_in_v = dt("w_in_v", [depth - 1, 128, DC, 64]) if depth > 1 else None
    w_out = dt("w_out", [depth, 4, 128, DC, 512])
    hglb = dt("hglb", [128, 4, 4])
    hgn = dt("hgn", [depth, 128, 1])
    sbg = dt("sbg", [depth, 128, 3])
    rwp = dt("rwp", [depth, 64, 7, 8])
    lop = dt("lop", [depth, 128, 8])
    w2 = dt("rw_w2", [depth, 96, 512])
    a2 = dt("rw_a2", [depth, 96, 512])
    g2 = dt("rw_g2", [depth, 128, 2, 512])
    v2 = dt("rw_v2", [depth, 64, 512])
    tmb = dt("tmb", [depth, 128, 5, 512])
    hT = dt("hT", [D, L], "ExternalOutput")
    pfm = dt("pfm", [NFM, L], "Internal")
    ptm = dt("ptm", [L, NTM], "Internal")
    vfirst = dt("vfirst", [L, 512], "Internal")
    mixT = dt("mixT", [D, L], "Internal", BF16)
    with ExitStack() as es:
        cx = make_ctx(nc, es)
        cx.eps_ap = lambda e: float(e)
        init_consts(cx)
        make_masks(cx, es)
        make_lb(cx, es, hglb)
        P = Phase(cx.st)
        P.dma("sp", hT, h0)
        P.flush()
        for l in range(depth):
            ffn_phase(cx, hT, f1wi[l], f1wo[l], gains[l, 0], L, Lr=Lr)
            proj_phase(cx, hT, w_in[l], (w_in_v[l - 1] if l > 0 else None), gains[l, 1], pfm, ptm, L)
            hg_phase(cx, pfm, ptm, l, hgn[l], mixT, L)
            sb_phase(cx, pfm, ptm, sbg[l], mixT, L)
            prm = {"rwp": rwp[l], "lop": lop[l], "w2": w2[l], "a2": a2[l], "g2": g2[l], "v2": v2[l], "tmb": tmb[l]}
            rw_phase(cx, pfm, ptm, l, prm, vfirst, mixT, L)
            ffn_phase(cx, hT, f2wi[l], f2wo[l], gains[l, 2], L, pre=(mixT, w_out[l]), Lr=Lr)
    return nc, cx.st.nops


def _c(a):
    return np.ascontiguousarray(a, dtype=np.float32)


def layout_params(inp, depth=DEPTH):
    fm16 = lambda v: v.reshape(DC, 128).T
    fmh = lambda v: v.reshape(8, 64).T
    out = {}
    out["gains"] = _c(np.stack([np.stack([fm16(inp[k][l]) for k in ("norm_ffn1", "norm_mix", "norm_ffn2")]) for l in range(depth)]))
    for k in ("ffn1_wi", "ffn2_wi"):
        out[k] = _c(inp[k][:depth].reshape(depth, DC, 128, FC, 256).transpose(0, 3, 2, 1, 4))
    for k in ("ffn1_wo", "ffn2_wo"):
        out[k] = _c(inp[k][:depth].reshape(depth, FC, 128, DC, 128).transpose(0, 3, 2, 1, 4))
    wpad = np.zeros((depth, D, 7168), np.float32)
    wpad[:, :, :7104] = inp["w_in"][:depth]
    out["w_in"] = _c(wpad.reshape(depth, DC, 128, 14, 512).transpose(0, 3, 2, 1, 4))
    out["w_out"] = _c(inp["w_out"][:depth].reshape(depth, DC, 128, 4, 512).transpose(0, 3, 2, 1, 4))
    if depth > 1:
        out["w_in_v"] = _c(inp["w_in_v"][:depth - 1].reshape(depth - 1, DC, 128, 64).transpose(0, 2, 1, 3))
    out["hglb"] = _c(inp["hg_lb"].reshape(4, 4, 128).transpose(2, 1, 0))
    out["hgn"] = _c(inp["hg_norm"][:depth].reshape(depth, 128, 1))
    out["sbg"] = _c(np.stack([np.stack([inp["sb_qn"][l], inp["sb_kn"][l], inp["sb_on"][l]], axis=1) for l in range(depth)]))
    rwp, lop, v2, tmb = [], [], [], []
    for l in range(depth):
        mu = inp["rw_mu"][l]
        rwp.append(np.stack([fmh(mu[0:512]), fmh(mu[512:1024]), fmh(inp["rw_w0"][l]), fmh(inp["rw_a0"][l]), fmh(inp["rw_kk"][l]),
                             fmh(inp["rw_ka"][l]), fmh(inp["rw_rk"][l].reshape(-1))], axis=1))
        lp = np.zeros((128, 8), np.float32)
        lp[:96, 0] = mu[1536:1632]
        lp[:96, 1] = mu[1632:1728]
        lp[:, 2] = mu[1728:1856]
        lp[:, 3] = mu[1856:1984]
        tb = np.zeros((128, 5, 512), np.float32)
        tb[:, 0] = mu[1024:1536][None]
        tb[:, 2] = inp["rw_ln_w"][l][None]
        tb[:, 3] = inp["rw_ln_b"][l][None]
        vv = np.zeros((64, 512), np.float32)
        if l > 0:
            lp[:64, 4] = inp["rw_mu_v"][l - 1]
            tb[:, 1] = inp["rw_v0"][l - 1][None]
            vv = inp["rw_v2"][l - 1]
        lop.append(lp)
        tmb.append(tb)
        v2.append(vv)
    out["rwp"] = _c(np.stack(rwp))
    out["lop"] = _c(np.stack(lop))
    out["tmb"] = _c(np.stack(tmb))
    out["rw_v2"] = _c(np.stack(v2))
    out["rw_w2"] = _c(inp["rw_w2"][:depth])
    out["rw_a2"] = _c(inp["rw_a2"][:depth])
    out["rw_g2"] = _c(np.stack([inp["rw_g2"][l].reshape(2, 128, 512).transpose(1, 0, 2) for l in range(depth)]))
    return out


def kernel(**inputs):
    inp = {k: np.asarray(v) for k, v in inputs.items()}
    x = inp["x"]
    B, S, _ = x.shape
    depth = inp["norm_ffn1"].shape[0]
    L_real = N_META + S
    L = ((L_real + 127) // 128) * 128
    nc, nops = build_program(L, depth, L_real)
    shared = layout_params(inp, depth)
    n_cores = 8 if B == 4 else B
    in_maps = []
    for c in range(n_cores):
        b = c % B
        h0 = np.zeros((L, D), np.float32)
        h0[:N_META] = inp["meta"]
        h0[N_META:L_real] = x[b]
        m = dict(shared)
        m["h0"] = _c(h0.T)
        in_maps.append(m)
    res = run_bass_kernel_spmd(nc, in_maps, core_ids=list(range(n_cores)))
    out = np.stack([np.ascontiguousarray(res.results[b]["hT"].T[N_META:L_real]) for b in range(B)])
    return out.astype(np.float32)
```
